# Optimizing a Trainium2 kernel written in Bass

```python
import jax, jax.numpy as jnp
from jax import lax
import numpy as np

D_MODEL = 1024
BATCH = 16
SEQ = 2048
DEPTH = 1

D_MIX = D_MODEL
D_CONV = D_MIX // 2
CONV_HEADS = 8
CONV_WIDTH = 31
D_POOL = D_MIX - D_CONV
POOL_WINDOWS = (2, 4, 8, 16)
POOL_GROUPS = len(POOL_WINDOWS)
POOL_GROUP_DIM = D_POOL // POOL_GROUPS
D_IN = 2 * D_CONV + D_POOL
N_MEM = 256
XATTN_HEADS = 4
XATTN_HEAD_DIM = D_MODEL // XATTN_HEADS
D_FF = 2816
FFN_CONV_WIDTH = 3
EPS = 1e-6

kernel_name = "hybrid_conformer_pool_xattn_convffn"


def rmsnorm(x, g):
    xf = x.astype(jnp.float32)
    y = xf * lax.rsqrt(jnp.mean(xf * xf, axis=-1, keepdims=True) + EPS)
    return (y * g.astype(jnp.float32)).astype(x.dtype)


def layernorm(x, g, b):
    xf = x.astype(jnp.float32)
    mu = jnp.mean(xf, axis=-1, keepdims=True)
    var = jnp.mean(jnp.square(xf - mu), axis=-1, keepdims=True)
    y = (xf - mu) * lax.rsqrt(var + EPS)
    return (y * g.astype(jnp.float32) + b.astype(jnp.float32)).astype(x.dtype)


def causal_depthwise_conv(x, w, b):
    k, c = w.shape
    y = lax.conv_general_dilated(
        x, w[:, None, :].astype(x.dtype), window_strides=(1,),
        padding=[(k - 1, 0)], dimension_numbers=("NWC", "WIO", "NWC"),
        feature_group_count=c)
    return y + b.astype(x.dtype)


def conformer_conv_mixer(u, dw_w, dw_b, ln_g, ln_b):
    val, gate = jnp.split(u, 2, axis=-1)
    h = val * jax.nn.sigmoid(gate)
    h = causal_depthwise_conv(h, dw_w, dw_b)
    h = layernorm(h, ln_g, ln_b)
    return jax.nn.silu(h)


def causal_window_mean_minus_self(v, w):
    s = v.shape[1]
    vf = v.astype(jnp.float32)
    c = jnp.cumsum(vf, axis=1)
    c_shift = jnp.pad(c, ((0, 0), (w, 0), (0, 0)))[:, :s]
    count = jnp.minimum(jnp.arange(1, s + 1, dtype=jnp.float32), float(w))
    mean = (c - c_shift) / count[None, :, None]
    return (mean - vf).astype(v.dtype)


def pooling_mixer(u, pool_w, pool_scale):
    groups = jnp.split(u, POOL_GROUPS, axis=-1)
    pooled = jnp.stack([causal_window_mean_minus_self(gv, w)
                        for gv, w in zip(groups, POOL_WINDOWS)], axis=2)
    mixed = jnp.einsum("bsgc,gcd->bsgd", pooled, pool_w.astype(u.dtype))
    b, s = u.shape[:2]
    return mixed.reshape(b, s, D_POOL) * pool_scale.astype(u.dtype)


def memory_cross_attention(h, mem_n, w_q, w_kv, w_o):
    b, s, _ = h.shape
    q = (h @ w_q).reshape(b, s, XATTN_HEADS, XATTN_HEAD_DIM)
    k, v = jnp.split(mem_n @ w_kv, 2, axis=-1)
    k = k.reshape(b, N_MEM, XATTN_HEADS, XATTN_HEAD_DIM)
    v = v.reshape(b, N_MEM, XATTN_HEADS, XATTN_HEAD_DIM)
    scores = jnp.einsum("bqhd,bkhd->bhqk", q.astype(jnp.float32), k.astype(jnp.float32))
    probs = jax.nn.softmax(scores * (XATTN_HEAD_DIM ** -0.5), axis=-1).astype(h.dtype)
    o = jnp.einsum("bhqk,bkhd->bqhd", probs, v).reshape(b, s, D_MODEL)
    return o @ w_o


def conv_ffn(h, w_up, dw_w, dw_b, w_down):
    u = causal_depthwise_conv(h @ w_up, dw_w, dw_b)
    gate, val = jnp.split(u, 2, axis=-1)
    return (jax.nn.silu(gate) * val) @ w_down


def setup_inputs(seed: int = 0) -> dict:
    key = jax.random.key(seed)
    ks = jax.random.split(key, 24)
    f32 = jnp.float32

    def nrm(k, shape, scale):
        return jax.random.normal(k, shape, f32) * scale

    def gain(k, shape):
        return 1.0 + 0.05 * jax.random.normal(k, shape, f32)

    L = DEPTH
    return {
        "x": jax.random.normal(ks[0], (BATCH, SEQ, D_MODEL), f32),
        "mem": jax.random.normal(ks[1], (BATCH, N_MEM, D_MODEL), f32),
        "norm_mix_g": gain(ks[2], (L, D_MODEL)),
        "w_in": nrm(ks[3], (L, D_MODEL, D_IN), D_MODEL ** -0.5),
        "conv_dw_w": nrm(ks[4], (L, CONV_WIDTH, D_CONV), CONV_WIDTH ** -0.5),
        "conv_dw_b": nrm(ks[5], (L, D_CONV), 0.02),
        "conv_ln_g": gain(ks[6], (L, D_CONV)),
        "conv_ln_b": nrm(ks[7], (L, D_CONV), 0.02),
        "pool_w": nrm(ks[8], (L, POOL_GROUPS, POOL_GROUP_DIM, POOL_GROUP_DIM), POOL_GROUP_DIM ** -0.5),
        "pool_scale": gain(ks[9], (L, D_POOL)),
        "w_out": nrm(ks[10], (L, D_MIX, D_MODEL), D_MIX ** -0.5),
        "norm_xattn_g": gain(ks[11], (L, D_MODEL)),
        "norm_mem_g": gain(ks[12], (L, D_MODEL)),
        "w_q": nrm(ks[13], (L, D_MODEL, D_MODEL), D_MODEL ** -0.5),
        "w_kv": nrm(ks[14], (L, D_MODEL, 2 * D_MODEL), D_MODEL ** -0.5),
        "w_o": nrm(ks[15], (L, D_MODEL, D_MODEL), D_MODEL ** -0.5),
        "norm_ffn_g": gain(ks[16], (L, D_MODEL)),
        "w_up": nrm(ks[17], (L, D_MODEL, 2 * D_FF), D_MODEL ** -0.5),
        "ffn_dw_w": nrm(ks[18], (L, FFN_CONV_WIDTH, 2 * D_FF), FFN_CONV_WIDTH ** -0.5),
        "ffn_dw_b": nrm(ks[19], (L, 2 * D_FF), 0.02),
        "w_down": nrm(ks[20], (L, D_FF, D_MODEL), D_FF ** -0.5),
        "norm_final_g": gain(ks[21], (D_MODEL,)),
    }


def reference(x, mem, norm_mix_g, w_in, conv_dw_w, conv_dw_b, conv_ln_g, conv_ln_b,
              pool_w, pool_scale, w_out, norm_xattn_g, norm_mem_g, w_q, w_kv, w_o,
              norm_ffn_g, w_up, ffn_dw_w, ffn_dw_b, w_down, norm_final_g):
    for l in range(DEPTH):
        h = rmsnorm(x, norm_mix_g[l])
        u = h @ w_in[l]
        u_conv = u[..., :2 * D_CONV]
        u_pool = u[..., 2 * D_CONV:]
        y_conv = conformer_conv_mixer(u_conv, conv_dw_w[l], conv_dw_b[l],
                                      conv_ln_g[l], conv_ln_b[l])
        y_pool = pooling_mixer(u_pool, pool_w[l], pool_scale[l])
        y = jnp.concatenate([y_conv, y_pool], axis=-1)
        x = x + y @ w_out[l]
        h = rmsnorm(x, norm_xattn_g[l])
        mem_n = rmsnorm(mem, norm_mem_g[l])
        x = x + memory_cross_attention(h, mem_n, w_q[l], w_kv[l], w_o[l])
        h = rmsnorm(x, norm_ffn_g[l])
        x = x + conv_ffn(h, w_up[l], ffn_dw_w[l], ffn_dw_b[l], w_down[l])
    return rmsnorm(x, norm_final_g)
```

```python
import numpy as np
from contextlib import ExitStack
import concourse.bass as bass
import concourse.mybir as mybir
from concourse.bass_utils import run_bass_kernel_spmd

F32 = mybir.dt.float32
BF16 = mybir.dt.bfloat16
AF = mybir.ActivationFunctionType
ALU = mybir.AluOpType

NCORES = 8
SEQ_PER_CORE = 2
D = 1024
KD = 8
SEQ = 2048
GT = 1024
NT = GT // 128
NBLK = GT // 512
NMEM = 256
DFF = 2816
NFF = DFF // 128
CW = 31
EPS = 1e-6
FFN_G = 4
FFN_PASSES = [(0, 4), (4, 4), (8, 4), (12, 4), (16, 3), (19, 3)]

V_GMIX, V_GX, V_GMEM, V_GFFN = 0, 8, 16, 24
V_CB, V_LNG, V_LNB, V_PSC = 32, 36, 40, 44
V_CW = 48
V_FW = V_CW + 4 * CW
V_FB = V_FW + 44 * 3
V_N = V_FB + 44


class Res:
    __slots__ = ("w", "r", "ov", "lo", "hi")

    def __init__(self, lo=None, hi=None):
        self.w = None
        self.r = []
        self.ov = []
        self.lo = lo
        self.hi = hi


class Sched:
    ENG = ("pe", "act", "dve", "pool", "sp")

    def __init__(self, nc, stack):
        self.nc = nc
        self.stack = stack
        self.prog = {e: [] for e in self.ENG}
        self.sem = {e: stack.enter_context(nc.semaphore("s_" + e)) for e in self.ENG}
        self.cnt = {e: 0 for e in self.ENG}
        self.waited = {e: {} for e in self.ENG}
        self.fence_tok = []
        self.abs = {e: [] for e in self.ENG}

    def check(self):
        val = {}
        ptr = {e: 0 for e in self.ENG}
        progress = True
        while progress:
            progress = False
            for e in self.ENG:
                while ptr[e] < len(self.abs[e]):
                    waits, sem, n = self.abs[e][ptr[e]]
                    if all(val.get(id(s_), 0) >= v for s_, v in waits):
                        if sem is not None:
                            val[id(sem)] = val.get(id(sem), 0) + n
                        ptr[e] += 1
                        progress = True
                    else:
                        break
        stuck = {e: (ptr[e], len(self.abs[e])) for e in self.ENG if ptr[e] < len(self.abs[e])}
        return stuck

    def _deps(self, eng, reads, writes, extra=(), skip_key=None):
        deps = {}

        def add(tok):
            if tok is None:
                return
            k, v = tok
            if deps.get(k, 0) < v:
                deps[k] = v

        for r in reads:
            add(r.w)
            for o in r.ov:
                add(o.w)
        for w in writes:
            add(w.w)
            for t in w.r:
                add(t)
            for o in w.ov:
                add(o.w)
                for t in o.r:
                    add(t)
        for t in extra:
            add(t)
        out = []
        for k, v in deps.items():
            if k == "pe" and eng == "pe":
                continue
            if skip_key is not None and k is skip_key:
                continue
            if self.waited[eng].get(k, 0) >= v:
                continue
            self.waited[eng][k] = v
            out.append((self.sem[k] if isinstance(k, str) else k, v))
        return out

    def _mark(self, tok, reads, writes):
        for r in reads:
            r.r.append(tok)
            if len(r.r) > 64:
                best = {}
                for k, v in r.r:
                    if best.get(k, 0) < v:
                        best[k] = v
                r.r = list(best.items())
        for w in writes:
            w.w = tok
            w.r = []

    def op(self, eng, fn, reads=(), writes=(), inc=True):
        waits = self._deps(eng, reads, writes)
        if inc:
            self.cnt[eng] += 1
        tok = (eng, self.cnt[eng] if inc else self.cnt[eng] + 1)
        sem = self.sem[eng]

        def thunk(e, waits=waits, fn=fn, inc=inc, sem=sem):
            for s, v in waits:
                e.wait_ge(s, v)
            ins = fn(e)
            if inc:
                ins.then_inc(sem, 1)

        self.prog[eng].append(thunk)
        self.abs[eng].append((waits, sem if inc else None, 1))
        self._mark(tok, reads, writes)
        return tok

    def newslot(self, name):
        return {"sem": self.stack.enter_context(self.nc.semaphore(name)), "cnt": 0}

    def dma(self, eng, slot, out, in_, reads=(), writes=(), extra=(), **kw):
        waits = self._deps(eng, reads, writes, extra, skip_key=slot["sem"])
        slot["cnt"] += 16
        tok = (slot["sem"], slot["cnt"])

        def thunk(e, waits=waits, out=out, in_=in_, sem=slot["sem"], kw=kw):
            for s, v in waits:
                e.wait_ge(s, v)
            e.dma_start(out=out, in_=in_, **kw).then_inc(sem, 16)

        self.prog[eng].append(thunk)
        self.abs[eng].append((waits, slot["sem"], 16))
        self._mark(tok, reads, writes)
        return tok

    def fence(self):
        comp = ("pe", "act", "dve")
        snap = [(e, self.cnt[e]) for e in comp if self.cnt[e] > 0]
        self.fence_tok = snap
        for e in comp:
            waits = []
            for p, v in snap:
                if p == e or self.waited[e].get(p, 0) >= v:
                    continue
                self.waited[e][p] = v
                waits.append((self.sem[p], v))
            if waits:
                self.prog[e].append(lambda eng, waits=waits: [eng.wait_ge(s, v) for s, v in waits])
                self.abs[e].append((waits, None, 0))

    def final_wait(self, eng, resources):
        waits = self._deps(eng, resources, ())
        self.prog[eng].append(lambda e, waits=waits: [e.wait_ge(s, v) for s, v in waits])

    def emit(self):
        with self.nc.Block() as block:
            @block.tensor
            def _(e):
                for t in self.prog["pe"]:
                    t(e)

            @block.scalar
            def _(e):
                for t in self.prog["act"]:
                    t(e)

            @block.vector
            def _(e):
                for t in self.prog["dve"]:
                    t(e)

            @block.gpsimd
            def _(e):
                for t in self.prog["pool"]:
                    t(e)

            @block.sync
            def _(e):
                for t in self.prog["sp"]:
                    t(e)


def build_program(nseq=SEQ_PER_CORE, seqlen=SEQ):
    nc = bass.Bass("TRN2", target_bir_lowering=False)
    nhalf = seqlen // GT
    dx = nc.dram_tensor("x", [nseq, seqlen, D], F32, kind="ExternalInput").ap()
    dmem = nc.dram_tensor("mem", [nseq, NMEM, D], F32, kind="ExternalInput").ap()
    dvec = nc.dram_tensor("vecs", [128, V_N], F32, kind="ExternalInput").ap()
    dgfin = nc.dram_tensor("gfinb", [128, D], F32, kind="ExternalInput").ap()
    dw_in = nc.dram_tensor("w_in", [D, 1536], F32, kind="ExternalInput").ap()
    dpoolw = nc.dram_tensor("pool_w", [4, 128, 128], F32, kind="ExternalInput").ap()
    dw_out = nc.dram_tensor("w_out", [D, D], F32, kind="ExternalInput").ap()
    dw_q = nc.dram_tensor("w_q", [D, D], F32, kind="ExternalInput").ap()
    dw_kv = nc.dram_tensor("w_kv", [D, 2 * D], F32, kind="ExternalInput").ap()
    dw_o = nc.dram_tensor("w_o", [D, D], F32, kind="ExternalInput").ap()
    dw_up = nc.dram_tensor("w_up", [D, 2 * DFF], F32, kind="ExternalInput").ap()
    dw_down = nc.dram_tensor("w_down", [DFF, D], F32, kind="ExternalInput").ap()
    dout = nc.dram_tensor("out", [nseq, seqlen, D], F32, kind="ExternalOutput").ap()

    kp = lambda ap: ap.rearrange("(k p) n -> p k n", p=128)

    with ExitStack() as st:
        S = Sched(nc, st)
        base = (nc._sbuf_addr_for_side("left") + 63) // 64 * 64
        top = nc._sbuf_addr_for_side("right")
        cur = [base]
        nalloc = [0]
        binfo = {}
        regs = []

        def alloc(shape, dt, at=None):
            nb = 2 if dt == BF16 else 4
            sz = nb
            for s_ in shape[1:]:
                sz *= s_
            sz = (sz + 63) // 64 * 64
            if at is None:
                off = cur[0]
                cur[0] += sz
            else:
                off = at[0]
                at[0] += sz
            assert off + sz <= top, ("SBUF overflow", off + sz - top)
            nalloc[0] += 1
            t_ = nc.alloc_sbuf_tensor_at("t%d" % nalloc[0], list(shape), dt, offset=off)
            binfo[id(t_)] = (off, sz)
            return t_

        def REG(buf, idx=None, nsub=1, n=1):
            off, sz = binfo[id(buf)]
            if idx is None:
                lo, hi = off, off + sz
            else:
                sub = sz // nsub
                lo, hi = off + idx * sub, off + (idx + n) * sub
            r_ = Res(lo, hi)
            regs.append(r_)
            return r_

        def link_regs():
            for i_, a_ in enumerate(regs):
                for b_ in regs[i_ + 1:]:
                    if a_.lo < b_.hi and b_.lo < a_.hi:
                        a_.ov.append(b_)
                        b_.ov.append(a_)

        ident = alloc([128, 128], BF16)
        hident = alloc([128, 128], BF16)
        ones32 = alloc([128, 128], F32)
        onesb = alloc([128, 128], BF16)
        vecs = alloc([128, V_N], F32)
        gfinb = alloc([128, D], F32)
        invc = alloc([128, 4, 16], F32)
        fH = alloc([128, 2, 44, 2], F32)
        poolw = alloc([128, 4, 128], BF16)
        ssb = alloc([128, 16], F32)
        rstd = alloc([128, 16], F32)
        diag = alloc([128, 4 * CW, 128], BF16)
        KT = alloc([128, 1, 8, NMEM], BF16)
        Vt = alloc([128, 1, 2, D], BF16)
        X = alloc([128, NT, D], F32)
        WA = alloc([128, 12288], BF16)
        WB = alloc([128, 24576], BF16)
        hb2 = alloc([128, 2, D], BF16)
        smt = alloc([128, 4, 16], F32)
        gtail = alloc([128, 4, 32], BF16)
        utail = alloc([128, 4, 16], F32)
        ov_base = cur[0]
        identf = alloc([128, 128], F32, [ov_base])

        w_in_v = WA[:, 0:8 * 1536].rearrange("p (k n) -> p k n", k=8)
        w_q_v = WA[:, 0:8 * 1024].rearrange("p (k n) -> p k n", k=8)
        w_kv_v = WB[:, 0:16384].rearrange("p (k n) -> p k n", k=8)
        w_out_v = WB[:, 0:8192].rearrange("p (k n) -> p k n", k=8)
        w_o_v = WB[:, 12288:20480].rearrange("p (k n) -> p k n", k=8)

        def ffn_views(slot):
            b0 = slot * 12288
            g = WB[:, b0:b0 + 4096].rearrange("p (k n) -> p k n", k=8)
            v = WB[:, b0 + 4096:b0 + 8192].rearrange("p (k n) -> p k n", k=8)
            dn = WB[:, b0 + 8192:b0 + 12288].rearrange("p (j n) -> p j n", j=4)
            return g, v, dn

        o = [ov_base]
        memx = alloc([128, 2, D], F32, o)
        memT = alloc([128, 8, NMEM], BF16, o)
        o = [ov_base]
        gluT = alloc([128, 4, 32 + GT], BF16, o)
        pooledT = alloc([128, 4, GT], BF16, o)
        o1 = [o[0]]
        hTb = alloc([128, 2, 8, 512], BF16, o1)
        upT = alloc([128, 4, 528], F32, o1)
        ptmp = alloc([128, 2, 528], F32, o1)
        th = alloc([128, 2, 512], F32, o1)
        o1 = [o[0]]
        yT = alloc([128, 8, 512], BF16, o1)
        hc = alloc([128, 2, 4, 512], F32, o1)
        hsq = alloc([128, 2, 512], F32, o1)
        lst = alloc([128, 2, 512], F32, o1)
        o = [ov_base]
        hTb2 = alloc([128, 8, 512], BF16, o)
        QT = alloc([128, 8, 512], BF16, o)
        PT = alloc([128, 2, 2, 512], BF16, o)
        OT = alloc([128, 2, 8, 512], BF16, o)
        rr = alloc([128, 2, 512], F32, o)
        o = [ov_base]
        hTg = alloc([128, 8, GT], BF16, o)
        actT = alloc([128, 2, FFN_G, 512], BF16, o)
        aGV = alloc([128, 2, 2, 512], F32, o)
        Ub = alloc([128, 2, 2, 514], F32, o)

        ptb = [st.enter_context(nc.psum_tensor("ptb%d" % i, [128, 1024], BF16)) for i in range(2)]
        pmb = [st.enter_context(nc.psum_tensor("pmb%d" % i, [128, 512], F32)) for i in range(6)]
        Rptb = [Res() for _ in range(2)]
        Rpmb = [Res() for _ in range(6)]
        rot = {"t": 0, "m": 0}

        def tbank():
            i = rot["t"] % 2
            rot["t"] += 1
            return ptb[i], Rptb[i]

        def mbank():
            i = rot["m"] % 6
            rot["m"] += 1
            return pmb[i], Rpmb[i]

        Rc = Res()
        RX = [Res() for _ in range(NT)]
        RWA, RWB0, RWB1 = Res(), Res(), Res()
        Rdiag, RKV = Res(), Res()
        Rss, Rrstd, Rhb2, Rsm, Rtail = Res(), Res(), [Res(), Res()], Res(), Res()
        RfH = [Res(), Res()]
        Rout = [Res() for _ in range(NT)]
        Rm1, RmemT = REG(memx), REG(memT)
        Rglu, Rpooled, RupT, Rptmp = REG(gluT), REG(pooledT), REG(upT), REG(ptmp)
        RhTb = [REG(hTb, 0, 2), REG(hTb, 1, 2)]
        Rth = [REG(th, i, 2) for i in range(2)]
        RyT = REG(yT)
        Rhc = [[REG(hc, b_ * 4 + c, 8) for c in range(4)] for b_ in range(2)]
        Rhsq = [REG(hsq, i, 2) for i in range(2)]
        Rlst = [REG(lst, 0, 2), REG(lst, 1, 2)]
        RhTb2 = REG(hTb2)
        RQT = [REG(QT, 2 * hd, 8, 2) for hd in range(4)]
        RPT = [REG(PT, i, 2) for i in range(2)]
        ROT = [REG(OT, i, 2) for i in range(2)]
        Rrr = [REG(rr, i, 2) for i in range(2)]
        RhTg = REG(hTg)
        RactT = [REG(actT, i, 2) for i in range(2)]
        RaGV = [[REG(aGV, pa * 2 + xi, 4) for xi in range(2)] for pa in range(2)]
        RUb = [[REG(Ub, pa * 2 + xi, 4) for xi in range(2)] for pa in range(2)]
        Ridf = REG(identf)
        link_regs()

        slots = {n: S.newslot("d_" + n) for n in
                 ("c", "x0", "x1", "x2", "x3", "x4", "x5", "x6", "x7", "wa", "wb0", "wb1", "wa_h", "wb0_h", "wb1_h", "o0", "o1", "o2", "o3", "o4", "o5", "o6", "o7", "mem", "pw")}
        xslots = [slots["x%d" % i] for i in range(NT)]
        oslots = [slots["o%d" % i] for i in range(NT)]

        def vcol(c0, n=1):
            return vecs[:, c0:c0 + n]

        S.dma("sp", slots["c"], vecs[:], dvec, writes=[Rc])
        S.dma("sp", slots["c"], gfinb[:], dgfin, writes=[Rc])
        S.dma("pool", slots["pw"], poolw[:], dpoolw.rearrange("g c d -> c g d"), writes=[Rc])
        S.op("pool", lambda e: e.memset(identf[:], 0.0), writes=[Rc, Ridf])
        S.op("pool", lambda e: e.affine_select(out=identf[:], in_=identf[:], compare_op=ALU.not_equal, fill=1.0,
                                               base=0, pattern=[[-1, 128]], channel_multiplier=1),
             reads=[Rc, Ridf], writes=[Rc, Ridf])
        S.op("pool", lambda e: e.tensor_copy(ident[:], identf[:]), reads=[Rc, Ridf], writes=[Rc])
        S.op("dve", lambda e: e.tensor_scalar(out=hident[:], in0=ident[:], scalar1=0.5, scalar2=None, op0=ALU.mult),
             reads=[Rc], writes=[Rc])
        S.op("pool", lambda e: e.memset(ones32[:], 1.0 / 512.0), writes=[Rc])
        S.op("pool", lambda e: e.memset(onesb[:], 1.0), writes=[Rc])
        for g in range(4):
            w = 2 << g
            S.op("pool", lambda e, g=g, w=w: e.memset(invc[:, g, :], 1.0 / w), writes=[Rc])
            for t in range(w - 1):
                S.op("pool", lambda e, g=g, t=t: e.memset(invc[:, g, t:t + 1], 1.0 / (t + 1)), writes=[Rc])
        def mm_group(bank_ap, pairs, reads, Rbank, extra_reads=()):
            n = len(pairs)
            for i, (l, r) in enumerate(pairs):
                S.op("pe", lambda e, l=l, r=r, i=i: e.matmul(bank_ap, lhsT=l, rhs=r, start=(i == 0), stop=(i == n - 1)),
                     reads=list(reads) + list(extra_reads), writes=[Rbank], inc=(i == n - 1))

        def early_square(i):
            S.op("act", lambda e: e.activation(out=hb2[:, i % 2, :], in_=X[:, i, :], func=AF.Square,
                                               accum_out=ssb[:, i:i + 1]),
                 reads=[RX[i]], writes=[Rhb2[i % 2], Rss])

        def norm_stats(xtiles, Rx, ntile, squares=True, c0=0):
            for i in range(ntile if squares else 0):
                S.op("act", lambda e, i=i: e.activation(out=hb2[:, i % 2, :], in_=xtiles[i], func=AF.Square,
                                                        accum_out=ssb[:, c0 + i:c0 + i + 1]),
                     reads=[Rx[i]], writes=[Rhb2[i % 2], Rss])
            S.op("dve", lambda e: e.tensor_scalar(out=rstd[:, c0:c0 + ntile], in0=ssb[:, c0:c0 + ntile], scalar1=1.0 / D,
                                                  scalar2=EPS, op0=ALU.mult, op1=ALU.add), reads=[Rss], writes=[Rrstd])
            S.op("act", lambda e: e.activation(out=rstd[:, c0:c0 + ntile], in_=rstd[:, c0:c0 + ntile], func=AF.Sqrt),
                 reads=[Rrstd], writes=[Rrstd])
            S.op("dve", lambda e: e.reciprocal(out=rstd[:, c0:c0 + ntile], in_=rstd[:, c0:c0 + ntile]),
                 reads=[Rrstd], writes=[Rrstd])

        def norm_tile_to_T(xt, Rxt, i, gcol, dst, Rdst):
            hbi = hb2[:, i % 2, :]
            Rh = Rhb2[i % 2]
            S.op("act", lambda e: e.activation(out=hbi, in_=xt, func=AF.Copy, scale=rstd[:, i:i + 1]),
                 reads=[Rxt, Rrstd], writes=[Rh])
            tb, Rtb = tbank()
            for k in range(KD):
                S.op("pe", lambda e, k=k: e.transpose(out=tb[:, k * 128:(k + 1) * 128], in_=hbi[:, k * 128:(k + 1) * 128],
                                                      identity=ident[:]),
                     reads=[Rh, Rc], writes=[Rtb], inc=(k == KD - 1))
            S.op("dve", lambda e: e.tensor_tensor(out=dst, in0=tb[:].rearrange("p (k t) -> p k t", k=KD),
                                                  in1=vecs[:, gcol:gcol + KD].unsqueeze(2).to_broadcast([128, KD, 128]),
                                                  op=ALU.mult), reads=[Rtb, Rc], writes=[Rdst])

        pool_q = []

        scr = {}
        pend_st = []

        def flush_stores():
            while pend_st:
                key_, flat_, R_ = pend_st.pop(0)
                sc_, Rsc_, ssl_ = scr[key_]
                S.dma("sp", ssl_, sc_, flat_, reads=R_, writes=[Rsc_])

        def wload(key, slot, flat, parts, R):
            flush_stores()
            if key not in scr:
                scr[key] = (nc.dram_tensor("sc_" + key, [128, flat.shape[1]], BF16).ap(), Res(), S.newslot("st_" + key))
                for dst_, src_ in parts:
                    S.dma("pool", slot, dst_, src_, writes=R)
                pend_st.append((key, flat, R))
            else:
                sc_, Rsc_, ssl_ = scr[key]
                hslot = {id(slots["wa"]): slots["wa_h"], id(slots["wb0"]): slots["wb0_h"], id(slots["wb1"]): slots["wb1_h"]}[id(slot)]
                S.dma("sp", hslot, flat, sc_, reads=[Rsc_], writes=R)

        def load_w_in():
            wload("w_in", slots["wa"], WA[:, 0:12288], [(w_in_v, kp(dw_in))], [RWA])

        def load_w_q():
            wload("w_q", slots["wa"], WA[:, 0:8192], [(w_q_v, kp(dw_q))], [RWA])

        def load_w_kv():
            wload("w_kv", slots["wb0"], WB[:, 0:16384], [(w_kv_v, kp(dw_kv))], [RWB0, RWB1])

        def load_w_out_o():
            wload("w_out", slots["wb0"], WB[:, 0:8192], [(w_out_v, kp(dw_out))], [RWB0])
            wload("w_o", slots["wb1"], WB[:, 12288:20480], [(w_o_v, kp(dw_o))], [RWB1])

        diag_pending = [True]

        def build_diag():
            for c in range(4):
                for k in range(CW):
                    S.op("pool", lambda e, c=c, k=k: e.tensor_tensor(
                        out=diag[:, c * CW + k, :], in0=hident[:],
                        in1=vcol(V_CW + c * CW + k).to_broadcast([128, 128]), op=ALU.mult),
                        reads=[Rc], writes=[Rdiag])

        groups = [(s, h) for s in range(nseq) for h in range(nhalf)]

        def issue_x(gidx, tiles):
            s_, h_ = groups[gidx]
            for i in tiles:
                S.dma("sp", xslots[i], X[:, i, :], dx[s_, h_ * GT + i * 128: h_ * GT + (i + 1) * 128, :], writes=[RX[i]])
        def p1a_stats(b):
            tl = list(range(b * 4, b * 4 + 4))
            norm_stats([X[:, i, :] for i in tl], [RX[i] for i in tl], 4, squares=True, c0=b * 4)

        def p1a_tile(b, i4):
            i = b * 4 + i4
            norm_tile_to_T(X[:, i, :], RX[i], i, V_GMIX, hTb[:, b, :, i4 * 128:(i4 + 1) * 128], RhTb[b])

        def p1a_norm(b):
            p1a_stats(b)
            for i4 in range(4):
                p1a_tile(b, i4)

        for gi, (s, h) in enumerate(groups):
            tok0 = h * GT
            first = (h == 0)
            last_half = (h == nhalf - 1)
            if gi == 0:
                issue_x(0, range(NT))
                load_w_in()
            if first:
                load_w_kv()
                if diag_pending[0]:
                    diag_pending[0] = False
                    build_diag()
                Rmem = [Rm1, Rm1]
                for sq in (s,):
                    for j in range(2):
                        S.dma("sp", slots["mem"], memx[:, j, :], dmem[sq, j * 128:(j + 1) * 128, :], writes=[Rm1])
                    norm_stats([memx[:, j, :] for j in range(2)], Rmem, 2, c0=8)
                    for j in range(2):
                        norm_tile_to_T(memx[:, j, :], Rmem[j], 8 + j, V_GMEM, memT[:, :, j * 128:(j + 1) * 128], RmemT)
                    for c in range(8):
                        bk, Rb = mbank()
                        mm_group(bk[:, 0:NMEM], [(w_kv_v[:, k, c * 128:(c + 1) * 128], memT[:, k, :]) for k in range(KD)],
                                 [RmemT, RWB0, RWB1], Rb)
                        if c % 2 == 0:
                            S.op("act", lambda e, c=c, bk=bk, sq=sq: e.copy(out=KT[:, 0, c, :], in_=bk[:, 0:NMEM]),
                                 reads=[Rb], writes=[RKV])
                        else:
                            S.op("dve", lambda e, c=c, bk=bk, sq=sq: e.tensor_copy(KT[:, 0, c, :], bk[:, 0:NMEM]),
                                 reads=[Rb], writes=[RKV])
                    for kc in range(2):
                        for hf in range(2):
                            bk, Rb = mbank()
                            mm_group(bk[:], [(memT[:, k, kc * 128:(kc + 1) * 128], w_kv_v[:, k, D + hf * 512:D + (hf + 1) * 512])
                                             for k in range(KD)], [RmemT, RWB0, RWB1], Rb)
                            if hf == 0:
                                S.op("act", lambda e, kc=kc, bk=bk, sq=sq: e.copy(out=Vt[:, 0, kc, 0:512], in_=bk[:]),
                                     reads=[Rb], writes=[RKV])
                            else:
                                S.op("dve", lambda e, kc=kc, bk=bk, sq=sq: e.tensor_copy(Vt[:, 0, kc, 512:1024], bk[:]),
                                     reads=[Rb], writes=[RKV])
                load_w_out_o()

            if first:
                S.op("dve", lambda e: e.memset(gluT[:, :, 0:32], 0.0), writes=[Rglu])
                S.op("dve", lambda e: e.memset(upT[:, :, 0:16], 0.0), writes=[RupT])
            elif True:
                S.op("dve", lambda e: e.tensor_copy(gluT[:, :, 0:32], gtail[:]), reads=[Rtail], writes=[Rglu])
                S.op("dve", lambda e: e.tensor_copy(upT[:, :, 0:16], utail[:]), reads=[Rtail], writes=[RupT])
            def p1a_conv(b, c):
                bg, Rbg = mbank()
                mm_group(bg[:], [(w_in_v[:, k, 512 + c * 128:512 + (c + 1) * 128], hTb[:, b, k, :]) for k in range(KD)],
                         [RhTb[b], RWA], Rbg)
                thb = th[:, c % 2, :]
                Rt = Rth[c % 2]
                S.op("act", lambda e: e.activation(out=thb, in_=bg[:], func=AF.Tanh, scale=0.5), reads=[Rbg], writes=[Rt])
                bv, Rbv = mbank()
                mm_group(bv[:], [(w_in_v[:, k, c * 128:(c + 1) * 128], hTb[:, b, k, :]) for k in range(KD)],
                         [RhTb[b], RWA], Rbv)
                S.op("dve", lambda e: e.scalar_tensor_tensor(
                    out=gluT[:, c, 32 + b * 512:32 + (b + 1) * 512], in0=thb, scalar=1.0, in1=bv[:],
                    op0=ALU.add, op1=ALU.mult), reads=[Rbv, Rt], writes=[Rglu])

            def p1a_pool(b, g):
                w = 2 << g
                bu, Rbu = mbank()
                mm_group(bu[:], [(w_in_v[:, k, 1024 + g * 128:1024 + (g + 1) * 128], hTb[:, b, k, :]) for k in range(KD)],
                         [RhTb[b], RWA], Rbu)
                S.op("act", lambda e: e.copy(out=upT[:, g, 16:528], in_=bu[:]), reads=[Rbu], writes=[RupT])
                src = upT[:, g, :]
                m = 2
                lvl = 0
                while m <= w:
                    dstb = ptmp[:, lvl % 2, :]
                    lo = m - 1
                    hs = m // 2
                    S.op("dve", lambda e, dstb=dstb, src=src, lo=lo, hs=hs: e.tensor_tensor(
                        out=dstb[:, lo:528], in0=src[:, lo:528], in1=src[:, lo - hs:528 - hs], op=ALU.add),
                        reads=[RupT, Rptmp], writes=[Rptmp])
                    src = dstb
                    m *= 2
                    lvl += 1
                S.op("dve", lambda e, src=src: e.scalar_tensor_tensor(
                    out=pooledT[:, g, b * 512:(b + 1) * 512], in0=src[:, 16:528], scalar=1.0 / w, in1=upT[:, g, 16:528],
                    op0=ALU.mult, op1=ALU.subtract), reads=[Rptmp, RupT], writes=[Rpooled])
                if first and b == 0:
                    S.op("dve", lambda e, src=src: e.tensor_tensor(
                        out=smt[:, g, :], in0=src[:, 16:32], in1=invc[:, g, :], op=ALU.mult),
                        reads=[Rptmp, Rc], writes=[Rsm])
                    S.op("dve", lambda e: e.tensor_tensor(
                        out=pooledT[:, g, 0:16], in0=smt[:, g, :], in1=upT[:, g, 16:32], op=ALU.subtract),
                        reads=[Rsm, RupT], writes=[Rpooled])
                S.op("dve", lambda e: e.tensor_copy(upT[:, g, 0:16], upT[:, g, 512:528]),
                     reads=[RupT, Rptmp], writes=[RupT])

            if gi == 0:
                p1a_norm(0)
            p1a_conv(0, 0)
            p1a_conv(0, 1)
            p1a_stats(1)
            p1a_conv(0, 2)
            p1a_tile(1, 0)
            p1a_conv(0, 3)
            p1a_tile(1, 1)
            p1a_pool(0, 0)
            p1a_tile(1, 2)
            p1a_pool(0, 1)
            p1a_tile(1, 3)
            p1a_pool(0, 2)
            p1a_pool(0, 3)
            for c in range(4):
                p1a_conv(1, c)
            for g in range(4):
                p1a_pool(1, g)
            if not last_half:
                S.op("dve", lambda e: e.tensor_copy(utail[:], upT[:, :, 0:16]), reads=[RupT], writes=[Rtail])
            load_w_q()


            def p1b_conv(b):
                bm, Rbm = mbank()
                bq, Rbq = mbank()

                def emit_conv(c):
                    bk, Rb = mbank()
                    c0 = 32 + b * 512 - (CW - 1)
                    mm_group(bk[:], [(diag[:, c * CW + k, :], gluT[:, c, c0 + k:c0 + k + 512]) for k in range(CW)],
                             [Rglu, Rdiag], Rb)
                    S.op("act", lambda e: e.activation(out=hc[:, b, c, :], in_=bk[:], func=AF.Identity,
                                                       bias=vcol(V_CB + c)), reads=[Rb, Rc], writes=[Rhc[b][c]])
                    S.op("act", lambda e: e.activation(out=hsq[:, c % 2, :], in_=bk[:], func=AF.Square,
                                                       bias=vcol(V_CB + c)), reads=[Rb, Rc], writes=[Rhsq[c % 2]])

                def emit_stat(c):
                    S.op("pe", lambda e: e.matmul(bm[:], lhsT=ones32[:], rhs=hc[:, b, c, :], start=(c == 0), stop=(c == 3)),
                         reads=[Rhc[b][c], Rc], writes=[Rbm])
                    S.op("pe", lambda e: e.matmul(bq[:], lhsT=ones32[:], rhs=hsq[:, c % 2, :], start=(c == 0), stop=(c == 3)),
                         reads=[Rhsq[c % 2], Rc], writes=[Rbq])

                emit_conv(0)
                emit_conv(1)
                emit_stat(0)
                emit_conv(2)
                emit_stat(1)
                emit_conv(3)
                emit_stat(2)
                emit_stat(3)
                return bm, Rbm, bq, Rbq

            def p1b_lnstat(b, bm, Rbm, bq, Rbq):
                S.op("act", lambda e: e.activation(out=hsq[:, 0, :], in_=bm[:], func=AF.Square), reads=[Rbm], writes=[Rhsq[0]])
                S.op("act", lambda e: e.copy(out=lst[:, 0, :], in_=bm[:]), reads=[Rbm], writes=[Rlst[0]])
                S.op("dve", lambda e: e.scalar_tensor_tensor(out=lst[:, 1, :], in0=bq[:], scalar=EPS, in1=hsq[:, 0, :],
                                                             op0=ALU.add, op1=ALU.subtract),
                     reads=[Rbq, Rhsq[0]], writes=[Rlst[1]])
                S.op("act", lambda e: e.activation(out=lst[:, 1, :], in_=lst[:, 1, :], func=AF.Sqrt), reads=[Rlst[1]], writes=[Rlst[1]])
                S.op("dve", lambda e: e.reciprocal(out=lst[:, 1, :], in_=lst[:, 1, :]), reads=[Rlst[1]], writes=[Rlst[1]])

            def p1b_lnapply(b):
                def ln_apply(c):
                    hcb = hc[:, b, c, :]
                    S.op("dve", lambda e: e.tensor_tensor(out=hcb, in0=hcb, in1=lst[:, 0, :], op=ALU.subtract),
                         reads=[Rhc[b][c], Rlst[0]], writes=[Rhc[b][c]])
                    S.op("dve", lambda e: e.tensor_tensor(out=hcb, in0=hcb, in1=lst[:, 1, :], op=ALU.mult),
                         reads=[Rlst[1], Rhc[b][c]], writes=[Rhc[b][c]])
                    S.op("act", lambda e: e.activation(out=yT[:, c, :], in_=hcb, func=AF.Silu,
                                                       scale=vcol(V_LNG + c), bias=vcol(V_LNB + c)),
                         reads=[Rhc[b][c], Rc], writes=[RyT])

                for c in range(4):
                    ln_apply(c)

            def p1b_poolproj(b):
                def pool_proj(g):
                    bk, Rb = mbank()
                    mm_group(bk[:], [(poolw[:, g, :], pooledT[:, g, b * 512:(b + 1) * 512])], [Rpooled, Rc], Rb)
                    S.op("act", lambda e: e.activation(out=yT[:, 4 + g, :], in_=bk[:], func=AF.Copy,
                                                       scale=vcol(V_PSC + g)), reads=[Rb, Rc], writes=[RyT])

                for g in range(4):
                    pool_proj(g)

            def p1b_wout(b):
                def wout_tile(i4, hf):
                    i = b * 4 + i4
                    bk, Rb = mbank()
                    mm_group(bk[:], [(yT[:, k, i4 * 128:(i4 + 1) * 128], w_out_v[:, k, hf * 512:(hf + 1) * 512]) for k in range(KD)],
                             [RyT, RWB0], Rb)
                    S.op("dve", lambda e: e.tensor_tensor(
                        out=X[:, i, hf * 512:(hf + 1) * 512], in0=bk[:], in1=X[:, i, hf * 512:(hf + 1) * 512], op=ALU.add),
                        reads=[Rb, RX[i]], writes=[RX[i]])
                    if hf == 1:
                        early_square(i)

                for i4 in range(4):
                    for hf in range(2):
                        wout_tile(i4, hf)

            passes = FFN_PASSES

            def load_pass(p):
                j0, n = passes[p]
                slot = p % 2
                g_, v_, dn_ = ffn_views(slot)
                R = [RWB0] if slot == 0 else [RWB1]
                sl = slots["wb0"] if slot == 0 else slots["wb1"]
                wload("p%d" % p, sl, WB[:, slot * 12288:(slot + 1) * 12288],
                      [(g_[:, :, 0:n * 128], kp(dw_up)[:, :, j0 * 128:(j0 + n) * 128]),
                       (v_[:, :, 0:n * 128], kp(dw_up)[:, :, DFF + j0 * 128:DFF + (j0 + n) * 128]),
                       (dn_[:, 0:n, :], dw_down.rearrange("(j p) n -> p j n", p=128)[:, j0:j0 + n, :])], R)


            def p2_norm(b):
                for i4 in range(4):
                    i = b * 4 + i4
                    norm_tile_to_T(X[:, i, :], RX[i], i, V_GX, hTb2[:, :, i4 * 128:(i4 + 1) * 128], RhTb2)

            def q_chunk(c):
                bk, Rb = mbank()
                mm_group(bk[:], [(w_q_v[:, k, c * 128:(c + 1) * 128], hTb2[:, k, :]) for k in range(KD)], [RhTb2, RWA], Rb)
                if c % 2 == 0:
                    S.op("act", lambda e: e.copy(out=QT[:, c, :], in_=bk[:]), reads=[Rb], writes=[RQT[c // 2]])
                else:
                    S.op("dve", lambda e: e.tensor_copy(QT[:, c, :], bk[:]), reads=[Rb], writes=[RQT[c // 2]])

            def emit_scores(hd):
                pb = hd % 2
                for kc in range(2):
                    bk, Rb = mbank()
                    mm_group(bk[:], [(KT[:, 0, 2 * hd + cc, kc * 128:(kc + 1) * 128], QT[:, 2 * hd + cc, :]) for cc in range(2)],
                             [RKV, RQT[hd]], Rb)
                    S.op("act", lambda e, bk=bk, kc=kc: e.activation(out=PT[:, pb, kc, :], in_=bk[:], func=AF.Exp,
                                                                      scale=1.0 / 16.0), reads=[Rb], writes=[RPT[pb]])

            def emit_pv(b, hd):
                pb = hd % 2
                bs, Rbs = mbank()
                mm_group(bs[:], [(onesb[:], PT[:, pb, kc, :]) for kc in range(2)], [RPT[pb], Rc], Rbs)
                S.op("dve", lambda e: e.reciprocal(out=rr[:, pb, :], in_=bs[:]), reads=[Rbs], writes=[Rrr[pb]])
                for cc in range(2):
                    bo, Rbo = mbank()
                    mm_group(bo[:], [(Vt[:, 0, kc, (2 * hd + cc) * 128:(2 * hd + cc + 1) * 128], PT[:, pb, kc, :]) for kc in range(2)],
                             [RPT[pb], RKV], Rbo)
                    S.op("dve", lambda e, bo=bo, cc=cc: e.tensor_tensor(
                        out=OT[:, b, 2 * hd + cc, :], in0=bo[:], in1=rr[:, pb, :], op=ALU.mult),
                        reads=[Rbo, Rrr[pb]], writes=[ROT[b]])

            def wo_group(b, gidx):
                i4, hf = gidx // 2, gidx % 2
                i = b * 4 + i4
                bk, Rb = mbank()
                mm_group(bk[:], [(OT[:, b, k, i4 * 128:(i4 + 1) * 128], w_o_v[:, k, hf * 512:(hf + 1) * 512]) for k in range(KD)],
                         [ROT[b], RWB1], Rb)
                S.op("dve", lambda e: e.tensor_tensor(
                    out=X[:, i, hf * 512:(hf + 1) * 512], in0=bk[:], in1=X[:, i, hf * 512:(hf + 1) * 512], op=ALU.add),
                    reads=[Rb, RX[i]], writes=[RX[i]])
                if hf == 1:
                    early_square(i)

            st0 = p1b_conv(0)
            p1b_lnstat(0, *st0)
            p1b_lnapply(0)
            st1 = p1b_conv(1)
            p1b_lnstat(1, *st1)
            if not last_half:
                S.op("dve", lambda e: e.tensor_copy(gtail[:], gluT[:, :, GT:GT + 32]), reads=[Rglu], writes=[Rtail])
            p1b_poolproj(0)
            p1b_wout(0)
            p1b_poolproj(1)
            norm_stats([X[:, i, :] for i in range(4)], RX[0:4], 4, squares=False, c0=0)
            p2_norm(0)
            p1b_lnapply(1)
            for c in range(8):
                q_chunk(c)
            p1b_wout(1)
            load_pass(0)
            norm_stats([X[:, i, :] for i in range(4, 8)], RX[4:8], 4, squares=False, c0=4)
            p2_norm(1)
            emit_scores(0)
            emit_scores(1)
            q_chunk(0); q_chunk(1)
            emit_pv(0, 0)
            emit_scores(2)
            q_chunk(2); q_chunk(3)
            emit_pv(0, 1)
            emit_scores(3)
            q_chunk(4); q_chunk(5)
            emit_pv(0, 2)
            q_chunk(6); q_chunk(7)
            emit_pv(0, 3)
            emit_scores(0)
            emit_scores(1)
            wo_group(0, 0); wo_group(0, 1)
            emit_pv(1, 0)
            emit_scores(2)
            wo_group(0, 2); wo_group(0, 3)
            emit_pv(1, 1)
            emit_scores(3)
            wo_group(0, 4); wo_group(0, 5)
            emit_pv(1, 2)
            wo_group(0, 6); wo_group(0, 7)
            emit_pv(1, 3)
            for g_ in range(8):
                wo_group(1, g_)
            load_pass(1)
            if gi + 1 < len(groups):
                load_w_in()

            norm_stats([X[:, i, :] for i in range(NT)], RX, NT, squares=False)
            for i in range(NT):
                norm_tile_to_T(X[:, i, :], RX[i], i, V_GFFN, hTg[:, :, i * 128:(i + 1) * 128], RhTg)
            units = [(p, b) for p in range(len(passes)) for b in range(NBLK)]
            seq_first_blk = first
            ucount = [0]

            def emit_up(p, b, jj, ab):
                j0, n = passes[p]
                slot = p % 2
                g_, v_, dn_ = ffn_views(slot)
                RW = RWB0 if slot == 0 else RWB1
                j = j0 + jj
                pa = (ucount[0]) % 2
                ucount[0] += 1
                gb = h * NBLK + b
                rpar, wpar = gb % 2, (gb + 1) % 2
                banks = []
                for xi, wv in enumerate((g_, v_)):
                    bk, Rb = mbank()
                    mm_group(bk[:], [(wv[:, k, jj * 128:(jj + 1) * 128], hTg[:, k, b * 512:(b + 1) * 512]) for k in range(KD)],
                             [RhTg, RW], Rb)
                    banks.append((bk, Rb))
                for xi in range(2):
                    bk, Rb = banks[xi]
                    ch = j + xi * NFF
                    a = aGV[:, pa, xi, :]
                    Ra = RaGV[pa][xi]
                    U = Ub[:, pa, xi, :]
                    RU = RUb[pa][xi]
                    if not (last_half and b == NBLK - 1):
                        S.op("act", lambda e, bk=bk, ch=ch: e.copy(out=fH[:, wpar, ch, :], in_=bk[:, 510:512]),
                             reads=[Rb], writes=[RfH[wpar]])
                    if first and b == 0:
                        S.op("act", lambda e, U=U: e.activation(out=U[:, 0:2], in_=vecs[:, 0:2], func=AF.Copy, scale=0.0),
                             reads=[Rc], writes=[RU])
                    else:
                        S.op("act", lambda e, U=U, ch=ch: e.copy(out=U[:, 0:2], in_=fH[:, rpar, ch, :]),
                             reads=[RfH[rpar]], writes=[RU])
                    S.op("act", lambda e, bk=bk, U=U: e.copy(out=U[:, 2:514], in_=bk[:]), reads=[Rb], writes=[RU])
                    S.op("act", lambda e, bk=bk, a=a, ch=ch: e.activation(
                        out=a, in_=bk[:], func=AF.Identity, scale=vcol(V_FW + ch * 3 + 2), bias=vcol(V_FB + ch)),
                        reads=[Rb, Rc], writes=[Ra])
                for xi in range(2):
                    ch = j + xi * NFF
                    a = aGV[:, pa, xi, :]
                    Ra = RaGV[pa][xi]
                    U = Ub[:, pa, xi, :]
                    RU = RUb[pa][xi]
                    S.op("dve", lambda e, U=U, a=a, ch=ch: e.scalar_tensor_tensor(
                        out=a, in0=U[:, 1:513], scalar=vcol(V_FW + ch * 3 + 1), in1=a,
                        op0=ALU.mult, op1=ALU.add), reads=[RU, Rc, Ra], writes=[Ra])
                    S.op("dve", lambda e, U=U, a=a, ch=ch: e.scalar_tensor_tensor(
                        out=a, in0=U[:, 0:512], scalar=vcol(V_FW + ch * 3 + 0), in1=a,
                        op0=ALU.mult, op1=ALU.add), reads=[RU, Rc, Ra], writes=[Ra])
                return (pa, ab, jj)

            def emit_gate(pa, ab, jj):
                ag = aGV[:, pa, 0, :]
                S.op("act", lambda e: e.activation(out=ag, in_=ag, func=AF.Silu),
                     reads=[RaGV[pa][0]], writes=[RaGV[pa][0]])
                S.op("dve", lambda e: e.tensor_tensor(out=actT[:, ab, jj, :], in0=ag, in1=aGV[:, pa, 1, :], op=ALU.mult),
                     reads=[RaGV[pa][0], RaGV[pa][1]], writes=[RactT[ab]])

            def emit_down(p, b, ab):
                j0, n = passes[p]
                slot = p % 2
                g_, v_, dn_ = ffn_views(slot)
                RW = RWB0 if slot == 0 else RWB1
                for i4 in range(4):
                    i = b * 4 + i4
                    for hf in range(2):
                        bk, Rb = mbank()
                        mm_group(bk[:], [(actT[:, ab, jj, i4 * 128:(i4 + 1) * 128], dn_[:, jj, hf * 512:(hf + 1) * 512]) for jj in range(n)],
                                 [RactT[ab], RW], Rb)
                        S.op("dve", lambda e, bk=bk, i=i, hf=hf: e.tensor_tensor(
                            out=X[:, i, hf * 512:(hf + 1) * 512], in0=bk[:], in1=X[:, i, hf * 512:(hf + 1) * 512], op=ALU.add),
                            reads=[Rb, RX[i]], writes=[RX[i]])

            def emit_final(b):
                tiles = [b * 4 + i4 for i4 in range(4)]
                for i in tiles:
                    S.op("act", lambda e, i=i: e.activation(out=hb2[:, i % 2, :], in_=X[:, i, :], func=AF.Square,
                                                            accum_out=ssb[:, i:i + 1]),
                         reads=[RX[i]], writes=[Rhb2[i % 2], Rss])
                lo_, hi_ = tiles[0], tiles[-1] + 1
                S.op("dve", lambda e: e.tensor_scalar(out=rstd[:, lo_:hi_], in0=ssb[:, lo_:hi_], scalar1=1.0 / D, scalar2=EPS,
                                                      op0=ALU.mult, op1=ALU.add), reads=[Rss], writes=[Rrstd])
                S.op("act", lambda e: e.activation(out=rstd[:, lo_:hi_], in_=rstd[:, lo_:hi_], func=AF.Sqrt),
                     reads=[Rrstd], writes=[Rrstd])
                S.op("dve", lambda e: e.reciprocal(out=rstd[:, lo_:hi_], in_=rstd[:, lo_:hi_]), reads=[Rrstd], writes=[Rrstd])
                for i in tiles:
                    ob = i % 2
                    S.op("dve", lambda e, i=i: e.scalar_tensor_tensor(
                        out=X[:, i, :], in0=X[:, i, :], scalar=rstd[:, i:i + 1], in1=gfinb[:], op0=ALU.mult, op1=ALU.mult),
                        reads=[RX[i], Rrstd, Rc], writes=[RX[i]])
                    S.dma("sp", oslots[i], dout[s, tok0 + i * 128: tok0 + (i + 1) * 128, :], X[:, i, :],
                          reads=[RX[i]], writes=[Rout[i]])

            prev = None
            pend_load = []
            STAGED = False
            pend_gate = None
            for ui, (p, b) in enumerate(units):
                j0, n = passes[p]
                ab = ui % 2
                g0 = emit_up(p, b, 0, ab)
                if STAGED:
                    if pend_gate is not None:
                        emit_gate(*pend_gate)
                    pend_gate = g0
                else:
                    emit_gate(*g0)
                if prev is not None:
                    pp, pb_, pab = prev
                    emit_down(pp, pb_, pab)
                    if pp == len(passes) - 1:
                        emit_final(pb_)
                        if gi + 1 < len(groups):
                            issue_x(gi + 1, range(pb_ * 4, pb_ * 4 + 4))
                    if pb_ == NBLK - 1 and pp + 2 < len(passes):
                        load_pass(pp + 2)
                for jj in range(1, n):
                    gj = emit_up(p, b, jj, ab)
                    if STAGED:
                        emit_gate(*pend_gate)
                        pend_gate = gj
                    else:
                        emit_gate(*gj)
                    if jj == 1 and pend_load:
                        load_pass(pend_load.pop())
                prev = (p, b, ab)
            if STAGED:
                emit_gate(*pend_gate)
            pp, pb_, pab = prev
            emit_down(pp, pb_, pab)
            if gi + 1 < len(groups):
                p1a_norm(0)
            emit_final(pb_)
            if gi + 1 < len(groups):
                issue_x(gi + 1, range(pb_ * 4, pb_ * 4 + 4))
            if gi + 1 < len(groups) and groups[gi + 1][1] != 0:
                load_w_out_o()

        flush_stores()
        S.final_wait("sp", Rout + [v_[1] for v_ in scr.values()])
        S.emit()
    return nc


_tt_small_cache = {}


def _prep_inputs(inputs):
    f = lambda a: np.ascontiguousarray(np.asarray(a, dtype=np.float32))
    vec = np.zeros((128, V_N), np.float32)

    def fm(v, n):
        return np.asarray(v, np.float32).reshape(n, 128).T

    vec[:, V_GMIX:V_GMIX + 8] = fm(inputs["norm_mix_g"][0], 8)
    vec[:, V_GX:V_GX + 8] = fm(inputs["norm_xattn_g"][0], 8)
    vec[:, V_GMEM:V_GMEM + 8] = fm(inputs["norm_mem_g"][0], 8)
    vec[:, V_GFFN:V_GFFN + 8] = fm(inputs["norm_ffn_g"][0], 8)
    vec[:, V_CB:V_CB + 4] = fm(inputs["conv_dw_b"][0], 4)
    vec[:, V_LNG:V_LNG + 4] = fm(inputs["conv_ln_g"][0], 4)
    vec[:, V_LNB:V_LNB + 4] = fm(inputs["conv_ln_b"][0], 4)
    vec[:, V_PSC:V_PSC + 4] = fm(inputs["pool_scale"][0], 4)
    cw = np.asarray(inputs["conv_dw_w"][0], np.float32)
    vec[:, V_CW:V_CW + 4 * CW] = cw.reshape(CW, 4, 128).transpose(2, 1, 0).reshape(128, 4 * CW)
    fw = np.asarray(inputs["ffn_dw_w"][0], np.float32)
    vec[:, V_FW:V_FW + 132] = fw.reshape(3, 44, 128).transpose(2, 1, 0).reshape(128, 132)
    vec[:, V_FB:V_FB + 44] = fm(inputs["ffn_dw_b"][0], 44)
    gfinb = np.ascontiguousarray(np.broadcast_to(np.asarray(inputs["norm_final_g"], np.float32)[None, :], (128, D)))
    shared = {
        "vecs": vec, "gfinb": gfinb,
        "w_in": f(inputs["w_in"][0]), "pool_w": f(inputs["pool_w"][0]), "w_out": f(inputs["w_out"][0]),
        "w_q": f(inputs["w_q"][0]), "w_kv": f(inputs["w_kv"][0]), "w_o": f(inputs["w_o"][0]),
        "w_up": f(inputs["w_up"][0]), "w_down": f(inputs["w_down"][0]),
    }
    return shared


def kernel(**inputs):
    x = np.asarray(inputs["x"], np.float32)
    mem = np.asarray(inputs["mem"], np.float32)
    shared = _prep_inputs(inputs)
    nc = build_program()
    in_maps = []
    for c in range(NCORES):
        m = dict(shared)
        m["x"] = np.ascontiguousarray(x[c * SEQ_PER_CORE:(c + 1) * SEQ_PER_CORE])
        m["mem"] = np.ascontiguousarray(mem[c * SEQ_PER_CORE:(c + 1) * SEQ_PER_CORE])
        in_maps.append(m)
    res = run_bass_kernel_spmd(nc, in_maps, core_ids=list(range(NCORES)))
    out = np.concatenate([np.asarray(r["out"], np.float32) for r in res.results], axis=0)
    return out
```

```python
import numpy as np
from contextlib import ExitStack
import concourse.bass as bass
import concourse.mybir as mybir
from concourse.bass_utils import run_bass_kernel_spmd

F32 = mybir.dt.float32
BF16 = mybir.dt.bfloat16
AF = mybir.ActivationFunctionType
ALU = mybir.AluOpType

NCORES = 8
SEQ_PER_CORE = 2
D = 1024
KD = 8
SEQ = 2048
GT = 1024
NT = GT // 128
NBLK = GT // 512
NMEM = 256
DFF = 2816
NFF = DFF // 128
CW = 31
EPS = 1e-6
FFN_G = 4
FFN_PASSES = [(0, 4), (4, 4), (8, 4), (12, 4), (16, 3), (19, 3)]

V_GMIX, V_GX, V_GMEM, V_GFFN = 0, 8, 16, 24
V_CB, V_LNG, V_LNB, V_PSC = 32, 36, 40, 44
V_CW = 48
V_FW = V_CW + 4 * CW
V_FB = V_FW + 44 * 3
V_N = V_FB + 44


class Res:
    __slots__ = ("w", "r", "ov", "lo", "hi")

    def __init__(self, lo=None, hi=None):
        self.w = None
        self.r = []
        self.ov = []
        self.lo = lo
        self.hi = hi


class Sched:
    ENG = ("pe", "act", "dve", "pool", "sp")

    def __init__(self, nc, stack):
        self.nc = nc
        self.stack = stack
        self.prog = {e: [] for e in self.ENG}
        self.sem = {e: stack.enter_context(nc.semaphore("s_" + e)) for e in self.ENG}
        self.cnt = {e: 0 for e in self.ENG}
        self.waited = {e: {} for e in self.ENG}
        self.fence_tok = []
        self.abs = {e: [] for e in self.ENG}

    def check(self):
        val = {}
        ptr = {e: 0 for e in self.ENG}
        progress = True
        while progress:
            progress = False
            for e in self.ENG:
                while ptr[e] < len(self.abs[e]):
                    waits, sem, n = self.abs[e][ptr[e]]
                    if all(val.get(id(s_), 0) >= v for s_, v in waits):
                        if sem is not None:
                            val[id(sem)] = val.get(id(sem), 0) + n
                        ptr[e] += 1
                        progress = True
                    else:
                        break
        stuck = {e: (ptr[e], len(self.abs[e])) for e in self.ENG if ptr[e] < len(self.abs[e])}
        return stuck

    def _deps(self, eng, reads, writes, extra=(), skip_key=None):
        deps = {}

        def add(tok):
            if tok is None:
                return
            k, v = tok
            if deps.get(k, 0) < v:
                deps[k] = v

        for r in reads:
            add(r.w)
            for o in r.ov:
                add(o.w)
        for w in writes:
            add(w.w)
            for t in w.r:
                add(t)
            for o in w.ov:
                add(o.w)
                for t in o.r:
                    add(t)
        for t in extra:
            add(t)
        out = []
        for k, v in deps.items():
            if k == "pe" and eng == "pe":
                continue
            if skip_key is not None and k is skip_key:
                continue
            if self.waited[eng].get(k, 0) >= v:
                continue
            self.waited[eng][k] = v
            out.append((self.sem[k] if isinstance(k, str) else k, v))
        return out

    def _mark(self, tok, reads, writes):
        for r in reads:
            r.r.append(tok)
            if len(r.r) > 64:
                best = {}
                for k, v in r.r:
                    if best.get(k, 0) < v:
                        best[k] = v
                r.r = list(best.items())
        for w in writes:
            w.w = tok
            w.r = []

    def op(self, eng, fn, reads=(), writes=(), inc=True):
        waits = self._deps(eng, reads, writes)
        if inc:
            self.cnt[eng] += 1
        tok = (eng, self.cnt[eng] if inc else self.cnt[eng] + 1)
        sem = self.sem[eng]

        def thunk(e, waits=waits, fn=fn, inc=inc, sem=sem):
            for s, v in waits:
                e.wait_ge(s, v)
            ins = fn(e)
            if inc:
                ins.then_inc(sem, 1)

        self.prog[eng].append(thunk)
        self.abs[eng].append((waits, sem if inc else None, 1))
        self._mark(tok, reads, writes)
        return tok

    def newslot(self, name):
        return {"sem": self.stack.enter_context(self.nc.semaphore(name)), "cnt": 0}

    def dma(self, eng, slot, out, in_, reads=(), writes=(), extra=(), **kw):
        waits = self._deps(eng, reads, writes, extra, skip_key=slot["sem"])
        slot["cnt"] += 16
        tok = (slot["sem"], slot["cnt"])

        def thunk(e, waits=waits, out=out, in_=in_, sem=slot["sem"], kw=kw):
            for s, v in waits:
                e.wait_ge(s, v)
            e.dma_start(out=out, in_=in_, **kw).then_inc(sem, 16)

        self.prog[eng].append(thunk)
        self.abs[eng].append((waits, slot["sem"], 16))
        self._mark(tok, reads, writes)
        return tok

    def fence(self):
        comp = ("pe", "act", "dve")
        snap = [(e, self.cnt[e]) for e in comp if self.cnt[e] > 0]
        self.fence_tok = snap
        for e in comp:
            waits = []
            for p, v in snap:
                if p == e or self.waited[e].get(p, 0) >= v:
                    continue
                self.waited[e][p] = v
                waits.append((self.sem[p], v))
            if waits:
                self.prog[e].append(lambda eng, waits=waits: [eng.wait_ge(s, v) for s, v in waits])
                self.abs[e].append((waits, None, 0))

    def final_wait(self, eng, resources):
        waits = self._deps(eng, resources, ())
        self.prog[eng].append(lambda e, waits=waits: [e.wait_ge(s, v) for s, v in waits])

    def emit(self):
        with self.nc.Block() as block:
            @block.tensor
            def _(e):
                for t in self.prog["pe"]:
                    t(e)

            @block.scalar
            def _(e):
                for t in self.prog["act"]:
                    t(e)

            @block.vector
            def _(e):
                for t in self.prog["dve"]:
                    t(e)

            @block.gpsimd
            def _(e):
                for t in self.prog["pool"]:
                    t(e)

            @block.sync
            def _(e):
                for t in self.prog["sp"]:
                    t(e)


def build_program(nseq=SEQ_PER_CORE, seqlen=SEQ):
    nc = bass.Bass("TRN2", target_bir_lowering=False)
    nhalf = seqlen // GT
    dx = nc.dram_tensor("x", [nseq, seqlen, D], F32, kind="ExternalInput").ap()
    dmem = nc.dram_tensor("mem", [nseq, NMEM, D], F32, kind="ExternalInput").ap()
    dvec = nc.dram_tensor("vecs", [128, V_N], F32, kind="ExternalInput").ap()
    dgfin = nc.dram_tensor("gfinb", [128, D], F32, kind="ExternalInput").ap()
    dw_in = nc.dram_tensor("w_in", [D, 1536], F32, kind="ExternalInput").ap()
    dpoolw = nc.dram_tensor("pool_w", [4, 128, 128], F32, kind="ExternalInput").ap()
    dw_out = nc.dram_tensor("w_out", [D, D], F32, kind="ExternalInput").ap()
    dw_q = nc.dram_tensor("w_q", [D, D], F32, kind="ExternalInput").ap()
    dw_kv = nc.dram_tensor("w_kv", [D, 2 * D], F32, kind="ExternalInput").ap()
    dw_o = nc.dram_tensor("w_o", [D, D], F32, kind="ExternalInput").ap()
    dw_up = nc.dram_tensor("w_up", [D, 2 * DFF], F32, kind="ExternalInput").ap()
    dw_down = nc.dram_tensor("w_down", [DFF, D], F32, kind="ExternalInput").ap()
    dout = nc.dram_tensor("out", [nseq, seqlen, D], F32, kind="ExternalOutput").ap()

    kp = lambda ap: ap.rearrange("(k p) n -> p k n", p=128)

    with ExitStack() as st:
        S = Sched(nc, st)
        base = (nc._sbuf_addr_for_side("left") + 63) // 64 * 64
        top = nc._sbuf_addr_for_side("right")
        cur = [base]
        nalloc = [0]
        binfo = {}
        regs = []

        def alloc(shape, dt, at=None):
            nb = 2 if dt == BF16 else 4
            sz = nb
            for s_ in shape[1:]:
                sz *= s_
            sz = (sz + 63) // 64 * 64
            if at is None:
                off = cur[0]
                cur[0] += sz
            else:
                off = at[0]
                at[0] += sz
            assert off + sz <= top, ("SBUF overflow", off + sz - top)
            nalloc[0] += 1
            t_ = nc.alloc_sbuf_tensor_at("t%d" % nalloc[0], list(shape), dt, offset=off)
            binfo[id(t_)] = (off, sz)
            return t_

        def REG(buf, idx=None, nsub=1, n=1):
            off, sz = binfo[id(buf)]
            if idx is None:
                lo, hi = off, off + sz
            else:
                sub = sz // nsub
                lo, hi = off + idx * sub, off + (idx + n) * sub
            r_ = Res(lo, hi)
            regs.append(r_)
            return r_

        def link_regs():
            for i_, a_ in enumerate(regs):
                for b_ in regs[i_ + 1:]:
                    if a_.lo < b_.hi and b_.lo < a_.hi:
                        a_.ov.append(b_)
                        b_.ov.append(a_)

        ident = alloc([128, 128], BF16)
        hident = alloc([128, 128], BF16)
        ones32 = alloc([128, 128], F32)
        onesb = alloc([128, 128], BF16)
        vecs = alloc([128, V_N], F32)
        gfinb = alloc([128, D], F32)
        invc = alloc([128, 4, 16], F32)
        fH = alloc([128, 2, 44, 2], F32)
        poolw = alloc([128, 4, 128], BF16)
        ssb = alloc([128, 16], F32)
        rstd = alloc([128, 16], F32)
        diag = alloc([128, 4 * CW, 128], BF16)
        KT = alloc([128, 1, 8, NMEM], BF16)
        Vt = alloc([128, 1, 2, D], BF16)
        X = alloc([128, NT, D], F32)
        WA = alloc([128, 12288], BF16)
        WB = alloc([128, 24576], BF16)
        hb2 = alloc([128, 2, D], BF16)
        smt = alloc([128, 4, 16], F32)
        gtail = alloc([128, 4, 32], BF16)
        utail = alloc([128, 4, 16], F32)
        ov_base = cur[0]
        identf = alloc([128, 128], F32, [ov_base + 16384])

        w_in_v = WA[:, 0:8 * 1536].rearrange("p (k n) -> p k n", k=8)
        w_q_v = WA[:, 0:8 * 1024].rearrange("p (k n) -> p k n", k=8)
        w_kv_v = WB[:, 0:16384].rearrange("p (k n) -> p k n", k=8)
        w_out_v = WB[:, 0:8192].rearrange("p (k n) -> p k n", k=8)
        w_o_v = WB[:, 12288:20480].rearrange("p (k n) -> p k n", k=8)

        def ffn_views(slot):
            b0 = slot * 12288
            g = WB[:, b0:b0 + 4096].rearrange("p (k n) -> p k n", k=8)
            v = WB[:, b0 + 4096:b0 + 8192].rearrange("p (k n) -> p k n", k=8)
            dn = WB[:, b0 + 8192:b0 + 12288].rearrange("p (j n) -> p j n", j=4)
            return g, v, dn

        o = [ov_base]
        memx = alloc([128, 2, D], F32, o)
        memT = alloc([128, 8, NMEM], BF16, o)
        o = [ov_base]
        gluT = alloc([128, 4, 32 + GT], BF16, o)
        pooledT = alloc([128, 4, GT], BF16, o)
        o1 = [o[0]]
        hTb = alloc([128, 2, 8, 512], BF16, o1)
        upT = alloc([128, 4, 528], F32, o1)
        ptmp = alloc([128, 2, 528], F32, o1)
        th = alloc([128, 2, 512], F32, o1)
        o1 = [o[0]]
        yT = alloc([128, 8, 512], BF16, o1)
        hc = alloc([128, 2, 4, 512], F32, o1)
        hsq = alloc([128, 2, 512], F32, o1)
        lst = alloc([128, 2, 512], F32, o1)
        o = [ov_base]
        hTb2 = alloc([128, 8, 512], BF16, o)
        QT = alloc([128, 8, 512], BF16, o)
        PT = alloc([128, 2, 2, 512], BF16, o)
        OT = alloc([128, 2, 8, 512], BF16, o)
        rr = alloc([128, 2, 512], F32, o)
        o = [ov_base]
        hTg = alloc([128, 8, GT], BF16, o)
        actT = alloc([128, 2, FFN_G, 512], BF16, o)
        aGV = alloc([128, 2, 2, 512], F32, o)
        Ub = alloc([128, 2, 2, 514], F32, o)

        ptb = [st.enter_context(nc.psum_tensor("ptb%d" % i, [128, 1024], BF16)) for i in range(2)]
        pmb = [st.enter_context(nc.psum_tensor("pmb%d" % i, [128, 512], F32)) for i in range(6)]
        Rptb = [Res() for _ in range(2)]
        Rpmb = [Res() for _ in range(6)]
        rot = {"t": 0, "m": 0}

        def tbank():
            i = rot["t"] % 2
            rot["t"] += 1
            return ptb[i], Rptb[i]

        def mbank():
            i = rot["m"] % 6
            rot["m"] += 1
            return pmb[i], Rpmb[i]

        Rc = Res()
        RX = [Res() for _ in range(NT)]
        RWA, RWB0, RWB1 = Res(), Res(), Res()
        Rdiag, RKV = Res(), Res()
        Rss, Rrstd, Rhb2, Rsm, Rtail = Res(), Res(), [Res(), Res()], Res(), Res()
        RfH = [Res(), Res()]
        Rout = [Res() for _ in range(NT)]
        Rm1, RmemT = REG(memx), REG(memT)
        Rglu, Rpooled, RupT, Rptmp = REG(gluT), REG(pooledT), REG(upT), REG(ptmp)
        RhTb = [REG(hTb, 0, 2), REG(hTb, 1, 2)]
        Rth = [REG(th, i, 2) for i in range(2)]
        RyT = REG(yT)
        Rhc = [[REG(hc, b_ * 4 + c, 8) for c in range(4)] for b_ in range(2)]
        Rhsq = [REG(hsq, i, 2) for i in range(2)]
        Rlst = [REG(lst, 0, 2), REG(lst, 1, 2)]
        RhTb2 = REG(hTb2)
        RQT = [REG(QT, 2 * hd, 8, 2) for hd in range(4)]
        RPT = [REG(PT, i, 2) for i in range(2)]
        ROT = [REG(OT, i, 2) for i in range(2)]
        Rrr = [REG(rr, i, 2) for i in range(2)]
        RhTg = REG(hTg)
        RactT = [REG(actT, i, 2) for i in range(2)]
        RaGV = [[REG(aGV, pa * 2 + xi, 4) for xi in range(2)] for pa in range(2)]
        RUb = [[REG(Ub, pa * 2 + xi, 4) for xi in range(2)] for pa in range(2)]
        Ridf = REG(identf)
        link_regs()

        slots = {n: S.newslot("d_" + n) for n in
                 ("c", "x0", "x1", "x2", "x3", "x4", "x5", "x6", "x7", "wa", "wb0", "wb1", "wa_h", "wb0_h", "wb1_h", "o0", "o1", "o2", "o3", "o4", "o5", "o6", "o7", "mem", "pw")}
        xslots = [slots["x%d" % i] for i in range(NT)]
        oslots = [slots["o%d" % i] for i in range(NT)]

        def vcol(c0, n=1):
            return vecs[:, c0:c0 + n]

        def mm_group(bank_ap, pairs, reads, Rbank, extra_reads=()):
            n = len(pairs)
            for i, (l, r) in enumerate(pairs):
                S.op("pe", lambda e, l=l, r=r, i=i: e.matmul(bank_ap, lhsT=l, rhs=r, start=(i == 0), stop=(i == n - 1)),
                     reads=list(reads) + list(extra_reads), writes=[Rbank], inc=(i == n - 1))

        def early_square(i):
            S.op("act", lambda e: e.activation(out=hb2[:, i % 2, :], in_=X[:, i, :], func=AF.Square,
                                               accum_out=ssb[:, i:i + 1]),
                 reads=[RX[i]], writes=[Rhb2[i % 2], Rss])

        def norm_stats(xtiles, Rx, ntile, squares=True, c0=0):
            for i in range(ntile if squares else 0):
                S.op("act", lambda e, i=i: e.activation(out=hb2[:, i % 2, :], in_=xtiles[i], func=AF.Square,
                                                        accum_out=ssb[:, c0 + i:c0 + i + 1]),
                     reads=[Rx[i]], writes=[Rhb2[i % 2], Rss])
            S.op("dve", lambda e: e.tensor_scalar(out=rstd[:, c0:c0 + ntile], in0=ssb[:, c0:c0 + ntile], scalar1=1.0 / D,
                                                  scalar2=EPS, op0=ALU.mult, op1=ALU.add), reads=[Rss], writes=[Rrstd])
            S.op("act", lambda e: e.activation(out=rstd[:, c0:c0 + ntile], in_=rstd[:, c0:c0 + ntile], func=AF.Sqrt),
                 reads=[Rrstd], writes=[Rrstd])
            S.op("dve", lambda e: e.reciprocal(out=rstd[:, c0:c0 + ntile], in_=rstd[:, c0:c0 + ntile]),
                 reads=[Rrstd], writes=[Rrstd])

        def norm_tile_to_T(xt, Rxt, i, gcol, dst, Rdst):
            hbi = hb2[:, i % 2, :]
            Rh = Rhb2[i % 2]
            S.op("act", lambda e: e.activation(out=hbi, in_=xt, func=AF.Copy, scale=rstd[:, i:i + 1]),
                 reads=[Rxt, Rrstd], writes=[Rh])
            tb, Rtb = tbank()
            for k in range(KD):
                S.op("pe", lambda e, k=k: e.transpose(out=tb[:, k * 128:(k + 1) * 128], in_=hbi[:, k * 128:(k + 1) * 128],
                                                      identity=ident[:]),
                     reads=[Rh, Rc], writes=[Rtb], inc=(k == KD - 1))
            S.op("dve", lambda e: e.tensor_tensor(out=dst, in0=tb[:].rearrange("p (k t) -> p k t", k=KD),
                                                  in1=vecs[:, gcol:gcol + KD].unsqueeze(2).to_broadcast([128, KD, 128]),
                                                  op=ALU.mult), reads=[Rtb, Rc], writes=[Rdst])

        pool_q = []

        scr = {}
        pend_st = []

        def flush_stores():
            while pend_st:
                key_, flat_, R_ = pend_st.pop(0)
                sc_, Rsc_, ssl_ = scr[key_]
                S.dma("sp", ssl_, sc_, flat_, reads=R_, writes=[Rsc_])

        def wload(key, slot, flat, parts, R):
            flush_stores()
            if key not in scr:
                scr[key] = (nc.dram_tensor("sc_" + key, [128, flat.shape[1]], BF16).ap(), Res(), S.newslot("st_" + key))
                for dst_, src_ in parts:
                    S.dma("pool", slot, dst_, src_, writes=R)
                pend_st.append((key, flat, R))
            else:
                sc_, Rsc_, ssl_ = scr[key]
                hslot = {id(slots["wa"]): slots["wa_h"], id(slots["wb0"]): slots["wb0_h"], id(slots["wb1"]): slots["wb1_h"]}[id(slot)]
                S.dma("sp", hslot, flat, sc_, reads=[Rsc_], writes=R)

        def load_w_in():
            wload("w_in", slots["wa"], WA[:, 0:12288], [(w_in_v, kp(dw_in))], [RWA])

        def load_w_q():
            wload("w_q", slots["wa"], WA[:, 0:8192], [(w_q_v, kp(dw_q))], [RWA])

        def load_w_kv():
            wload("w_kv", slots["wb0"], WB[:, 0:16384], [(w_kv_v, kp(dw_kv))], [RWB0, RWB1])

        def load_w_out_o():
            wload("w_out", slots["wb0"], WB[:, 0:8192], [(w_out_v, kp(dw_out))], [RWB0])
            wload("w_o", slots["wb1"], WB[:, 12288:20480], [(w_o_v, kp(dw_o))], [RWB1])

        diag_pending = [True]

        def build_diag():
            for c in range(4):
                for k in range(CW):
                    S.op("pool", lambda e, c=c, k=k: e.tensor_tensor(
                        out=diag[:, c * CW + k, :], in0=hident[:],
                        in1=vcol(V_CW + c * CW + k).to_broadcast([128, 128]), op=ALU.mult),
                        reads=[Rc], writes=[Rdiag])

        groups = [(s, h) for s in range(nseq) for h in range(nhalf)]

        def issue_x(gidx, tiles):
            s_, h_ = groups[gidx]
            for i in tiles:
                S.dma("sp", xslots[i], X[:, i, :], dx[s_, h_ * GT + i * 128: h_ * GT + (i + 1) * 128, :], writes=[RX[i]])
        S.dma("sp", slots["c"], vecs[:], dvec, writes=[Rc])
        S.dma("sp", slots["c"], gfinb[:], dgfin, writes=[Rc])
        def mem_loads(sq):
            for j in range(2):
                S.dma("sp", slots["mem"], memx[:, j, :], dmem[sq, j * 128:(j + 1) * 128, :], writes=[Rm1])

        issue_x(0, range(NT))
        mem_loads(0)
        load_w_in()
        load_w_kv()
        S.dma("pool", slots["pw"], poolw[:], dpoolw.rearrange("g c d -> c g d"), writes=[Rc])
        S.op("pool", lambda e: e.memset(identf[:], 0.0), writes=[Rc, Ridf])
        S.op("pool", lambda e: e.affine_select(out=identf[:], in_=identf[:], compare_op=ALU.not_equal, fill=1.0,
                                               base=0, pattern=[[-1, 128]], channel_multiplier=1),
             reads=[Rc, Ridf], writes=[Rc, Ridf])
        S.op("pool", lambda e: e.tensor_copy(ident[:], identf[:]), reads=[Rc, Ridf], writes=[Rc])
        S.op("dve", lambda e: e.tensor_scalar(out=hident[:], in0=ident[:], scalar1=0.5, scalar2=None, op0=ALU.mult),
             reads=[Rc], writes=[Rc])
        S.op("pool", lambda e: e.memset(ones32[:], 1.0 / 512.0), writes=[Rc])
        S.op("pool", lambda e: e.memset(onesb[:], 1.0), writes=[Rc])
        for g in range(4):
            w = 2 << g
            S.op("pool", lambda e, g=g, w=w: e.memset(invc[:, g, :], 1.0 / w), writes=[Rc])
            for t in range(w - 1):
                S.op("pool", lambda e, g=g, t=t: e.memset(invc[:, g, t:t + 1], 1.0 / (t + 1)), writes=[Rc])

        def p1a_stats(b):
            tl = list(range(b * 4, b * 4 + 4))
            norm_stats([X[:, i, :] for i in tl], [RX[i] for i in tl], 4, squares=True, c0=b * 4)

        def p1a_tile(b, i4):
            i = b * 4 + i4
            norm_tile_to_T(X[:, i, :], RX[i], i, V_GMIX, hTb[:, b, :, i4 * 128:(i4 + 1) * 128], RhTb[b])

        def p1a_norm(b):
            p1a_stats(b)
            for i4 in range(4):
                p1a_tile(b, i4)

        for gi, (s, h) in enumerate(groups):
            tok0 = h * GT
            first = (h == 0)
            last_half = (h == nhalf - 1)
            if first:
                if gi > 0:
                    mem_loads(s)
                    load_w_kv()
                if diag_pending[0]:
                    diag_pending[0] = False
                    build_diag()
                Rmem = [Rm1, Rm1]
                for sq in (s,):
                    norm_stats([memx[:, j, :] for j in range(2)], Rmem, 2, c0=8)
                    for j in range(2):
                        norm_tile_to_T(memx[:, j, :], Rmem[j], 8 + j, V_GMEM, memT[:, :, j * 128:(j + 1) * 128], RmemT)
                    for c in range(8):
                        bk, Rb = mbank()
                        mm_group(bk[:, 0:NMEM], [(w_kv_v[:, k, c * 128:(c + 1) * 128], memT[:, k, :]) for k in range(KD)],
                                 [RmemT, RWB0, RWB1], Rb)
                        if c % 2 == 0:
                            S.op("act", lambda e, c=c, bk=bk, sq=sq: e.copy(out=KT[:, 0, c, :], in_=bk[:, 0:NMEM]),
                                 reads=[Rb], writes=[RKV])
                        else:
                            S.op("dve", lambda e, c=c, bk=bk, sq=sq: e.tensor_copy(KT[:, 0, c, :], bk[:, 0:NMEM]),
                                 reads=[Rb], writes=[RKV])
                    for kc in range(2):
                        for hf in range(2):
                            bk, Rb = mbank()
                            mm_group(bk[:], [(memT[:, k, kc * 128:(kc + 1) * 128], w_kv_v[:, k, D + hf * 512:D + (hf + 1) * 512])
                                             for k in range(KD)], [RmemT, RWB0, RWB1], Rb)
                            if hf == 0:
                                S.op("act", lambda e, kc=kc, bk=bk, sq=sq: e.copy(out=Vt[:, 0, kc, 0:512], in_=bk[:]),
                                     reads=[Rb], writes=[RKV])
                            else:
                                S.op("dve", lambda e, kc=kc, bk=bk, sq=sq: e.tensor_copy(Vt[:, 0, kc, 512:1024], bk[:]),
                                     reads=[Rb], writes=[RKV])
                load_w_out_o()

            if first:
                S.op("dve", lambda e: e.memset(gluT[:, :, 0:32], 0.0), writes=[Rglu])
                S.op("dve", lambda e: e.memset(upT[:, :, 0:16], 0.0), writes=[RupT])
            elif True:
                S.op("dve", lambda e: e.tensor_copy(gluT[:, :, 0:32], gtail[:]), reads=[Rtail], writes=[Rglu])
                S.op("dve", lambda e: e.tensor_copy(upT[:, :, 0:16], utail[:]), reads=[Rtail], writes=[RupT])
            def p1a_conv(b, c):
                bg, Rbg = mbank()
                mm_group(bg[:], [(w_in_v[:, k, 512 + c * 128:512 + (c + 1) * 128], hTb[:, b, k, :]) for k in range(KD)],
                         [RhTb[b], RWA], Rbg)
                thb = th[:, c % 2, :]
                Rt = Rth[c % 2]
                S.op("act", lambda e: e.activation(out=thb, in_=bg[:], func=AF.Tanh, scale=0.5), reads=[Rbg], writes=[Rt])
                bv, Rbv = mbank()
                mm_group(bv[:], [(w_in_v[:, k, c * 128:(c + 1) * 128], hTb[:, b, k, :]) for k in range(KD)],
                         [RhTb[b], RWA], Rbv)
                S.op("dve", lambda e: e.scalar_tensor_tensor(
                    out=gluT[:, c, 32 + b * 512:32 + (b + 1) * 512], in0=thb, scalar=1.0, in1=bv[:],
                    op0=ALU.add, op1=ALU.mult), reads=[Rbv, Rt], writes=[Rglu])

            def p1a_pool(b, g):
                w = 2 << g
                bu, Rbu = mbank()
                mm_group(bu[:], [(w_in_v[:, k, 1024 + g * 128:1024 + (g + 1) * 128], hTb[:, b, k, :]) for k in range(KD)],
                         [RhTb[b], RWA], Rbu)
                S.op("act", lambda e: e.copy(out=upT[:, g, 16:528], in_=bu[:]), reads=[Rbu], writes=[RupT])
                src = upT[:, g, :]
                m = 2
                lvl = 0
                while m <= w:
                    dstb = ptmp[:, lvl % 2, :]
                    lo = m - 1
                    hs = m // 2
                    S.op("dve", lambda e, dstb=dstb, src=src, lo=lo, hs=hs: e.tensor_tensor(
                        out=dstb[:, lo:528], in0=src[:, lo:528], in1=src[:, lo - hs:528 - hs], op=ALU.add),
                        reads=[RupT, Rptmp], writes=[Rptmp])
                    src = dstb
                    m *= 2
                    lvl += 1
                S.op("dve", lambda e, src=src: e.scalar_tensor_tensor(
                    out=pooledT[:, g, b * 512:(b + 1) * 512], in0=src[:, 16:528], scalar=1.0 / w, in1=upT[:, g, 16:528],
                    op0=ALU.mult, op1=ALU.subtract), reads=[Rptmp, RupT], writes=[Rpooled])
                if first and b == 0:
                    S.op("dve", lambda e, src=src: e.tensor_tensor(
                        out=smt[:, g, :], in0=src[:, 16:32], in1=invc[:, g, :], op=ALU.mult),
                        reads=[Rptmp, Rc], writes=[Rsm])
                    S.op("dve", lambda e: e.tensor_tensor(
                        out=pooledT[:, g, 0:16], in0=smt[:, g, :], in1=upT[:, g, 16:32], op=ALU.subtract),
                        reads=[Rsm, RupT], writes=[Rpooled])
                S.op("dve", lambda e: e.tensor_copy(upT[:, g, 0:16], upT[:, g, 512:528]),
                     reads=[RupT, Rptmp], writes=[RupT])

            if gi == 0:
                p1a_norm(0)
            p1a_conv(0, 0)
            p1a_conv(0, 1)
            p1a_stats(1)
            p1a_conv(0, 2)
            p1a_tile(1, 0)
            p1a_conv(0, 3)
            p1a_tile(1, 1)
            p1a_pool(0, 0)
            p1a_tile(1, 2)
            p1a_pool(0, 1)
            p1a_tile(1, 3)
            p1a_pool(0, 2)
            p1a_pool(0, 3)
            for c in range(4):
                p1a_conv(1, c)
            for g in range(4):
                p1a_pool(1, g)
            if not last_half:
                S.op("dve", lambda e: e.tensor_copy(utail[:], upT[:, :, 0:16]), reads=[RupT], writes=[Rtail])
            load_w_q()


            def p1b_conv(b):
                bm, Rbm = mbank()
                bq, Rbq = mbank()

                def emit_conv(c):
                    bk, Rb = mbank()
                    c0 = 32 + b * 512 - (CW - 1)
                    mm_group(bk[:], [(diag[:, c * CW + k, :], gluT[:, c, c0 + k:c0 + k + 512]) for k in range(CW)],
                             [Rglu, Rdiag], Rb)
                    S.op("act", lambda e: e.activation(out=hc[:, b, c, :], in_=bk[:], func=AF.Identity,
                                                       bias=vcol(V_CB + c)), reads=[Rb, Rc], writes=[Rhc[b][c]])
                    S.op("act", lambda e: e.activation(out=hsq[:, c % 2, :], in_=bk[:], func=AF.Square,
                                                       bias=vcol(V_CB + c)), reads=[Rb, Rc], writes=[Rhsq[c % 2]])

                def emit_stat(c):
                    S.op("pe", lambda e: e.matmul(bm[:], lhsT=ones32[:], rhs=hc[:, b, c, :], start=(c == 0), stop=(c == 3)),
                         reads=[Rhc[b][c], Rc], writes=[Rbm])
                    S.op("pe", lambda e: e.matmul(bq[:], lhsT=ones32[:], rhs=hsq[:, c % 2, :], start=(c == 0), stop=(c == 3)),
                         reads=[Rhsq[c % 2], Rc], writes=[Rbq])

                emit_conv(0)
                emit_conv(1)
                emit_stat(0)
                emit_conv(2)
                emit_stat(1)
                emit_conv(3)
                emit_stat(2)
                emit_stat(3)
                return bm, Rbm, bq, Rbq

            def p1b_lnstat(b, bm, Rbm, bq, Rbq):
                S.op("act", lambda e: e.activation(out=hsq[:, 0, :], in_=bm[:], func=AF.Square), reads=[Rbm], writes=[Rhsq[0]])
                S.op("act", lambda e: e.copy(out=lst[:, 0, :], in_=bm[:]), reads=[Rbm], writes=[Rlst[0]])
                S.op("dve", lambda e: e.scalar_tensor_tensor(out=lst[:, 1, :], in0=bq[:], scalar=EPS, in1=hsq[:, 0, :],
                                                             op0=ALU.add, op1=ALU.subtract),
                     reads=[Rbq, Rhsq[0]], writes=[Rlst[1]])
                S.op("act", lambda e: e.activation(out=lst[:, 1, :], in_=lst[:, 1, :], func=AF.Sqrt), reads=[Rlst[1]], writes=[Rlst[1]])
                S.op("dve", lambda e: e.reciprocal(out=lst[:, 1, :], in_=lst[:, 1, :]), reads=[Rlst[1]], writes=[Rlst[1]])

            def p1b_lnapply(b):
                def ln_apply(c):
                    hcb = hc[:, b, c, :]
                    S.op("dve", lambda e: e.tensor_tensor(out=hcb, in0=hcb, in1=lst[:, 0, :], op=ALU.subtract),
                         reads=[Rhc[b][c], Rlst[0]], writes=[Rhc[b][c]])
                    S.op("dve", lambda e: e.tensor_tensor(out=hcb, in0=hcb, in1=lst[:, 1, :], op=ALU.mult),
                         reads=[Rlst[1], Rhc[b][c]], writes=[Rhc[b][c]])
                    S.op("act", lambda e: e.activation(out=yT[:, c, :], in_=hcb, func=AF.Silu,
                                                       scale=vcol(V_LNG + c), bias=vcol(V_LNB + c)),
                         reads=[Rhc[b][c], Rc], writes=[RyT])

                for c in range(4):
                    ln_apply(c)

            def p1b_poolproj(b):
                def pool_proj(g):
                    bk, Rb = mbank()
                    mm_group(bk[:], [(poolw[:, g, :], pooledT[:, g, b * 512:(b + 1) * 512])], [Rpooled, Rc], Rb)
                    S.op("act", lambda e: e.activation(out=yT[:, 4 + g, :], in_=bk[:], func=AF.Copy,
                                                       scale=vcol(V_PSC + g)), reads=[Rb, Rc], writes=[RyT])

                for g in range(4):
                    pool_proj(g)

            def p1b_wout(b):
                def wout_tile(i4, hf):
                    i = b * 4 + i4
                    bk, Rb = mbank()
                    mm_group(bk[:], [(yT[:, k, i4 * 128:(i4 + 1) * 128], w_out_v[:, k, hf * 512:(hf + 1) * 512]) for k in range(KD)],
                             [RyT, RWB0], Rb)
                    S.op("dve", lambda e: e.tensor_tensor(
                        out=X[:, i, hf * 512:(hf + 1) * 512], in0=bk[:], in1=X[:, i, hf * 512:(hf + 1) * 512], op=ALU.add),
                        reads=[Rb, RX[i]], writes=[RX[i]])
                    if hf == 1:
                        early_square(i)

                for i4 in range(4):
                    for hf in range(2):
                        wout_tile(i4, hf)

            passes = FFN_PASSES

            def load_pass(p):
                j0, n = passes[p]
                slot = p % 2
                g_, v_, dn_ = ffn_views(slot)
                R = [RWB0] if slot == 0 else [RWB1]
                sl = slots["wb0"] if slot == 0 else slots["wb1"]
                wload("p%d" % p, sl, WB[:, slot * 12288:(slot + 1) * 12288],
                      [(g_[:, :, 0:n * 128], kp(dw_up)[:, :, j0 * 128:(j0 + n) * 128]),
                       (v_[:, :, 0:n * 128], kp(dw_up)[:, :, DFF + j0 * 128:DFF + (j0 + n) * 128]),
                       (dn_[:, 0:n, :], dw_down.rearrange("(j p) n -> p j n", p=128)[:, j0:j0 + n, :])], R)


            def p2_norm(b):
                for i4 in range(4):
                    i = b * 4 + i4
                    norm_tile_to_T(X[:, i, :], RX[i], i, V_GX, hTb2[:, :, i4 * 128:(i4 + 1) * 128], RhTb2)

            def q_chunk(c):
                bk, Rb = mbank()
                mm_group(bk[:], [(w_q_v[:, k, c * 128:(c + 1) * 128], hTb2[:, k, :]) for k in range(KD)], [RhTb2, RWA], Rb)
                if c % 2 == 0:
                    S.op("act", lambda e: e.copy(out=QT[:, c, :], in_=bk[:]), reads=[Rb], writes=[RQT[c // 2]])
                else:
                    S.op("dve", lambda e: e.tensor_copy(QT[:, c, :], bk[:]), reads=[Rb], writes=[RQT[c // 2]])

            def emit_scores(hd):
                pb = hd % 2
                for kc in range(2):
                    bk, Rb = mbank()
                    mm_group(bk[:], [(KT[:, 0, 2 * hd + cc, kc * 128:(kc + 1) * 128], QT[:, 2 * hd + cc, :]) for cc in range(2)],
                             [RKV, RQT[hd]], Rb)
                    S.op("act", lambda e, bk=bk, kc=kc: e.activation(out=PT[:, pb, kc, :], in_=bk[:], func=AF.Exp,
                                                                      scale=1.0 / 16.0), reads=[Rb], writes=[RPT[pb]])

            def emit_pv(b, hd):
                pb = hd % 2
                bs, Rbs = mbank()
                mm_group(bs[:], [(onesb[:], PT[:, pb, kc, :]) for kc in range(2)], [RPT[pb], Rc], Rbs)
                S.op("dve", lambda e: e.reciprocal(out=rr[:, pb, :], in_=bs[:]), reads=[Rbs], writes=[Rrr[pb]])
                for cc in range(2):
                    bo, Rbo = mbank()
                    mm_group(bo[:], [(Vt[:, 0, kc, (2 * hd + cc) * 128:(2 * hd + cc + 1) * 128], PT[:, pb, kc, :]) for kc in range(2)],
                             [RPT[pb], RKV], Rbo)
                    S.op("dve", lambda e, bo=bo, cc=cc: e.tensor_tensor(
                        out=OT[:, b, 2 * hd + cc, :], in0=bo[:], in1=rr[:, pb, :], op=ALU.mult),
                        reads=[Rbo, Rrr[pb]], writes=[ROT[b]])

            def wo_group(b, gidx):
                i4, hf = gidx // 2, gidx % 2
                i = b * 4 + i4
                bk, Rb = mbank()
                mm_group(bk[:], [(OT[:, b, k, i4 * 128:(i4 + 1) * 128], w_o_v[:, k, hf * 512:(hf + 1) * 512]) for k in range(KD)],
                         [ROT[b], RWB1], Rb)
                S.op("dve", lambda e: e.tensor_tensor(
                    out=X[:, i, hf * 512:(hf + 1) * 512], in0=bk[:], in1=X[:, i, hf * 512:(hf + 1) * 512], op=ALU.add),
                    reads=[Rb, RX[i]], writes=[RX[i]])
                if hf == 1:
                    early_square(i)

            st0 = p1b_conv(0)
            p1b_lnstat(0, *st0)
            p1b_lnapply(0)
            st1 = p1b_conv(1)
            p1b_lnstat(1, *st1)
            if not last_half:
                S.op("dve", lambda e: e.tensor_copy(gtail[:], gluT[:, :, GT:GT + 32]), reads=[Rglu], writes=[Rtail])
            p1b_poolproj(0)
            p1b_wout(0)
            p1b_poolproj(1)
            norm_stats([X[:, i, :] for i in range(4)], RX[0:4], 4, squares=False, c0=0)
            p2_norm(0)
            p1b_lnapply(1)
            for c in range(8):
                q_chunk(c)
            p1b_wout(1)
            load_pass(0)
            norm_stats([X[:, i, :] for i in range(4, 8)], RX[4:8], 4, squares=False, c0=4)
            p2_norm(1)
            emit_scores(0)
            emit_scores(1)
            q_chunk(0); q_chunk(1)
            emit_pv(0, 0)
            emit_scores(2)
            q_chunk(2); q_chunk(3)
            emit_pv(0, 1)
            emit_scores(3)
            q_chunk(4); q_chunk(5)
            emit_pv(0, 2)
            q_chunk(6); q_chunk(7)
            emit_pv(0, 3)
            emit_scores(0)
            emit_scores(1)
            wo_group(0, 0); wo_group(0, 1)
            emit_pv(1, 0)
            emit_scores(2)
            wo_group(0, 2); wo_group(0, 3)
            emit_pv(1, 1)
            emit_scores(3)
            wo_group(0, 4); wo_group(0, 5)
            emit_pv(1, 2)
            wo_group(0, 6); wo_group(0, 7)
            emit_pv(1, 3)
            for g_ in range(8):
                wo_group(1, g_)
            load_pass(1)
            if gi + 1 < len(groups):
                load_w_in()

            norm_stats([X[:, i, :] for i in range(NT)], RX, NT, squares=False)
            for i in range(NT):
                norm_tile_to_T(X[:, i, :], RX[i], i, V_GFFN, hTg[:, :, i * 128:(i + 1) * 128], RhTg)
            units = [(p, b) for p in range(len(passes)) for b in range(NBLK)]
            seq_first_blk = first
            ucount = [0]

            def emit_up(p, b, jj, ab):
                j0, n = passes[p]
                slot = p % 2
                g_, v_, dn_ = ffn_views(slot)
                RW = RWB0 if slot == 0 else RWB1
                j = j0 + jj
                pa = (ucount[0]) % 2
                ucount[0] += 1
                gb = h * NBLK + b
                rpar, wpar = gb % 2, (gb + 1) % 2
                banks = []
                for xi, wv in enumerate((g_, v_)):
                    bk, Rb = mbank()
                    mm_group(bk[:], [(wv[:, k, jj * 128:(jj + 1) * 128], hTg[:, k, b * 512:(b + 1) * 512]) for k in range(KD)],
                             [RhTg, RW], Rb)
                    banks.append((bk, Rb))
                for xi in range(2):
                    bk, Rb = banks[xi]
                    ch = j + xi * NFF
                    a = aGV[:, pa, xi, :]
                    Ra = RaGV[pa][xi]
                    U = Ub[:, pa, xi, :]
                    RU = RUb[pa][xi]
                    if not (last_half and b == NBLK - 1):
                        S.op("act", lambda e, bk=bk, ch=ch: e.copy(out=fH[:, wpar, ch, :], in_=bk[:, 510:512]),
                             reads=[Rb], writes=[RfH[wpar]])
                    if first and b == 0:
                        S.op("act", lambda e, U=U: e.activation(out=U[:, 0:2], in_=vecs[:, 0:2], func=AF.Copy, scale=0.0),
                             reads=[Rc], writes=[RU])
                    else:
                        S.op("act", lambda e, U=U, ch=ch: e.copy(out=U[:, 0:2], in_=fH[:, rpar, ch, :]),
                             reads=[RfH[rpar]], writes=[RU])
                    S.op("act", lambda e, bk=bk, U=U: e.copy(out=U[:, 2:514], in_=bk[:]), reads=[Rb], writes=[RU])
                    S.op("act", lambda e, bk=bk, a=a, ch=ch: e.activation(
                        out=a, in_=bk[:], func=AF.Identity, scale=vcol(V_FW + ch * 3 + 2), bias=vcol(V_FB + ch)),
                        reads=[Rb, Rc], writes=[Ra])
                for xi in range(2):
                    ch = j + xi * NFF
                    a = aGV[:, pa, xi, :]
                    Ra = RaGV[pa][xi]
                    U = Ub[:, pa, xi, :]
                    RU = RUb[pa][xi]
                    S.op("dve", lambda e, U=U, a=a, ch=ch: e.scalar_tensor_tensor(
                        out=a, in0=U[:, 1:513], scalar=vcol(V_FW + ch * 3 + 1), in1=a,
                        op0=ALU.mult, op1=ALU.add), reads=[RU, Rc, Ra], writes=[Ra])
                    S.op("dve", lambda e, U=U, a=a, ch=ch: e.scalar_tensor_tensor(
                        out=a, in0=U[:, 0:512], scalar=vcol(V_FW + ch * 3 + 0), in1=a,
                        op0=ALU.mult, op1=ALU.add), reads=[RU, Rc, Ra], writes=[Ra])
                return (pa, ab, jj)

            def emit_gate(pa, ab, jj):
                ag = aGV[:, pa, 0, :]
                S.op("act", lambda e: e.activation(out=ag, in_=ag, func=AF.Silu),
                     reads=[RaGV[pa][0]], writes=[RaGV[pa][0]])
                S.op("dve", lambda e: e.tensor_tensor(out=actT[:, ab, jj, :], in0=ag, in1=aGV[:, pa, 1, :], op=ALU.mult),
                     reads=[RaGV[pa][0], RaGV[pa][1]], writes=[RactT[ab]])

            def emit_down(p, b, ab):
                j0, n = passes[p]
                slot = p % 2
                g_, v_, dn_ = ffn_views(slot)
                RW = RWB0 if slot == 0 else RWB1
                for i4 in range(4):
                    i = b * 4 + i4
                    for hf in range(2):
                        bk, Rb = mbank()
                        mm_group(bk[:], [(actT[:, ab, jj, i4 * 128:(i4 + 1) * 128], dn_[:, jj, hf * 512:(hf + 1) * 512]) for jj in range(n)],
                                 [RactT[ab], RW], Rb)
                        S.op("dve", lambda e, bk=bk, i=i, hf=hf: e.tensor_tensor(
                            out=X[:, i, hf * 512:(hf + 1) * 512], in0=bk[:], in1=X[:, i, hf * 512:(hf + 1) * 512], op=ALU.add),
                            reads=[Rb, RX[i]], writes=[RX[i]])

            def emit_final(b):
                tiles = [b * 4 + i4 for i4 in range(4)]
                for i in tiles:
                    S.op("act", lambda e, i=i: e.activation(out=hb2[:, i % 2, :], in_=X[:, i, :], func=AF.Square,
                                                            accum_out=ssb[:, i:i + 1]),
                         reads=[RX[i]], writes=[Rhb2[i % 2], Rss])
                lo_, hi_ = tiles[0], tiles[-1] + 1
                S.op("dve", lambda e: e.tensor_scalar(out=rstd[:, lo_:hi_], in0=ssb[:, lo_:hi_], scalar1=1.0 / D, scalar2=EPS,
                                                      op0=ALU.mult, op1=ALU.add), reads=[Rss], writes=[Rrstd])
                S.op("act", lambda e: e.activation(out=rstd[:, lo_:hi_], in_=rstd[:, lo_:hi_], func=AF.Sqrt),
                     reads=[Rrstd], writes=[Rrstd])
                S.op("dve", lambda e: e.reciprocal(out=rstd[:, lo_:hi_], in_=rstd[:, lo_:hi_]), reads=[Rrstd], writes=[Rrstd])
                for i in tiles:
                    ob = i % 2
                    S.op("dve", lambda e, i=i: e.scalar_tensor_tensor(
                        out=X[:, i, :], in0=X[:, i, :], scalar=rstd[:, i:i + 1], in1=gfinb[:], op0=ALU.mult, op1=ALU.mult),
                        reads=[RX[i], Rrstd, Rc], writes=[RX[i]])
                    S.dma("sp", oslots[i], dout[s, tok0 + i * 128: tok0 + (i + 1) * 128, :], X[:, i, :],
                          reads=[RX[i]], writes=[Rout[i]])

            prev = None
            pend_load = []
            STAGED = False
            pend_gate = None
            for ui, (p, b) in enumerate(units):
                j0, n = passes[p]
                ab = ui % 2
                g0 = emit_up(p, b, 0, ab)
                if STAGED:
                    if pend_gate is not None:
                        emit_gate(*pend_gate)
                    pend_gate = g0
                else:
                    emit_gate(*g0)
                if prev is not None:
                    pp, pb_, pab = prev
                    emit_down(pp, pb_, pab)
                    if pp == len(passes) - 1:
                        emit_final(pb_)
                        if gi + 1 < len(groups):
                            issue_x(gi + 1, range(pb_ * 4, pb_ * 4 + 4))
                    if pb_ == NBLK - 1 and pp + 2 < len(passes):
                        load_pass(pp + 2)
                for jj in range(1, n):
                    gj = emit_up(p, b, jj, ab)
                    if STAGED:
                        emit_gate(*pend_gate)
                        pend_gate = gj
                    else:
                        emit_gate(*gj)
                    if jj == 1 and pend_load:
                        load_pass(pend_load.pop())
                prev = (p, b, ab)
            if STAGED:
                emit_gate(*pend_gate)
            pp, pb_, pab = prev
            emit_down(pp, pb_, pab)
            if gi + 1 < len(groups):
                p1a_norm(0)
            emit_final(pb_)
            if gi + 1 < len(groups):
                issue_x(gi + 1, range(pb_ * 4, pb_ * 4 + 4))
            if gi + 1 < len(groups) and groups[gi + 1][1] != 0:
                load_w_out_o()

        flush_stores()
        S.final_wait("sp", Rout + [v_[1] for v_ in scr.values()])
        S.emit()
    return nc


_tt_small_cache = {}


def _prep_inputs(inputs):
    f = lambda a: np.ascontiguousarray(np.asarray(a, dtype=np.float32))
    vec = np.zeros((128, V_N), np.float32)

    def fm(v, n):
        return np.asarray(v, np.float32).reshape(n, 128).T

    vec[:, V_GMIX:V_GMIX + 8] = fm(inputs["norm_mix_g"][0], 8)
    vec[:, V_GX:V_GX + 8] = fm(inputs["norm_xattn_g"][0], 8)
    vec[:, V_GMEM:V_GMEM + 8] = fm(inputs["norm_mem_g"][0], 8)
    vec[:, V_GFFN:V_GFFN + 8] = fm(inputs["norm_ffn_g"][0], 8)
    vec[:, V_CB:V_CB + 4] = fm(inputs["conv_dw_b"][0], 4)
    vec[:, V_LNG:V_LNG + 4] = fm(inputs["conv_ln_g"][0], 4)
    vec[:, V_LNB:V_LNB + 4] = fm(inputs["conv_ln_b"][0], 4)
    vec[:, V_PSC:V_PSC + 4] = fm(inputs["pool_scale"][0], 4)
    cw = np.asarray(inputs["conv_dw_w"][0], np.float32)
    vec[:, V_CW:V_CW + 4 * CW] = cw.reshape(CW, 4, 128).transpose(2, 1, 0).reshape(128, 4 * CW)
    fw = np.asarray(inputs["ffn_dw_w"][0], np.float32)
    vec[:, V_FW:V_FW + 132] = fw.reshape(3, 44, 128).transpose(2, 1, 0).reshape(128, 132)
    vec[:, V_FB:V_FB + 44] = fm(inputs["ffn_dw_b"][0], 44)
    gfinb = np.ascontiguousarray(np.broadcast_to(np.asarray(inputs["norm_final_g"], np.float32)[None, :], (128, D)))
    shared = {
        "vecs": vec, "gfinb": gfinb,
        "w_in": f(inputs["w_in"][0]), "pool_w": f(inputs["pool_w"][0]), "w_out": f(inputs["w_out"][0]),
        "w_q": f(inputs["w_q"][0]), "w_kv": f(inputs["w_kv"][0]), "w_o": f(inputs["w_o"][0]),
        "w_up": f(inputs["w_up"][0]), "w_down": f(inputs["w_down"][0]),
    }
    return shared


def kernel(**inputs):
    x = np.asarray(inputs["x"], np.float32)
    mem = np.asarray(inputs["mem"], np.float32)
    shared = _prep_inputs(inputs)
    nc = build_program()
    in_maps = []
    for c in range(NCORES):
        m = dict(shared)
        m["x"] = np.ascontiguousarray(x[c * SEQ_PER_CORE:(c + 1) * SEQ_PER_CORE])
        m["mem"] = np.ascontiguousarray(mem[c * SEQ_PER_CORE:(c + 1) * SEQ_PER_CORE])
        in_maps.append(m)
    res = run_bass_kernel_spmd(nc, in_maps, core_ids=list(range(NCORES)))
    out = np.concatenate([np.asarray(r["out"], np.float32) for r in res.results], axis=0)
    return out
```

```python
import numpy as np
from contextlib import ExitStack
import concourse.bass as bass
import concourse.mybir as mybir
from concourse.bass_utils import run_bass_kernel_spmd

F32 = mybir.dt.float32
BF16 = mybir.dt.bfloat16
AF = mybir.ActivationFunctionType
ALU = mybir.AluOpType

NCORES = 8
SEQ_PER_CORE = 2
D = 1024
KD = 8
SEQ = 2048
GT = 1024
NT = GT // 128
NBLK = GT // 512
NMEM = 256
DFF = 2816
NFF = DFF // 128
CW = 31
EPS = 1e-6
FFN_G = 4
FFN_PASSES = [(0, 4), (4, 4), (8, 4), (12, 4), (16, 3), (19, 3)]

V_GMIX, V_GX, V_GMEM, V_GFFN = 0, 8, 16, 24
V_CB, V_LNG, V_LNB, V_PSC = 32, 36, 40, 44
V_CW = 48
V_FW = V_CW + 4 * CW
V_FB = V_FW + 44 * 3
V_N = V_FB + 44


class Res:
    __slots__ = ("w", "r", "ov", "lo", "hi")

    def __init__(self, lo=None, hi=None):
        self.w = None
        self.r = []
        self.ov = []
        self.lo = lo
        self.hi = hi


class Sched:
    ENG = ("pe", "act", "dve", "pool", "sp")

    def __init__(self, nc, stack):
        self.nc = nc
        self.stack = stack
        self.prog = {e: [] for e in self.ENG}
        self.sem = {e: stack.enter_context(nc.semaphore("s_" + e)) for e in self.ENG}
        self.cnt = {e: 0 for e in self.ENG}
        self.waited = {e: {} for e in self.ENG}
        self.fence_tok = []
        self.abs = {e: [] for e in self.ENG}

    def check(self):
        val = {}
        ptr = {e: 0 for e in self.ENG}
        progress = True
        while progress:
            progress = False
            for e in self.ENG:
                while ptr[e] < len(self.abs[e]):
                    waits, sem, n = self.abs[e][ptr[e]]
                    if all(val.get(id(s_), 0) >= v for s_, v in waits):
                        if sem is not None:
                            val[id(sem)] = val.get(id(sem), 0) + n
                        ptr[e] += 1
                        progress = True
                    else:
                        break
        stuck = {e: (ptr[e], len(self.abs[e])) for e in self.ENG if ptr[e] < len(self.abs[e])}
        return stuck

    def _deps(self, eng, reads, writes, extra=(), skip_key=None):
        deps = {}

        def add(tok):
            if tok is None:
                return
            k, v = tok
            if deps.get(k, 0) < v:
                deps[k] = v

        for r in reads:
            add(r.w)
            for o in r.ov:
                add(o.w)
        for w in writes:
            add(w.w)
            for t in w.r:
                add(t)
            for o in w.ov:
                add(o.w)
                for t in o.r:
                    add(t)
        for t in extra:
            add(t)
        out = []
        for k, v in deps.items():
            if k == "pe" and eng == "pe":
                continue
            if skip_key is not None and k is skip_key:
                continue
            if self.waited[eng].get(k, 0) >= v:
                continue
            self.waited[eng][k] = v
            out.append((self.sem[k] if isinstance(k, str) else k, v))
        return out

    def _mark(self, tok, reads, writes):
        for r in reads:
            r.r.append(tok)
            if len(r.r) > 64:
                best = {}
                for k, v in r.r:
                    if best.get(k, 0) < v:
                        best[k] = v
                r.r = list(best.items())
        for w in writes:
            w.w = tok
            w.r = []

    def op(self, eng, fn, reads=(), writes=(), inc=True):
        waits = self._deps(eng, reads, writes)
        if inc:
            self.cnt[eng] += 1
        tok = (eng, self.cnt[eng] if inc else self.cnt[eng] + 1)
        sem = self.sem[eng]

        def thunk(e, waits=waits, fn=fn, inc=inc, sem=sem):
            for s, v in waits:
                e.wait_ge(s, v)
            ins = fn(e)
            if inc:
                ins.then_inc(sem, 1)

        self.prog[eng].append(thunk)
        self.abs[eng].append((waits, sem if inc else None, 1))
        self._mark(tok, reads, writes)
        return tok

    def newslot(self, name):
        return {"sem": self.stack.enter_context(self.nc.semaphore(name)), "cnt": 0}

    def dma(self, eng, slot, out, in_, reads=(), writes=(), extra=(), **kw):
        waits = self._deps(eng, reads, writes, extra, skip_key=slot["sem"])
        slot["cnt"] += 16
        tok = (slot["sem"], slot["cnt"])

        def thunk(e, waits=waits, out=out, in_=in_, sem=slot["sem"], kw=kw):
            for s, v in waits:
                e.wait_ge(s, v)
            e.dma_start(out=out, in_=in_, **kw).then_inc(sem, 16)

        self.prog[eng].append(thunk)
        self.abs[eng].append((waits, slot["sem"], 16))
        self._mark(tok, reads, writes)
        return tok

    def fence(self):
        comp = ("pe", "act", "dve")
        snap = [(e, self.cnt[e]) for e in comp if self.cnt[e] > 0]
        self.fence_tok = snap
        for e in comp:
            waits = []
            for p, v in snap:
                if p == e or self.waited[e].get(p, 0) >= v:
                    continue
                self.waited[e][p] = v
                waits.append((self.sem[p], v))
            if waits:
                self.prog[e].append(lambda eng, waits=waits: [eng.wait_ge(s, v) for s, v in waits])
                self.abs[e].append((waits, None, 0))

    def final_wait(self, eng, resources):
        waits = self._deps(eng, resources, ())
        self.prog[eng].append(lambda e, waits=waits: [e.wait_ge(s, v) for s, v in waits])

    def emit(self):
        with self.nc.Block() as block:
            @block.tensor
            def _(e):
                for t in self.prog["pe"]:
                    t(e)

            @block.scalar
            def _(e):
                for t in self.prog["act"]:
                    t(e)

            @block.vector
            def _(e):
                for t in self.prog["dve"]:
                    t(e)

            @block.gpsimd
            def _(e):
                for t in self.prog["pool"]:
                    t(e)

            @block.sync
            def _(e):
                for t in self.prog["sp"]:
                    t(e)


def build_program(nseq=SEQ_PER_CORE, seqlen=SEQ):
    nc = bass.Bass("TRN2", target_bir_lowering=False)
    nhalf = seqlen // GT
    dx = nc.dram_tensor("x", [nseq, seqlen, D], F32, kind="ExternalInput").ap()
    dmem = nc.dram_tensor("mem", [nseq, NMEM, D], F32, kind="ExternalInput").ap()
    dvec = nc.dram_tensor("vecs", [128, V_N], F32, kind="ExternalInput").ap()
    dgfin = nc.dram_tensor("gfinb", [128, D], F32, kind="ExternalInput").ap()
    dw_in = nc.dram_tensor("w_in", [D, 1536], F32, kind="ExternalInput").ap()
    dpoolw = nc.dram_tensor("pool_w", [4, 128, 128], F32, kind="ExternalInput").ap()
    dw_out = nc.dram_tensor("w_out", [D, D], F32, kind="ExternalInput").ap()
    dw_q = nc.dram_tensor("w_q", [D, D], F32, kind="ExternalInput").ap()
    dw_kv = nc.dram_tensor("w_kv", [D, 2 * D], F32, kind="ExternalInput").ap()
    dw_o = nc.dram_tensor("w_o", [D, D], F32, kind="ExternalInput").ap()
    dw_up = nc.dram_tensor("w_up", [D, 2 * DFF], F32, kind="ExternalInput").ap()
    dw_down = nc.dram_tensor("w_down", [DFF, D], F32, kind="ExternalInput").ap()
    dout = nc.dram_tensor("out", [nseq, seqlen, D], F32, kind="ExternalOutput").ap()

    kp = lambda ap: ap.rearrange("(k p) n -> p k n", p=128)

    with ExitStack() as st:
        S = Sched(nc, st)
        base = (nc._sbuf_addr_for_side("left") + 63) // 64 * 64
        top = nc._sbuf_addr_for_side("right")
        cur = [base]
        nalloc = [0]
        binfo = {}
        regs = []

        def alloc(shape, dt, at=None):
            nb = 2 if dt == BF16 else 4
            sz = nb
            for s_ in shape[1:]:
                sz *= s_
            sz = (sz + 63) // 64 * 64
            if at is None:
                off = cur[0]
                cur[0] += sz
            else:
                off = at[0]
                at[0] += sz
            assert off + sz <= top, ("SBUF overflow", off + sz - top)
            nalloc[0] += 1
            t_ = nc.alloc_sbuf_tensor_at("t%d" % nalloc[0], list(shape), dt, offset=off)
            binfo[id(t_)] = (off, sz)
            return t_

        def REG(buf, idx=None, nsub=1, n=1):
            off, sz = binfo[id(buf)]
            if idx is None:
                lo, hi = off, off + sz
            else:
                sub = sz // nsub
                lo, hi = off + idx * sub, off + (idx + n) * sub
            r_ = Res(lo, hi)
            regs.append(r_)
            return r_

        def link_regs():
            for i_, a_ in enumerate(regs):
                for b_ in regs[i_ + 1:]:
                    if a_.lo < b_.hi and b_.lo < a_.hi:
                        a_.ov.append(b_)
                        b_.ov.append(a_)

        ident = alloc([128, 128], BF16)
        hident = alloc([128, 128], BF16)
        ones32 = alloc([128, 128], F32)
        onesb = alloc([128, 128], BF16)
        vecs = alloc([128, V_N], F32)
        gfinb = alloc([128, D], F32)
        invc = alloc([128, 4, 16], F32)
        fH = alloc([128, 2, 44, 2], F32)
        poolw = alloc([128, 4, 128], BF16)
        ssb = alloc([128, 16], F32)
        rstd = alloc([128, 16], F32)
        diag = alloc([128, 4 * CW, 128], BF16)
        KT = alloc([128, 1, 8, NMEM], BF16)
        Vt = alloc([128, 1, 2, D], BF16)
        X = alloc([128, NT, D], F32)
        WA = alloc([128, 12288], BF16)
        WB = alloc([128, 24576], BF16)
        hb2 = alloc([128, 2, D], BF16)
        smt = alloc([128, 4, 16], F32)
        gtail = alloc([128, 4, 32], BF16)
        utail = alloc([128, 4, 16], F32)
        ov_base = cur[0]
        identf = alloc([128, 128], F32, [ov_base + 16384])

        w_in_v = WA[:, 0:8 * 1536].rearrange("p (k n) -> p k n", k=8)
        w_q_v = WA[:, 0:8 * 1024].rearrange("p (k n) -> p k n", k=8)
        w_kv_v = WB[:, 0:16384].rearrange("p (k n) -> p k n", k=8)
        w_out_v = WB[:, 0:8192].rearrange("p (k n) -> p k n", k=8)
        w_o_v = WB[:, 12288:20480].rearrange("p (k n) -> p k n", k=8)

        def ffn_views(slot):
            b0 = slot * 12288
            g = WB[:, b0:b0 + 4096].rearrange("p (k n) -> p k n", k=8)
            v = WB[:, b0 + 4096:b0 + 8192].rearrange("p (k n) -> p k n", k=8)
            dn = WB[:, b0 + 8192:b0 + 12288].rearrange("p (j n) -> p j n", j=4)
            return g, v, dn

        o = [ov_base]
        memx = alloc([128, 2, D], F32, o)
        memT = alloc([128, 8, NMEM], BF16, o)
        o = [ov_base]
        gluT = alloc([128, 4, 32 + GT], BF16, o)
        pooledT = alloc([128, 4, GT], BF16, o)
        o1 = [o[0]]
        hTb = alloc([128, 2, 8, 512], BF16, o1)
        upT = alloc([128, 4, 528], F32, o1)
        ptmp = alloc([128, 2, 528], F32, o1)
        th = alloc([128, 2, 512], F32, o1)
        o1 = [o[0]]
        yT = alloc([128, 8, 512], BF16, o1)
        hc = alloc([128, 2, 4, 512], F32, o1)
        hsq = alloc([128, 2, 512], F32, o1)
        lst = alloc([128, 2, 512], F32, o1)
        o = [ov_base]
        hTb2 = alloc([128, 8, 512], BF16, o)
        QT = alloc([128, 8, 512], BF16, o)
        PT = alloc([128, 2, 2, 512], BF16, o)
        OT = alloc([128, 2, 8, 512], BF16, o)
        rr = alloc([128, 2, 512], F32, o)
        o = [ov_base]
        hTg = alloc([128, 8, GT], BF16, o)
        actT = alloc([128, 2, FFN_G, 512], BF16, o)
        aGV = alloc([128, 2, 2, 512], F32, o)
        Ub = alloc([128, 2, 2, 514], F32, o)

        ptb = [st.enter_context(nc.psum_tensor("ptb%d" % i, [128, 1024], BF16)) for i in range(2)]
        pmb = [st.enter_context(nc.psum_tensor("pmb%d" % i, [128, 512], F32)) for i in range(6)]
        Rptb = [Res() for _ in range(2)]
        Rpmb = [Res() for _ in range(6)]
        rot = {"t": 0, "m": 0}

        def tbank():
            i = rot["t"] % 2
            rot["t"] += 1
            return ptb[i], Rptb[i]

        def mbank():
            i = rot["m"] % 6
            rot["m"] += 1
            return pmb[i], Rpmb[i]

        Rc = Res()
        RX = [Res() for _ in range(NT)]
        RWA, RWB0, RWB1 = Res(), Res(), Res()
        Rdiag, RKV = Res(), Res()
        Rss, Rrstd, Rhb2, Rsm, Rtail = Res(), Res(), [Res(), Res()], Res(), Res()
        RfH = [Res(), Res()]
        Rout = [Res() for _ in range(NT)]
        Rm1, RmemT = REG(memx), REG(memT)
        Rglu, Rpooled, RupT, Rptmp = REG(gluT), REG(pooledT), REG(upT), REG(ptmp)
        RhTb = [REG(hTb, 0, 2), REG(hTb, 1, 2)]
        Rth = [REG(th, i, 2) for i in range(2)]
        RyT = REG(yT)
        Rhc = [[REG(hc, b_ * 4 + c, 8) for c in range(4)] for b_ in range(2)]
        Rhsq = [REG(hsq, i, 2) for i in range(2)]
        Rlst = [REG(lst, 0, 2), REG(lst, 1, 2)]
        RhTb2 = REG(hTb2)
        RQT = [REG(QT, 2 * hd, 8, 2) for hd in range(4)]
        RPT = [REG(PT, i, 2) for i in range(2)]
        ROT = [REG(OT, i, 2) for i in range(2)]
        Rrr = [REG(rr, i, 2) for i in range(2)]
        RhTg = REG(hTg)
        RactT = [REG(actT, i, 2) for i in range(2)]
        RaGV = [[REG(aGV, pa * 2 + xi, 4) for xi in range(2)] for pa in range(2)]
        RUb = [[REG(Ub, pa * 2 + xi, 4) for xi in range(2)] for pa in range(2)]
        Ridf = REG(identf)
        link_regs()

        slots = {n: S.newslot("d_" + n) for n in
                 ("c", "x0", "x1", "x2", "x3", "x4", "x5", "x6", "x7", "wa", "wb0", "wb1", "wa_h", "wb0_h", "wb1_h", "o0", "o1", "o2", "o3", "o4", "o5", "o6", "o7", "mem", "pw")}
        xslots = [slots["x%d" % i] for i in range(NT)]
        oslots = [slots["o%d" % i] for i in range(NT)]

        def vcol(c0, n=1):
            return vecs[:, c0:c0 + n]

        def mm_group(bank_ap, pairs, reads, Rbank, extra_reads=()):
            n = len(pairs)
            for i, (l, r) in enumerate(pairs):
                S.op("pe", lambda e, l=l, r=r, i=i: e.matmul(bank_ap, lhsT=l, rhs=r, start=(i == 0), stop=(i == n - 1)),
                     reads=list(reads) + list(extra_reads), writes=[Rbank], inc=(i == n - 1))

        def early_square(i):
            S.op("act", lambda e: e.activation(out=hb2[:, i % 2, :], in_=X[:, i, :], func=AF.Square,
                                               accum_out=ssb[:, i:i + 1]),
                 reads=[RX[i]], writes=[Rhb2[i % 2], Rss])

        def norm_stats(xtiles, Rx, ntile, squares=True, c0=0):
            for i in range(ntile if squares else 0):
                S.op("act", lambda e, i=i: e.activation(out=hb2[:, i % 2, :], in_=xtiles[i], func=AF.Square,
                                                        accum_out=ssb[:, c0 + i:c0 + i + 1]),
                     reads=[Rx[i]], writes=[Rhb2[i % 2], Rss])
            S.op("dve", lambda e: e.tensor_scalar(out=rstd[:, c0:c0 + ntile], in0=ssb[:, c0:c0 + ntile], scalar1=1.0 / D,
                                                  scalar2=EPS, op0=ALU.mult, op1=ALU.add), reads=[Rss], writes=[Rrstd])
            S.op("act", lambda e: e.activation(out=rstd[:, c0:c0 + ntile], in_=rstd[:, c0:c0 + ntile], func=AF.Ln),
                 reads=[Rrstd], writes=[Rrstd])
            S.op("act", lambda e: e.activation(out=rstd[:, c0:c0 + ntile], in_=rstd[:, c0:c0 + ntile], func=AF.Exp, scale=-0.5),
                 reads=[Rrstd], writes=[Rrstd])

        def norm_tile_to_T(xt, Rxt, i, gcol, dst, Rdst):
            hbi = hb2[:, i % 2, :]
            Rh = Rhb2[i % 2]
            S.op("act", lambda e: e.activation(out=hbi, in_=xt, func=AF.Copy, scale=rstd[:, i:i + 1]),
                 reads=[Rxt, Rrstd], writes=[Rh])
            tb, Rtb = tbank()
            for k in range(KD):
                S.op("pe", lambda e, k=k: e.transpose(out=tb[:, k * 128:(k + 1) * 128], in_=hbi[:, k * 128:(k + 1) * 128],
                                                      identity=ident[:]),
                     reads=[Rh, Rc], writes=[Rtb], inc=(k == KD - 1))
            S.op("dve", lambda e: e.tensor_tensor(out=dst, in0=tb[:].rearrange("p (k t) -> p k t", k=KD),
                                                  in1=vecs[:, gcol:gcol + KD].unsqueeze(2).to_broadcast([128, KD, 128]),
                                                  op=ALU.mult), reads=[Rtb, Rc], writes=[Rdst])

        pool_q = []

        scr = {}
        pend_st = []

        def flush_stores():
            while pend_st:
                key_, flat_, R_ = pend_st.pop(0)
                sc_, Rsc_, ssl_ = scr[key_]
                S.dma("sp", ssl_, sc_, flat_, reads=R_, writes=[Rsc_])

        def wload(key, slot, flat, parts, R):
            flush_stores()
            if key not in scr:
                scr[key] = (nc.dram_tensor("sc_" + key, [128, flat.shape[1]], BF16).ap(), Res(), S.newslot("st_" + key))
                for dst_, src_ in parts:
                    S.dma("pool", slot, dst_, src_, writes=R)
                pend_st.append((key, flat, R))
            else:
                sc_, Rsc_, ssl_ = scr[key]
                hslot = {id(slots["wa"]): slots["wa_h"], id(slots["wb0"]): slots["wb0_h"], id(slots["wb1"]): slots["wb1_h"]}[id(slot)]
                S.dma("sp", hslot, flat, sc_, reads=[Rsc_], writes=R)

        def load_w_in():
            wload("w_in", slots["wa"], WA[:, 0:12288], [(w_in_v, kp(dw_in))], [RWA])

        def load_w_q():
            wload("w_q", slots["wa"], WA[:, 0:8192], [(w_q_v, kp(dw_q))], [RWA])

        def load_w_kv():
            wload("w_kv", slots["wb0"], WB[:, 0:16384], [(w_kv_v, kp(dw_kv))], [RWB0, RWB1])

        def load_w_out_o():
            wload("w_out", slots["wb0"], WB[:, 0:8192], [(w_out_v, kp(dw_out))], [RWB0])
            wload("w_o", slots["wb1"], WB[:, 12288:20480], [(w_o_v, kp(dw_o))], [RWB1])

        diag_pending = [True]

        def build_diag():
            for c in range(4):
                for k in range(CW):
                    S.op("pool", lambda e, c=c, k=k: e.tensor_tensor(
                        out=diag[:, c * CW + k, :], in0=hident[:],
                        in1=vcol(V_CW + c * CW + k).to_broadcast([128, 128]), op=ALU.mult),
                        reads=[Rc], writes=[Rdiag])

        groups = [(s, h) for s in range(nseq) for h in range(nhalf)]

        def issue_x(gidx, tiles):
            s_, h_ = groups[gidx]
            for i in tiles:
                S.dma("sp", xslots[i], X[:, i, :], dx[s_, h_ * GT + i * 128: h_ * GT + (i + 1) * 128, :], writes=[RX[i]])
        S.dma("sp", slots["c"], vecs[:], dvec, writes=[Rc])
        S.dma("sp", slots["c"], gfinb[:], dgfin, writes=[Rc])
        def mem_loads(sq):
            for j in range(2):
                S.dma("sp", slots["mem"], memx[:, j, :], dmem[sq, j * 128:(j + 1) * 128, :], writes=[Rm1])

        issue_x(0, range(NT))
        mem_loads(0)
        load_w_in()
        load_w_kv()
        S.dma("pool", slots["pw"], poolw[:], dpoolw.rearrange("g c d -> c g d"), writes=[Rc])
        S.op("pool", lambda e: e.memset(identf[:], 0.0), writes=[Rc, Ridf])
        S.op("pool", lambda e: e.affine_select(out=identf[:], in_=identf[:], compare_op=ALU.not_equal, fill=1.0,
                                               base=0, pattern=[[-1, 128]], channel_multiplier=1),
             reads=[Rc, Ridf], writes=[Rc, Ridf])
        S.op("pool", lambda e: e.tensor_copy(ident[:], identf[:]), reads=[Rc, Ridf], writes=[Rc])
        S.op("dve", lambda e: e.tensor_scalar(out=hident[:], in0=ident[:], scalar1=0.5, scalar2=None, op0=ALU.mult),
             reads=[Rc], writes=[Rc])
        S.op("pool", lambda e: e.memset(ones32[:], 1.0 / 512.0), writes=[Rc])
        S.op("pool", lambda e: e.memset(onesb[:], 1.0), writes=[Rc])
        for g in range(4):
            w = 2 << g
            S.op("pool", lambda e, g=g, w=w: e.memset(invc[:, g, :], 1.0 / w), writes=[Rc])
            for t in range(w - 1):
                S.op("pool", lambda e, g=g, t=t: e.memset(invc[:, g, t:t + 1], 1.0 / (t + 1)), writes=[Rc])

        def p1a_stats(b):
            tl = list(range(b * 4, b * 4 + 4))
            norm_stats([X[:, i, :] for i in tl], [RX[i] for i in tl], 4, squares=True, c0=b * 4)

        def p1a_tile(b, i4):
            i = b * 4 + i4
            norm_tile_to_T(X[:, i, :], RX[i], i, V_GMIX, hTb[:, b, :, i4 * 128:(i4 + 1) * 128], RhTb[b])

        def p1a_norm(b):
            p1a_stats(b)
            for i4 in range(4):
                p1a_tile(b, i4)

        for gi, (s, h) in enumerate(groups):
            tok0 = h * GT
            first = (h == 0)
            last_half = (h == nhalf - 1)
            if first:
                if gi > 0:
                    mem_loads(s)
                    load_w_kv()
                if diag_pending[0]:
                    diag_pending[0] = False
                    build_diag()
                Rmem = [Rm1, Rm1]
                for sq in (s,):
                    norm_stats([memx[:, j, :] for j in range(2)], Rmem, 2, c0=8)
                    for j in range(2):
                        norm_tile_to_T(memx[:, j, :], Rmem[j], 8 + j, V_GMEM, memT[:, :, j * 128:(j + 1) * 128], RmemT)
                    for c in range(8):
                        bk, Rb = mbank()
                        mm_group(bk[:, 0:NMEM], [(w_kv_v[:, k, c * 128:(c + 1) * 128], memT[:, k, :]) for k in range(KD)],
                                 [RmemT, RWB0, RWB1], Rb)
                        if c % 2 == 0:
                            S.op("act", lambda e, c=c, bk=bk, sq=sq: e.copy(out=KT[:, 0, c, :], in_=bk[:, 0:NMEM]),
                                 reads=[Rb], writes=[RKV])
                        else:
                            S.op("dve", lambda e, c=c, bk=bk, sq=sq: e.tensor_copy(KT[:, 0, c, :], bk[:, 0:NMEM]),
                                 reads=[Rb], writes=[RKV])
                    for kc in range(2):
                        for hf in range(2):
                            bk, Rb = mbank()
                            mm_group(bk[:], [(memT[:, k, kc * 128:(kc + 1) * 128], w_kv_v[:, k, D + hf * 512:D + (hf + 1) * 512])
                                             for k in range(KD)], [RmemT, RWB0, RWB1], Rb)
                            if hf == 0:
                                S.op("act", lambda e, kc=kc, bk=bk, sq=sq: e.copy(out=Vt[:, 0, kc, 0:512], in_=bk[:]),
                                     reads=[Rb], writes=[RKV])
                            else:
                                S.op("dve", lambda e, kc=kc, bk=bk, sq=sq: e.tensor_copy(Vt[:, 0, kc, 512:1024], bk[:]),
                                     reads=[Rb], writes=[RKV])
                load_w_out_o()

            if first:
                S.op("dve", lambda e: e.memset(gluT[:, :, 0:32], 0.0), writes=[Rglu])
                S.op("dve", lambda e: e.memset(upT[:, :, 0:16], 0.0), writes=[RupT])
            elif True:
                S.op("dve", lambda e: e.tensor_copy(gluT[:, :, 0:32], gtail[:]), reads=[Rtail], writes=[Rglu])
                S.op("dve", lambda e: e.tensor_copy(upT[:, :, 0:16], utail[:]), reads=[Rtail], writes=[RupT])
            def p1a_conv(b, c):
                bg, Rbg = mbank()
                mm_group(bg[:], [(w_in_v[:, k, 512 + c * 128:512 + (c + 1) * 128], hTb[:, b, k, :]) for k in range(KD)],
                         [RhTb[b], RWA], Rbg)
                thb = th[:, c % 2, :]
                Rt = Rth[c % 2]
                S.op("act", lambda e: e.activation(out=thb, in_=bg[:], func=AF.Tanh, scale=0.5), reads=[Rbg], writes=[Rt])
                bv, Rbv = mbank()
                mm_group(bv[:], [(w_in_v[:, k, c * 128:(c + 1) * 128], hTb[:, b, k, :]) for k in range(KD)],
                         [RhTb[b], RWA], Rbv)
                S.op("dve", lambda e: e.scalar_tensor_tensor(
                    out=gluT[:, c, 32 + b * 512:32 + (b + 1) * 512], in0=thb, scalar=1.0, in1=bv[:],
                    op0=ALU.add, op1=ALU.mult), reads=[Rbv, Rt], writes=[Rglu])

            def p1a_pool(b, g):
                w = 2 << g
                bu, Rbu = mbank()
                mm_group(bu[:], [(w_in_v[:, k, 1024 + g * 128:1024 + (g + 1) * 128], hTb[:, b, k, :]) for k in range(KD)],
                         [RhTb[b], RWA], Rbu)
                S.op("act", lambda e: e.copy(out=upT[:, g, 16:528], in_=bu[:]), reads=[Rbu], writes=[RupT])
                src = upT[:, g, :]
                m = 2
                lvl = 0
                while m <= w:
                    dstb = ptmp[:, lvl % 2, :]
                    lo = m - 1
                    hs = m // 2
                    S.op("dve", lambda e, dstb=dstb, src=src, lo=lo, hs=hs: e.tensor_tensor(
                        out=dstb[:, lo:528], in0=src[:, lo:528], in1=src[:, lo - hs:528 - hs], op=ALU.add),
                        reads=[RupT, Rptmp], writes=[Rptmp])
                    src = dstb
                    m *= 2
                    lvl += 1
                S.op("dve", lambda e, src=src: e.scalar_tensor_tensor(
                    out=pooledT[:, g, b * 512:(b + 1) * 512], in0=src[:, 16:528], scalar=1.0 / w, in1=upT[:, g, 16:528],
                    op0=ALU.mult, op1=ALU.subtract), reads=[Rptmp, RupT], writes=[Rpooled])
                if first and b == 0:
                    S.op("dve", lambda e, src=src: e.tensor_tensor(
                        out=smt[:, g, :], in0=src[:, 16:32], in1=invc[:, g, :], op=ALU.mult),
                        reads=[Rptmp, Rc], writes=[Rsm])
                    S.op("dve", lambda e: e.tensor_tensor(
                        out=pooledT[:, g, 0:16], in0=smt[:, g, :], in1=upT[:, g, 16:32], op=ALU.subtract),
                        reads=[Rsm, RupT], writes=[Rpooled])
                S.op("dve", lambda e: e.tensor_copy(upT[:, g, 0:16], upT[:, g, 512:528]),
                     reads=[RupT, Rptmp], writes=[RupT])

            if gi == 0:
                p1a_norm(0)
            p1a_conv(0, 0)
            p1a_conv(0, 1)
            p1a_stats(1)
            p1a_conv(0, 2)
            p1a_tile(1, 0)
            p1a_conv(0, 3)
            p1a_tile(1, 1)
            p1a_pool(0, 0)
            p1a_tile(1, 2)
            p1a_pool(0, 1)
            p1a_tile(1, 3)
            p1a_pool(0, 2)
            p1a_pool(0, 3)
            for c in range(4):
                p1a_conv(1, c)
            for g in range(4):
                p1a_pool(1, g)
            if not last_half:
                S.op("dve", lambda e: e.tensor_copy(utail[:], upT[:, :, 0:16]), reads=[RupT], writes=[Rtail])
            load_w_q()


            def p1b_conv(b):
                bm, Rbm = mbank()
                bq, Rbq = mbank()

                def emit_conv(c):
                    bk, Rb = mbank()
                    c0 = 32 + b * 512 - (CW - 1)
                    mm_group(bk[:], [(diag[:, c * CW + k, :], gluT[:, c, c0 + k:c0 + k + 512]) for k in range(CW)],
                             [Rglu, Rdiag], Rb)
                    S.op("act", lambda e: e.activation(out=hc[:, b, c, :], in_=bk[:], func=AF.Identity,
                                                       bias=vcol(V_CB + c)), reads=[Rb, Rc], writes=[Rhc[b][c]])
                    S.op("act", lambda e: e.activation(out=hsq[:, c % 2, :], in_=bk[:], func=AF.Square,
                                                       bias=vcol(V_CB + c)), reads=[Rb, Rc], writes=[Rhsq[c % 2]])

                def emit_stat(c):
                    S.op("pe", lambda e: e.matmul(bm[:], lhsT=ones32[:], rhs=hc[:, b, c, :], start=(c == 0), stop=(c == 3)),
                         reads=[Rhc[b][c], Rc], writes=[Rbm])
                    S.op("pe", lambda e: e.matmul(bq[:], lhsT=ones32[:], rhs=hsq[:, c % 2, :], start=(c == 0), stop=(c == 3)),
                         reads=[Rhsq[c % 2], Rc], writes=[Rbq])

                emit_conv(0)
                emit_conv(1)
                emit_stat(0)
                emit_conv(2)
                emit_stat(1)
                emit_conv(3)
                emit_stat(2)
                emit_stat(3)
                return bm, Rbm, bq, Rbq

            def p1b_lnstat(b, bm, Rbm, bq, Rbq):
                S.op("act", lambda e: e.activation(out=hsq[:, 0, :], in_=bm[:], func=AF.Square), reads=[Rbm], writes=[Rhsq[0]])
                S.op("act", lambda e: e.copy(out=lst[:, 0, :], in_=bm[:]), reads=[Rbm], writes=[Rlst[0]])
                S.op("dve", lambda e: e.scalar_tensor_tensor(out=lst[:, 1, :], in0=bq[:], scalar=EPS, in1=hsq[:, 0, :],
                                                             op0=ALU.add, op1=ALU.subtract),
                     reads=[Rbq, Rhsq[0]], writes=[Rlst[1]])
                S.op("act", lambda e: e.activation(out=lst[:, 1, :], in_=lst[:, 1, :], func=AF.Ln), reads=[Rlst[1]], writes=[Rlst[1]])
                S.op("act", lambda e: e.activation(out=lst[:, 1, :], in_=lst[:, 1, :], func=AF.Exp, scale=-0.5),
                     reads=[Rlst[1]], writes=[Rlst[1]])

            def p1b_lnapply(b):
                def ln_apply(c):
                    hcb = hc[:, b, c, :]
                    S.op("dve", lambda e: e.tensor_tensor(out=hcb, in0=hcb, in1=lst[:, 0, :], op=ALU.subtract),
                         reads=[Rhc[b][c], Rlst[0]], writes=[Rhc[b][c]])
                    S.op("dve", lambda e: e.tensor_tensor(out=hcb, in0=hcb, in1=lst[:, 1, :], op=ALU.mult),
                         reads=[Rlst[1], Rhc[b][c]], writes=[Rhc[b][c]])
                    S.op("act", lambda e: e.activation(out=yT[:, c, :], in_=hcb, func=AF.Silu,
                                                       scale=vcol(V_LNG + c), bias=vcol(V_LNB + c)),
                         reads=[Rhc[b][c], Rc], writes=[RyT])

                for c in range(4):
                    ln_apply(c)

            def p1b_poolproj(b):
                def pool_proj(g):
                    bk, Rb = mbank()
                    mm_group(bk[:], [(poolw[:, g, :], pooledT[:, g, b * 512:(b + 1) * 512])], [Rpooled, Rc], Rb)
                    S.op("act", lambda e: e.activation(out=yT[:, 4 + g, :], in_=bk[:], func=AF.Copy,
                                                       scale=vcol(V_PSC + g)), reads=[Rb, Rc], writes=[RyT])

                for g in range(4):
                    pool_proj(g)

            def p1b_wout(b):
                def wout_tile(i4, hf):
                    i = b * 4 + i4
                    bk, Rb = mbank()
                    mm_group(bk[:], [(yT[:, k, i4 * 128:(i4 + 1) * 128], w_out_v[:, k, hf * 512:(hf + 1) * 512]) for k in range(KD)],
                             [RyT, RWB0], Rb)
                    S.op("dve", lambda e: e.tensor_tensor(
                        out=X[:, i, hf * 512:(hf + 1) * 512], in0=bk[:], in1=X[:, i, hf * 512:(hf + 1) * 512], op=ALU.add),
                        reads=[Rb, RX[i]], writes=[RX[i]])
                    if hf == 1:
                        early_square(i)

                for i4 in range(4):
                    for hf in range(2):
                        wout_tile(i4, hf)

            passes = FFN_PASSES

            def load_pass(p):
                j0, n = passes[p]
                slot = p % 2
                g_, v_, dn_ = ffn_views(slot)
                R = [RWB0] if slot == 0 else [RWB1]
                sl = slots["wb0"] if slot == 0 else slots["wb1"]
                wload("p%d" % p, sl, WB[:, slot * 12288:(slot + 1) * 12288],
                      [(g_[:, :, 0:n * 128], kp(dw_up)[:, :, j0 * 128:(j0 + n) * 128]),
                       (v_[:, :, 0:n * 128], kp(dw_up)[:, :, DFF + j0 * 128:DFF + (j0 + n) * 128]),
                       (dn_[:, 0:n, :], dw_down.rearrange("(j p) n -> p j n", p=128)[:, j0:j0 + n, :])], R)


            def p2_norm(b):
                for i4 in range(4):
                    i = b * 4 + i4
                    norm_tile_to_T(X[:, i, :], RX[i], i, V_GX, hTb2[:, :, i4 * 128:(i4 + 1) * 128], RhTb2)

            def q_chunk(c):
                bk, Rb = mbank()
                mm_group(bk[:], [(w_q_v[:, k, c * 128:(c + 1) * 128], hTb2[:, k, :]) for k in range(KD)], [RhTb2, RWA], Rb)
                if c % 2 == 0:
                    S.op("act", lambda e: e.copy(out=QT[:, c, :], in_=bk[:]), reads=[Rb], writes=[RQT[c // 2]])
                else:
                    S.op("dve", lambda e: e.tensor_copy(QT[:, c, :], bk[:]), reads=[Rb], writes=[RQT[c // 2]])

            def emit_scores(hd):
                pb = hd % 2
                for kc in range(2):
                    bk, Rb = mbank()
                    mm_group(bk[:], [(KT[:, 0, 2 * hd + cc, kc * 128:(kc + 1) * 128], QT[:, 2 * hd + cc, :]) for cc in range(2)],
                             [RKV, RQT[hd]], Rb)
                    S.op("act", lambda e, bk=bk, kc=kc: e.activation(out=PT[:, pb, kc, :], in_=bk[:], func=AF.Exp,
                                                                      scale=1.0 / 16.0), reads=[Rb], writes=[RPT[pb]])

            def emit_pv(b, hd):
                pb = hd % 2
                bs, Rbs = mbank()
                mm_group(bs[:], [(onesb[:], PT[:, pb, kc, :]) for kc in range(2)], [RPT[pb], Rc], Rbs)
                S.op("act", lambda e: e.activation(out=rr[:, pb, :], in_=bs[:], func=AF.Ln), reads=[Rbs], writes=[Rrr[pb]])
                S.op("act", lambda e: e.activation(out=rr[:, pb, :], in_=rr[:, pb, :], func=AF.Exp, scale=-1.0),
                     reads=[Rrr[pb]], writes=[Rrr[pb]])
                for cc in range(2):
                    bo, Rbo = mbank()
                    mm_group(bo[:], [(Vt[:, 0, kc, (2 * hd + cc) * 128:(2 * hd + cc + 1) * 128], PT[:, pb, kc, :]) for kc in range(2)],
                             [RPT[pb], RKV], Rbo)
                    S.op("dve", lambda e, bo=bo, cc=cc: e.tensor_tensor(
                        out=OT[:, b, 2 * hd + cc, :], in0=bo[:], in1=rr[:, pb, :], op=ALU.mult),
                        reads=[Rbo, Rrr[pb]], writes=[ROT[b]])

            def wo_group(b, gidx):
                i4, hf = gidx // 2, gidx % 2
                i = b * 4 + i4
                bk, Rb = mbank()
                mm_group(bk[:], [(OT[:, b, k, i4 * 128:(i4 + 1) * 128], w_o_v[:, k, hf * 512:(hf + 1) * 512]) for k in range(KD)],
                         [ROT[b], RWB1], Rb)
                S.op("dve", lambda e: e.tensor_tensor(
                    out=X[:, i, hf * 512:(hf + 1) * 512], in0=bk[:], in1=X[:, i, hf * 512:(hf + 1) * 512], op=ALU.add),
                    reads=[Rb, RX[i]], writes=[RX[i]])
                if hf == 1:
                    early_square(i)

            st0 = p1b_conv(0)
            p1b_lnstat(0, *st0)
            p1b_lnapply(0)
            st1 = p1b_conv(1)
            p1b_lnstat(1, *st1)
            if not last_half:
                S.op("dve", lambda e: e.tensor_copy(gtail[:], gluT[:, :, GT:GT + 32]), reads=[Rglu], writes=[Rtail])
            p1b_poolproj(0)
            p1b_wout(0)
            p1b_poolproj(1)
            norm_stats([X[:, i, :] for i in range(4)], RX[0:4], 4, squares=False, c0=0)
            p2_norm(0)
            p1b_lnapply(1)
            for c in range(8):
                q_chunk(c)
            p1b_wout(1)
            load_pass(0)
            norm_stats([X[:, i, :] for i in range(4, 8)], RX[4:8], 4, squares=False, c0=4)
            p2_norm(1)
            emit_scores(0)
            emit_scores(1)
            q_chunk(0); q_chunk(1)
            emit_pv(0, 0)
            emit_scores(2)
            q_chunk(2); q_chunk(3)
            emit_pv(0, 1)
            emit_scores(3)
            q_chunk(4); q_chunk(5)
            emit_pv(0, 2)
            q_chunk(6); q_chunk(7)
            emit_pv(0, 3)
            emit_scores(0)
            emit_scores(1)
            wo_group(0, 0); wo_group(0, 1)
            emit_pv(1, 0)
            emit_scores(2)
            wo_group(0, 2); wo_group(0, 3)
            emit_pv(1, 1)
            emit_scores(3)
            wo_group(0, 4); wo_group(0, 5)
            emit_pv(1, 2)
            wo_group(0, 6); wo_group(0, 7)
            emit_pv(1, 3)
            for g_ in range(8):
                wo_group(1, g_)
            load_pass(1)
            if gi + 1 < len(groups):
                load_w_in()

            norm_stats([X[:, i, :] for i in range(NT)], RX, NT, squares=False)
            for i in range(NT):
                norm_tile_to_T(X[:, i, :], RX[i], i, V_GFFN, hTg[:, :, i * 128:(i + 1) * 128], RhTg)
            units = [(p, b) for p in range(len(passes)) for b in range(NBLK)]
            seq_first_blk = first
            ucount = [0]

            def emit_up(p, b, jj, ab):
                j0, n = passes[p]
                slot = p % 2
                g_, v_, dn_ = ffn_views(slot)
                RW = RWB0 if slot == 0 else RWB1
                j = j0 + jj
                pa = (ucount[0]) % 2
                ucount[0] += 1
                gb = h * NBLK + b
                rpar, wpar = gb % 2, (gb + 1) % 2
                banks = []
                for xi, wv in enumerate((g_, v_)):
                    bk, Rb = mbank()
                    mm_group(bk[:], [(wv[:, k, jj * 128:(jj + 1) * 128], hTg[:, k, b * 512:(b + 1) * 512]) for k in range(KD)],
                             [RhTg, RW], Rb)
                    banks.append((bk, Rb))
                for xi in range(2):
                    bk, Rb = banks[xi]
                    ch = j + xi * NFF
                    a = aGV[:, pa, xi, :]
                    Ra = RaGV[pa][xi]
                    U = Ub[:, pa, xi, :]
                    RU = RUb[pa][xi]
                    if not (last_half and b == NBLK - 1):
                        S.op("act", lambda e, bk=bk, ch=ch: e.copy(out=fH[:, wpar, ch, :], in_=bk[:, 510:512]),
                             reads=[Rb], writes=[RfH[wpar]])
                    if first and b == 0:
                        S.op("act", lambda e, U=U: e.activation(out=U[:, 0:2], in_=vecs[:, 0:2], func=AF.Copy, scale=0.0),
                             reads=[Rc], writes=[RU])
                    else:
                        S.op("act", lambda e, U=U, ch=ch: e.copy(out=U[:, 0:2], in_=fH[:, rpar, ch, :]),
                             reads=[RfH[rpar]], writes=[RU])
                    S.op("act", lambda e, bk=bk, U=U: e.copy(out=U[:, 2:514], in_=bk[:]), reads=[Rb], writes=[RU])
                    S.op("act", lambda e, bk=bk, a=a, ch=ch: e.activation(
                        out=a, in_=bk[:], func=AF.Identity, scale=vcol(V_FW + ch * 3 + 2), bias=vcol(V_FB + ch)),
                        reads=[Rb, Rc], writes=[Ra])
                for xi in range(2):
                    ch = j + xi * NFF
                    a = aGV[:, pa, xi, :]
                    Ra = RaGV[pa][xi]
                    U = Ub[:, pa, xi, :]
                    RU = RUb[pa][xi]
                    S.op("dve", lambda e, U=U, a=a, ch=ch: e.scalar_tensor_tensor(
                        out=a, in0=U[:, 1:513], scalar=vcol(V_FW + ch * 3 + 1), in1=a,
                        op0=ALU.mult, op1=ALU.add), reads=[RU, Rc, Ra], writes=[Ra])
                    S.op("dve", lambda e, U=U, a=a, ch=ch: e.scalar_tensor_tensor(
                        out=a, in0=U[:, 0:512], scalar=vcol(V_FW + ch * 3 + 0), in1=a,
                        op0=ALU.mult, op1=ALU.add), reads=[RU, Rc, Ra], writes=[Ra])
                return (pa, ab, jj)

            def emit_gate(pa, ab, jj):
                ag = aGV[:, pa, 0, :]
                S.op("act", lambda e: e.activation(out=ag, in_=ag, func=AF.Silu),
                     reads=[RaGV[pa][0]], writes=[RaGV[pa][0]])
                S.op("dve", lambda e: e.tensor_tensor(out=actT[:, ab, jj, :], in0=ag, in1=aGV[:, pa, 1, :], op=ALU.mult),
                     reads=[RaGV[pa][0], RaGV[pa][1]], writes=[RactT[ab]])

            def emit_down(p, b, ab):
                j0, n = passes[p]
                slot = p % 2
                g_, v_, dn_ = ffn_views(slot)
                RW = RWB0 if slot == 0 else RWB1
                for i4 in range(4):
                    i = b * 4 + i4
                    for hf in range(2):
                        bk, Rb = mbank()
                        mm_group(bk[:], [(actT[:, ab, jj, i4 * 128:(i4 + 1) * 128], dn_[:, jj, hf * 512:(hf + 1) * 512]) for jj in range(n)],
                                 [RactT[ab], RW], Rb)
                        S.op("dve", lambda e, bk=bk, i=i, hf=hf: e.tensor_tensor(
                            out=X[:, i, hf * 512:(hf + 1) * 512], in0=bk[:], in1=X[:, i, hf * 512:(hf + 1) * 512], op=ALU.add),
                            reads=[Rb, RX[i]], writes=[RX[i]])

            def emit_final(b):
                tiles = [b * 4 + i4 for i4 in range(4)]
                for i in tiles:
                    S.op("act", lambda e, i=i: e.activation(out=hb2[:, i % 2, :], in_=X[:, i, :], func=AF.Square,
                                                            accum_out=ssb[:, i:i + 1]),
                         reads=[RX[i]], writes=[Rhb2[i % 2], Rss])
                lo_, hi_ = tiles[0], tiles[-1] + 1
                S.op("dve", lambda e: e.tensor_scalar(out=rstd[:, lo_:hi_], in0=ssb[:, lo_:hi_], scalar1=1.0 / D, scalar2=EPS,
                                                      op0=ALU.mult, op1=ALU.add), reads=[Rss], writes=[Rrstd])
                S.op("act", lambda e: e.activation(out=rstd[:, lo_:hi_], in_=rstd[:, lo_:hi_], func=AF.Ln),
                     reads=[Rrstd], writes=[Rrstd])
                S.op("act", lambda e: e.activation(out=rstd[:, lo_:hi_], in_=rstd[:, lo_:hi_], func=AF.Exp, scale=-0.5),
                     reads=[Rrstd], writes=[Rrstd])
                for i in tiles:
                    ob = i % 2
                    S.op("dve", lambda e, i=i: e.scalar_tensor_tensor(
                        out=X[:, i, :], in0=X[:, i, :], scalar=rstd[:, i:i + 1], in1=gfinb[:], op0=ALU.mult, op1=ALU.mult),
                        reads=[RX[i], Rrstd, Rc], writes=[RX[i]])
                    S.dma("sp", oslots[i], dout[s, tok0 + i * 128: tok0 + (i + 1) * 128, :], X[:, i, :],
                          reads=[RX[i]], writes=[Rout[i]])

            prev = None
            pend_load = []
            STAGED = False
            pend_gate = None
            for ui, (p, b) in enumerate(units):
                j0, n = passes[p]
                ab = ui % 2
                g0 = emit_up(p, b, 0, ab)
                if STAGED:
                    if pend_gate is not None:
                        emit_gate(*pend_gate)
                    pend_gate = g0
                else:
                    emit_gate(*g0)
                if prev is not None:
                    pp, pb_, pab = prev
                    emit_down(pp, pb_, pab)
                    if pp == len(passes) - 1:
                        emit_final(pb_)
                        if gi + 1 < len(groups):
                            issue_x(gi + 1, range(pb_ * 4, pb_ * 4 + 4))
                    if pb_ == NBLK - 1 and pp + 2 < len(passes):
                        load_pass(pp + 2)
                for jj in range(1, n):
                    gj = emit_up(p, b, jj, ab)
                    if STAGED:
                        emit_gate(*pend_gate)
                        pend_gate = gj
                    else:
                        emit_gate(*gj)
                    if jj == 1 and pend_load:
                        load_pass(pend_load.pop())
                prev = (p, b, ab)
            if STAGED:
                emit_gate(*pend_gate)
            pp, pb_, pab = prev
            emit_down(pp, pb_, pab)
            if gi + 1 < len(groups):
                p1a_norm(0)
            emit_final(pb_)
            if gi + 1 < len(groups):
                issue_x(gi + 1, range(pb_ * 4, pb_ * 4 + 4))
            if gi + 1 < len(groups) and groups[gi + 1][1] != 0:
                load_w_out_o()

        flush_stores()
        S.final_wait("sp", Rout + [v_[1] for v_ in scr.values()])
        S.emit()
    return nc


_tt_small_cache = {}


def _prep_inputs(inputs):
    f = lambda a: np.ascontiguousarray(np.asarray(a, dtype=np.float32))
    vec = np.zeros((128, V_N), np.float32)

    def fm(v, n):
        return np.asarray(v, np.float32).reshape(n, 128).T

    vec[:, V_GMIX:V_GMIX + 8] = fm(inputs["norm_mix_g"][0], 8)
    vec[:, V_GX:V_GX + 8] = fm(inputs["norm_xattn_g"][0], 8)
    vec[:, V_GMEM:V_GMEM + 8] = fm(inputs["norm_mem_g"][0], 8)
    vec[:, V_GFFN:V_GFFN + 8] = fm(inputs["norm_ffn_g"][0], 8)
    vec[:, V_CB:V_CB + 4] = fm(inputs["conv_dw_b"][0], 4)
    vec[:, V_LNG:V_LNG + 4] = fm(inputs["conv_ln_g"][0], 4)
    vec[:, V_LNB:V_LNB + 4] = fm(inputs["conv_ln_b"][0], 4)
    vec[:, V_PSC:V_PSC + 4] = fm(inputs["pool_scale"][0], 4)
    cw = np.asarray(inputs["conv_dw_w"][0], np.float32)
    vec[:, V_CW:V_CW + 4 * CW] = cw.reshape(CW, 4, 128).transpose(2, 1, 0).reshape(128, 4 * CW)
    fw = np.asarray(inputs["ffn_dw_w"][0], np.float32)
    vec[:, V_FW:V_FW + 132] = fw.reshape(3, 44, 128).transpose(2, 1, 0).reshape(128, 132)
    vec[:, V_FB:V_FB + 44] = fm(inputs["ffn_dw_b"][0], 44)
    gfinb = np.ascontiguousarray(np.broadcast_to(np.asarray(inputs["norm_final_g"], np.float32)[None, :], (128, D)))
    shared = {
        "vecs": vec, "gfinb": gfinb,
        "w_in": f(inputs["w_in"][0]), "pool_w": f(inputs["pool_w"][0]), "w_out": f(inputs["w_out"][0]),
        "w_q": f(inputs["w_q"][0]), "w_kv": f(inputs["w_kv"][0]), "w_o": f(inputs["w_o"][0]),
        "w_up": f(inputs["w_up"][0]), "w_down": f(inputs["w_down"][0]),
    }
    return shared


def kernel(**inputs):
    x = np.asarray(inputs["x"], np.float32)
    mem = np.asarray(inputs["mem"], np.float32)
    shared = _prep_inputs(inputs)
    nc = build_program()
    in_maps = []
    for c in range(NCORES):
        m = dict(shared)
        m["x"] = np.ascontiguousarray(x[c * SEQ_PER_CORE:(c + 1) * SEQ_PER_CORE])
        m["mem"] = np.ascontiguousarray(mem[c * SEQ_PER_CORE:(c + 1) * SEQ_PER_CORE])
        in_maps.append(m)
    res = run_bass_kernel_spmd(nc, in_maps, core_ids=list(range(NCORES)))
    out = np.concatenate([np.asarray(r["out"], np.float32) for r in res.results], axis=0)
    return out
```

```python
import numpy as np
from contextlib import ExitStack
import concourse.bass as bass
import concourse.mybir as mybir
from concourse.bass_utils import run_bass_kernel_spmd

F32 = mybir.dt.float32
BF16 = mybir.dt.bfloat16
AF = mybir.ActivationFunctionType
ALU = mybir.AluOpType

NCORES = 8
SEQ_PER_CORE = 2
D = 1024
KD = 8
SEQ = 2048
GT = 1024
NT = GT // 128
NBLK = GT // 512
NMEM = 256
DFF = 2816
NFF = DFF // 128
CW = 31
EPS = 1e-6
FFN_G = 4
FFN_PASSES = [(0, 4), (4, 4), (8, 4), (12, 4), (16, 3), (19, 3)]

V_GMIX, V_GX, V_GMEM, V_GFFN = 0, 8, 16, 24
V_CB, V_LNG, V_LNB, V_PSC = 32, 36, 40, 44
V_CW = 48
V_FW = V_CW + 4 * CW
V_FB = V_FW + 44 * 3
V_N = V_FB + 44


class Res:
    __slots__ = ("w", "r", "ov", "lo", "hi")

    def __init__(self, lo=None, hi=None):
        self.w = None
        self.r = []
        self.ov = []
        self.lo = lo
        self.hi = hi


class Sched:
    ENG = ("pe", "act", "dve", "pool", "sp")

    def __init__(self, nc, stack):
        self.nc = nc
        self.stack = stack
        self.prog = {e: [] for e in self.ENG}
        self.sem = {e: stack.enter_context(nc.semaphore("s_" + e)) for e in self.ENG}
        self.cnt = {e: 0 for e in self.ENG}
        self.waited = {e: {} for e in self.ENG}
        self.fence_tok = []
        self.abs = {e: [] for e in self.ENG}

    def check(self):
        val = {}
        ptr = {e: 0 for e in self.ENG}
        progress = True
        while progress:
            progress = False
            for e in self.ENG:
                while ptr[e] < len(self.abs[e]):
                    waits, sem, n = self.abs[e][ptr[e]]
                    if all(val.get(id(s_), 0) >= v for s_, v in waits):
                        if sem is not None:
                            val[id(sem)] = val.get(id(sem), 0) + n
                        ptr[e] += 1
                        progress = True
                    else:
                        break
        stuck = {e: (ptr[e], len(self.abs[e])) for e in self.ENG if ptr[e] < len(self.abs[e])}
        return stuck

    def _deps(self, eng, reads, writes, extra=(), skip_key=None):
        deps = {}

        def add(tok):
            if tok is None:
                return
            k, v = tok
            if deps.get(k, 0) < v:
                deps[k] = v

        for r in reads:
            add(r.w)
            for o in r.ov:
                add(o.w)
        for w in writes:
            add(w.w)
            for t in w.r:
                add(t)
            for o in w.ov:
                add(o.w)
                for t in o.r:
                    add(t)
        for t in extra:
            add(t)
        out = []
        for k, v in deps.items():
            if k == "pe" and eng == "pe":
                continue
            if skip_key is not None and k is skip_key:
                continue
            if self.waited[eng].get(k, 0) >= v:
                continue
            self.waited[eng][k] = v
            out.append((self.sem[k] if isinstance(k, str) else k, v))
        return out

    def _mark(self, tok, reads, writes):
        for r in reads:
            r.r.append(tok)
            if len(r.r) > 64:
                best = {}
                for k, v in r.r:
                    if best.get(k, 0) < v:
                        best[k] = v
                r.r = list(best.items())
        for w in writes:
            w.w = tok
            w.r = []

    def op(self, eng, fn, reads=(), writes=(), inc=True):
        waits = self._deps(eng, reads, writes)
        if inc:
            self.cnt[eng] += 1
        tok = (eng, self.cnt[eng] if inc else self.cnt[eng] + 1)
        sem = self.sem[eng]

        def thunk(e, waits=waits, fn=fn, inc=inc, sem=sem):
            for s, v in waits:
                e.wait_ge(s, v)
            ins = fn(e)
            if inc:
                ins.then_inc(sem, 1)

        self.prog[eng].append(thunk)
        self.abs[eng].append((waits, sem if inc else None, 1))
        self._mark(tok, reads, writes)
        return tok

    def newslot(self, name):
        return {"sem": self.stack.enter_context(self.nc.semaphore(name)), "cnt": 0}

    def dma(self, eng, slot, out, in_, reads=(), writes=(), extra=(), **kw):
        waits = self._deps(eng, reads, writes, extra, skip_key=slot["sem"])
        slot["cnt"] += 16
        tok = (slot["sem"], slot["cnt"])

        def thunk(e, waits=waits, out=out, in_=in_, sem=slot["sem"], kw=kw):
            for s, v in waits:
                e.wait_ge(s, v)
            e.dma_start(out=out, in_=in_, **kw).then_inc(sem, 16)

        self.prog[eng].append(thunk)
        self.abs[eng].append((waits, slot["sem"], 16))
        self._mark(tok, reads, writes)
        return tok

    def fence(self):
        comp = ("pe", "act", "dve")
        snap = [(e, self.cnt[e]) for e in comp if self.cnt[e] > 0]
        self.fence_tok = snap
        for e in comp:
            waits = []
            for p, v in snap:
                if p == e or self.waited[e].get(p, 0) >= v:
                    continue
                self.waited[e][p] = v
                waits.append((self.sem[p], v))
            if waits:
                self.prog[e].append(lambda eng, waits=waits: [eng.wait_ge(s, v) for s, v in waits])
                self.abs[e].append((waits, None, 0))

    def final_wait(self, eng, resources):
        waits = self._deps(eng, resources, ())
        self.prog[eng].append(lambda e, waits=waits: [e.wait_ge(s, v) for s, v in waits])

    def emit(self):
        with self.nc.Block() as block:
            @block.tensor
            def _(e):
                for t in self.prog["pe"]:
                    t(e)

            @block.scalar
            def _(e):
                for t in self.prog["act"]:
                    t(e)

            @block.vector
            def _(e):
                for t in self.prog["dve"]:
                    t(e)

            @block.gpsimd
            def _(e):
                for t in self.prog["pool"]:
                    t(e)

            @block.sync
            def _(e):
                for t in self.prog["sp"]:
                    t(e)


def build_program(nseq=SEQ_PER_CORE, seqlen=SEQ):
    nc = bass.Bass("TRN2", target_bir_lowering=False)
    nhalf = seqlen // GT
    dx = nc.dram_tensor("x", [nseq, seqlen, D], F32, kind="ExternalInput").ap()
    dmem = nc.dram_tensor("mem", [nseq, NMEM, D], F32, kind="ExternalInput").ap()
    dvec = nc.dram_tensor("vecs", [128, V_N], F32, kind="ExternalInput").ap()
    dgfin = nc.dram_tensor("gfinb", [128, D], F32, kind="ExternalInput").ap()
    dw_in = nc.dram_tensor("w_in", [D, 1536], F32, kind="ExternalInput").ap()
    dpoolw = nc.dram_tensor("pool_w", [4, 128, 128], F32, kind="ExternalInput").ap()
    dw_out = nc.dram_tensor("w_out", [D, D], F32, kind="ExternalInput").ap()
    dw_q = nc.dram_tensor("w_q", [D, D], F32, kind="ExternalInput").ap()
    dw_kv = nc.dram_tensor("w_kv", [D, 2 * D], F32, kind="ExternalInput").ap()
    dw_o = nc.dram_tensor("w_o", [D, D], F32, kind="ExternalInput").ap()
    dw_up = nc.dram_tensor("w_up", [D, 2 * DFF], F32, kind="ExternalInput").ap()
    dw_down = nc.dram_tensor("w_down", [DFF, D], F32, kind="ExternalInput").ap()
    dout = nc.dram_tensor("out", [nseq, seqlen, D], F32, kind="ExternalOutput").ap()

    kp = lambda ap: ap.rearrange("(k p) n -> p k n", p=128)

    with ExitStack() as st:
        S = Sched(nc, st)
        base = (nc._sbuf_addr_for_side("left") + 63) // 64 * 64
        top = nc._sbuf_addr_for_side("right")
        cur = [base]
        nalloc = [0]
        binfo = {}
        regs = []

        def alloc(shape, dt, at=None):
            nb = 2 if dt == BF16 else 4
            sz = nb
            for s_ in shape[1:]:
                sz *= s_
            sz = (sz + 63) // 64 * 64
            if at is None:
                off = cur[0]
                cur[0] += sz
            else:
                off = at[0]
                at[0] += sz
            assert off + sz <= top, ("SBUF overflow", off + sz - top)
            nalloc[0] += 1
            t_ = nc.alloc_sbuf_tensor_at("t%d" % nalloc[0], list(shape), dt, offset=off)
            binfo[id(t_)] = (off, sz)
            return t_

        def REG(buf, idx=None, nsub=1, n=1):
            off, sz = binfo[id(buf)]
            if idx is None:
                lo, hi = off, off + sz
            else:
                sub = sz // nsub
                lo, hi = off + idx * sub, off + (idx + n) * sub
            r_ = Res(lo, hi)
            regs.append(r_)
            return r_

        def link_regs():
            for i_, a_ in enumerate(regs):
                for b_ in regs[i_ + 1:]:
                    if a_.lo < b_.hi and b_.lo < a_.hi:
                        a_.ov.append(b_)
                        b_.ov.append(a_)

        ident = alloc([128, 128], BF16)
        hident = alloc([128, 128], BF16)
        ones32 = alloc([128, 128], F32)
        onesb = alloc([128, 128], BF16)
        vecs = alloc([128, V_N], F32)
        gfinb = alloc([128, D], F32)
        invc = alloc([128, 4, 16], F32)
        fH = alloc([128, 2, 44, 2], F32)
        poolw = alloc([128, 4, 128], BF16)
        ssb = alloc([128, 16], F32)
        rstd = alloc([128, 16], F32)
        diag = alloc([128, 4 * CW, 128], BF16)
        KT = alloc([128, 1, 8, NMEM], BF16)
        Vt = alloc([128, 1, 2, D], BF16)
        X = alloc([128, NT, D], F32)
        WA = alloc([128, 12288], BF16)
        WB = alloc([128, 24576], BF16)
        hb2 = alloc([128, 2, D], BF16)
        smt = alloc([128, 4, 16], F32)
        gtail = alloc([128, 4, 32], BF16)
        utail = alloc([128, 4, 16], F32)
        ov_base = cur[0]
        identf = alloc([128, 128], F32, [ov_base + 16384])

        w_in_v = WA[:, 0:8 * 1536].rearrange("p (k n) -> p k n", k=8)
        w_q_v = WA[:, 0:8 * 1024].rearrange("p (k n) -> p k n", k=8)
        w_kv_v = WB[:, 0:16384].rearrange("p (k n) -> p k n", k=8)
        w_out_v = WB[:, 0:8192].rearrange("p (k n) -> p k n", k=8)
        w_o_v = WB[:, 12288:20480].rearrange("p (k n) -> p k n", k=8)

        def ffn_views(slot):
            b0 = slot * 12288
            g = WB[:, b0:b0 + 4096].rearrange("p (k n) -> p k n", k=8)
            v = WB[:, b0 + 4096:b0 + 8192].rearrange("p (k n) -> p k n", k=8)
            dn = WB[:, b0 + 8192:b0 + 12288].rearrange("p (j n) -> p j n", j=4)
            return g, v, dn

        o = [ov_base]
        memx = alloc([128, 2, D], F32, o)
        memT = alloc([128, 8, NMEM], BF16, o)
        o = [ov_base]
        gluT = alloc([128, 4, 32 + GT], BF16, o)
        pooledT = alloc([128, 4, GT], BF16, o)
        o1 = [o[0]]
        hTb = alloc([128, 2, 8, 512], BF16, o1)
        upT = alloc([128, 4, 528], F32, o1)
        ptmp = alloc([128, 2, 528], F32, o1)
        th = alloc([128, 2, 512], F32, o1)
        o1 = [o[0]]
        yT = alloc([128, 8, 512], BF16, o1)
        hc = alloc([128, 2, 4, 512], F32, o1)
        hsq = alloc([128, 2, 512], F32, o1)
        lst = alloc([128, 2, 512], F32, o1)
        o = [ov_base]
        hTb2 = alloc([128, 8, 512], BF16, o)
        QT = alloc([128, 8, 512], BF16, o)
        PT = alloc([128, 2, 2, 512], BF16, o)
        OT = alloc([128, 2, 8, 512], BF16, o)
        rr = alloc([128, 2, 512], F32, o)
        o = [ov_base]
        hTg = alloc([128, 8, GT], BF16, o)
        actT = alloc([128, 2, FFN_G, 512], BF16, o)
        aGV = alloc([128, 2, 2, 512], F32, o)
        Ub = alloc([128, 2, 2, 514], F32, o)

        ptb = [st.enter_context(nc.psum_tensor("ptb%d" % i, [128, 1024], BF16)) for i in range(2)]
        pmb = [st.enter_context(nc.psum_tensor("pmb%d" % i, [128, 512], F32)) for i in range(6)]
        Rptb = [Res() for _ in range(2)]
        Rpmb = [Res() for _ in range(6)]
        rot = {"t": 0, "m": 0}

        def tbank():
            i = rot["t"] % 2
            rot["t"] += 1
            return ptb[i], Rptb[i]

        def mbank():
            i = rot["m"] % 6
            rot["m"] += 1
            return pmb[i], Rpmb[i]

        Rc = Res()
        RX = [Res() for _ in range(NT)]
        RWA, RWB0, RWB1 = Res(), Res(), Res()
        Rdiag, RKV, Rpw = Res(), Res(), Res()
        Rss, Rrstd, Rhb2, Rsm, Rtail = Res(), Res(), [Res(), Res()], Res(), Res()
        RfH = [Res(), Res()]
        Rout = [Res() for _ in range(NT)]
        Rm1, RmemT = REG(memx), REG(memT)
        Rglu, Rpooled, RupT, Rptmp = REG(gluT), REG(pooledT), REG(upT), REG(ptmp)
        RhTb = [REG(hTb, 0, 2), REG(hTb, 1, 2)]
        Rth = [REG(th, i, 2) for i in range(2)]
        RyT = REG(yT)
        Rhc = [[REG(hc, b_ * 4 + c, 8) for c in range(4)] for b_ in range(2)]
        Rhsq = [REG(hsq, i, 2) for i in range(2)]
        Rlst = [REG(lst, 0, 2), REG(lst, 1, 2)]
        RhTb2 = REG(hTb2)
        RQT = [REG(QT, 2 * hd, 8, 2) for hd in range(4)]
        RPT = [REG(PT, i, 2) for i in range(2)]
        ROT = [REG(OT, i, 2) for i in range(2)]
        Rrr = [REG(rr, i, 2) for i in range(2)]
        RhTg = REG(hTg)
        RactT = [REG(actT, i, 2) for i in range(2)]
        RaGV = [[REG(aGV, pa * 2 + xi, 4) for xi in range(2)] for pa in range(2)]
        RUb = [[REG(Ub, pa * 2 + xi, 4) for xi in range(2)] for pa in range(2)]
        Ridf = REG(identf)
        link_regs()

        slots = {n: S.newslot("d_" + n) for n in
                 ("c", "x0", "x1", "x2", "x3", "x4", "x5", "x6", "x7", "wa", "wb0", "wb1", "wa_h", "wb0_h", "wb1_h", "o0", "o1", "o2", "o3", "o4", "o5", "o6", "o7", "mem", "pw")}
        xslots = [slots["x%d" % i] for i in range(NT)]
        oslots = [slots["o%d" % i] for i in range(NT)]

        def vcol(c0, n=1):
            return vecs[:, c0:c0 + n]

        def mm_group(bank_ap, pairs, reads, Rbank, extra_reads=()):
            n = len(pairs)
            for i, (l, r) in enumerate(pairs):
                S.op("pe", lambda e, l=l, r=r, i=i: e.matmul(bank_ap, lhsT=l, rhs=r, start=(i == 0), stop=(i == n - 1)),
                     reads=list(reads) + list(extra_reads), writes=[Rbank], inc=(i == n - 1))

        def early_square(i):
            S.op("act", lambda e: e.activation(out=hb2[:, i % 2, :], in_=X[:, i, :], func=AF.Square,
                                               accum_out=ssb[:, i:i + 1]),
                 reads=[RX[i]], writes=[Rhb2[i % 2], Rss])

        def norm_stats(xtiles, Rx, ntile, squares=True, c0=0):
            for i in range(ntile if squares else 0):
                S.op("act", lambda e, i=i: e.activation(out=hb2[:, i % 2, :], in_=xtiles[i], func=AF.Square,
                                                        accum_out=ssb[:, c0 + i:c0 + i + 1]),
                     reads=[Rx[i]], writes=[Rhb2[i % 2], Rss])
            S.op("dve", lambda e: e.tensor_scalar(out=rstd[:, c0:c0 + ntile], in0=ssb[:, c0:c0 + ntile], scalar1=1.0 / D,
                                                  scalar2=EPS, op0=ALU.mult, op1=ALU.add), reads=[Rss], writes=[Rrstd])
            S.op("act", lambda e: e.activation(out=rstd[:, c0:c0 + ntile], in_=rstd[:, c0:c0 + ntile], func=AF.Ln),
                 reads=[Rrstd], writes=[Rrstd])
            S.op("act", lambda e: e.activation(out=rstd[:, c0:c0 + ntile], in_=rstd[:, c0:c0 + ntile], func=AF.Exp, scale=-0.5),
                 reads=[Rrstd], writes=[Rrstd])

        def norm_tile_to_T(xt, Rxt, i, gcol, dst, Rdst):
            hbi = hb2[:, i % 2, :]
            Rh = Rhb2[i % 2]
            S.op("act", lambda e: e.activation(out=hbi, in_=xt, func=AF.Copy, scale=rstd[:, i:i + 1]),
                 reads=[Rxt, Rrstd], writes=[Rh])
            tb, Rtb = tbank()
            for k in range(KD):
                S.op("pe", lambda e, k=k: e.transpose(out=tb[:, k * 128:(k + 1) * 128], in_=hbi[:, k * 128:(k + 1) * 128],
                                                      identity=ident[:]),
                     reads=[Rh, Rc], writes=[Rtb], inc=(k == KD - 1))
            S.op("dve", lambda e: e.tensor_tensor(out=dst, in0=tb[:].rearrange("p (k t) -> p k t", k=KD),
                                                  in1=vecs[:, gcol:gcol + KD].unsqueeze(2).to_broadcast([128, KD, 128]),
                                                  op=ALU.mult), reads=[Rtb, Rc], writes=[Rdst])

        pool_q = []

        scr = {}
        pend_st = []

        def flush_stores():
            while pend_st:
                key_, flat_, R_ = pend_st.pop(0)
                sc_, Rsc_, ssl_ = scr[key_]
                S.dma("sp", ssl_, sc_, flat_, reads=R_, writes=[Rsc_])

        def wload(key, slot, flat, parts, R, extra=()):
            flush_stores()
            if key not in scr:
                scr[key] = (nc.dram_tensor("sc_" + key, [128, flat.shape[1]], BF16).ap(), Res(), S.newslot("st_" + key))
                for dst_, src_ in parts:
                    S.dma("pool", slot, dst_, src_, writes=R, extra=extra)
                pend_st.append((key, flat, R))
            else:
                sc_, Rsc_, ssl_ = scr[key]
                hslot = {id(slots["wa"]): slots["wa_h"], id(slots["wb0"]): slots["wb0_h"], id(slots["wb1"]): slots["wb1_h"]}[id(slot)]
                S.dma("sp", hslot, flat, sc_, reads=[Rsc_], writes=R)

        def load_w_in(extra=()):
            wload("w_in", slots["wa"], WA[:, 0:12288], [(w_in_v, kp(dw_in))], [RWA], extra=extra)

        def load_w_q():
            wload("w_q", slots["wa"], WA[:, 0:8192], [(w_q_v, kp(dw_q))], [RWA])

        def load_w_kv():
            wload("w_kv", slots["wb0"], WB[:, 0:16384], [(w_kv_v, kp(dw_kv))], [RWB0, RWB1])

        def load_w_out_o():
            wload("w_out", slots["wb0"], WB[:, 0:8192], [(w_out_v, kp(dw_out))], [RWB0])
            wload("w_o", slots["wb1"], WB[:, 12288:20480], [(w_o_v, kp(dw_o))], [RWB1])

        diag_pending = [True]

        def build_diag():
            for c in range(4):
                for k in range(CW):
                    S.op("pool", lambda e, c=c, k=k: e.tensor_tensor(
                        out=diag[:, c * CW + k, :], in0=hident[:],
                        in1=vcol(V_CW + c * CW + k).to_broadcast([128, 128]), op=ALU.mult),
                        reads=[Rc], writes=[Rdiag])

        groups = [(s, h) for s in range(nseq) for h in range(nhalf)]

        def issue_x(gidx, tiles):
            s_, h_ = groups[gidx]
            for i in tiles:
                S.dma("sp", xslots[i], X[:, i, :], dx[s_, h_ * GT + i * 128: h_ * GT + (i + 1) * 128, :], writes=[RX[i]])
        S.dma("sp", slots["c"], vecs[:], dvec, writes=[Rc])
        S.dma("sp", slots["c"], gfinb[:], dgfin, writes=[Rc])
        def mem_loads(sq):
            for j in range(2):
                S.dma("sp", slots["mem"], memx[:, j, :], dmem[sq, j * 128:(j + 1) * 128, :], writes=[Rm1])

        mem_loads(0)
        issue_x(0, range(NT))
        load_w_kv()
        S.op("pool", lambda e: e.memset(identf[:], 0.0), writes=[Rc, Ridf])
        S.op("pool", lambda e: e.affine_select(out=identf[:], in_=identf[:], compare_op=ALU.not_equal, fill=1.0,
                                               base=0, pattern=[[-1, 128]], channel_multiplier=1),
             reads=[Rc, Ridf], writes=[Rc, Ridf])
        S.op("pool", lambda e: e.tensor_copy(ident[:], identf[:]), reads=[Rc, Ridf], writes=[Rc])
        S.op("dve", lambda e: e.tensor_scalar(out=hident[:], in0=ident[:], scalar1=0.5, scalar2=None, op0=ALU.mult),
             reads=[Rc], writes=[Rc])
        S.op("pool", lambda e: e.memset(ones32[:], 1.0 / 512.0), writes=[Rc])
        S.op("pool", lambda e: e.memset(onesb[:], 1.0), writes=[Rc])
        for g in range(4):
            w = 2 << g
            S.op("pool", lambda e, g=g, w=w: e.memset(invc[:, g, :], 1.0 / w), writes=[Rc])
            for t in range(w - 1):
                S.op("pool", lambda e, g=g, t=t: e.memset(invc[:, g, t:t + 1], 1.0 / (t + 1)), writes=[Rc])
        load_w_in(extra=[RWB1.w])
        S.dma("pool", slots["pw"], poolw[:], dpoolw.rearrange("g c d -> c g d"), writes=[Rpw])

        def p1a_stats(b):
            tl = list(range(b * 4, b * 4 + 4))
            norm_stats([X[:, i, :] for i in tl], [RX[i] for i in tl], 4, squares=True, c0=b * 4)

        def p1a_tile(b, i4):
            i = b * 4 + i4
            norm_tile_to_T(X[:, i, :], RX[i], i, V_GMIX, hTb[:, b, :, i4 * 128:(i4 + 1) * 128], RhTb[b])

        def p1a_norm(b):
            p1a_stats(b)
            for i4 in range(4):
                p1a_tile(b, i4)

        for gi, (s, h) in enumerate(groups):
            tok0 = h * GT
            first = (h == 0)
            last_half = (h == nhalf - 1)
            if first:
                if gi > 0:
                    mem_loads(s)
                    load_w_kv()
                if diag_pending[0]:
                    diag_pending[0] = False
                    build_diag()
                Rmem = [Rm1, Rm1]
                for sq in (s,):
                    norm_stats([memx[:, j, :] for j in range(2)], Rmem, 2, c0=8)
                    for j in range(2):
                        norm_tile_to_T(memx[:, j, :], Rmem[j], 8 + j, V_GMEM, memT[:, :, j * 128:(j + 1) * 128], RmemT)
                    for c in range(8):
                        bk, Rb = mbank()
                        mm_group(bk[:, 0:NMEM], [(w_kv_v[:, k, c * 128:(c + 1) * 128], memT[:, k, :]) for k in range(KD)],
                                 [RmemT, RWB0, RWB1], Rb)
                        if c % 2 == 0:
                            S.op("act", lambda e, c=c, bk=bk, sq=sq: e.copy(out=KT[:, 0, c, :], in_=bk[:, 0:NMEM]),
                                 reads=[Rb], writes=[RKV])
                        else:
                            S.op("dve", lambda e, c=c, bk=bk, sq=sq: e.tensor_copy(KT[:, 0, c, :], bk[:, 0:NMEM]),
                                 reads=[Rb], writes=[RKV])
                    for kc in range(2):
                        for hf in range(2):
                            bk, Rb = mbank()
                            mm_group(bk[:], [(memT[:, k, kc * 128:(kc + 1) * 128], w_kv_v[:, k, D + hf * 512:D + (hf + 1) * 512])
                                             for k in range(KD)], [RmemT, RWB0, RWB1], Rb)
                            if hf == 0:
                                S.op("act", lambda e, kc=kc, bk=bk, sq=sq: e.copy(out=Vt[:, 0, kc, 0:512], in_=bk[:]),
                                     reads=[Rb], writes=[RKV])
                            else:
                                S.op("dve", lambda e, kc=kc, bk=bk, sq=sq: e.tensor_copy(Vt[:, 0, kc, 512:1024], bk[:]),
                                     reads=[Rb], writes=[RKV])
                load_w_out_o()

            if first:
                S.op("dve", lambda e: e.memset(gluT[:, :, 0:32], 0.0), writes=[Rglu])
                S.op("dve", lambda e: e.memset(upT[:, :, 0:16], 0.0), writes=[RupT])
            elif True:
                S.op("dve", lambda e: e.tensor_copy(gluT[:, :, 0:32], gtail[:]), reads=[Rtail], writes=[Rglu])
                S.op("dve", lambda e: e.tensor_copy(upT[:, :, 0:16], utail[:]), reads=[Rtail], writes=[RupT])
            def p1a_conv(b, c):
                bg, Rbg = mbank()
                mm_group(bg[:], [(w_in_v[:, k, 512 + c * 128:512 + (c + 1) * 128], hTb[:, b, k, :]) for k in range(KD)],
                         [RhTb[b], RWA], Rbg)
                thb = th[:, c % 2, :]
                Rt = Rth[c % 2]
                S.op("act", lambda e: e.activation(out=thb, in_=bg[:], func=AF.Tanh, scale=0.5), reads=[Rbg], writes=[Rt])
                bv, Rbv = mbank()
                mm_group(bv[:], [(w_in_v[:, k, c * 128:(c + 1) * 128], hTb[:, b, k, :]) for k in range(KD)],
                         [RhTb[b], RWA], Rbv)
                S.op("dve", lambda e: e.scalar_tensor_tensor(
                    out=gluT[:, c, 32 + b * 512:32 + (b + 1) * 512], in0=thb, scalar=1.0, in1=bv[:],
                    op0=ALU.add, op1=ALU.mult), reads=[Rbv, Rt], writes=[Rglu])

            def p1a_pool(b, g):
                w = 2 << g
                bu, Rbu = mbank()
                mm_group(bu[:], [(w_in_v[:, k, 1024 + g * 128:1024 + (g + 1) * 128], hTb[:, b, k, :]) for k in range(KD)],
                         [RhTb[b], RWA], Rbu)
                S.op("act", lambda e: e.copy(out=upT[:, g, 16:528], in_=bu[:]), reads=[Rbu], writes=[RupT])
                src = upT[:, g, :]
                m = 2
                lvl = 0
                while m <= w:
                    dstb = ptmp[:, lvl % 2, :]
                    lo = m - 1
                    hs = m // 2
                    S.op("dve", lambda e, dstb=dstb, src=src, lo=lo, hs=hs: e.tensor_tensor(
                        out=dstb[:, lo:528], in0=src[:, lo:528], in1=src[:, lo - hs:528 - hs], op=ALU.add),
                        reads=[RupT, Rptmp], writes=[Rptmp])
                    src = dstb
                    m *= 2
                    lvl += 1
                S.op("dve", lambda e, src=src: e.scalar_tensor_tensor(
                    out=pooledT[:, g, b * 512:(b + 1) * 512], in0=src[:, 16:528], scalar=1.0 / w, in1=upT[:, g, 16:528],
                    op0=ALU.mult, op1=ALU.subtract), reads=[Rptmp, RupT], writes=[Rpooled])
                if first and b == 0:
                    S.op("dve", lambda e, src=src: e.tensor_tensor(
                        out=smt[:, g, :], in0=src[:, 16:32], in1=invc[:, g, :], op=ALU.mult),
                        reads=[Rptmp, Rc], writes=[Rsm])
                    S.op("dve", lambda e: e.tensor_tensor(
                        out=pooledT[:, g, 0:16], in0=smt[:, g, :], in1=upT[:, g, 16:32], op=ALU.subtract),
                        reads=[Rsm, RupT], writes=[Rpooled])
                S.op("dve", lambda e: e.tensor_copy(upT[:, g, 0:16], upT[:, g, 512:528]),
                     reads=[RupT, Rptmp], writes=[RupT])

            if gi == 0:
                p1a_norm(0)
            p1a_conv(0, 0)
            p1a_conv(0, 1)
            p1a_stats(1)
            p1a_conv(0, 2)
            p1a_tile(1, 0)
            p1a_conv(0, 3)
            p1a_tile(1, 1)
            p1a_pool(0, 0)
            p1a_tile(1, 2)
            p1a_pool(0, 1)
            p1a_tile(1, 3)
            p1a_pool(0, 2)
            p1a_pool(0, 3)
            for c in range(4):
                p1a_conv(1, c)
            for g in range(4):
                p1a_pool(1, g)
            if not last_half:
                S.op("dve", lambda e: e.tensor_copy(utail[:], upT[:, :, 0:16]), reads=[RupT], writes=[Rtail])
            load_w_q()


            def p1b_conv(b):
                bm, Rbm = mbank()
                bq, Rbq = mbank()

                def emit_conv(c):
                    bk, Rb = mbank()
                    c0 = 32 + b * 512 - (CW - 1)
                    mm_group(bk[:], [(diag[:, c * CW + k, :], gluT[:, c, c0 + k:c0 + k + 512]) for k in range(CW)],
                             [Rglu, Rdiag], Rb)
                    S.op("act", lambda e: e.activation(out=hc[:, b, c, :], in_=bk[:], func=AF.Identity,
                                                       bias=vcol(V_CB + c)), reads=[Rb, Rc], writes=[Rhc[b][c]])
                    S.op("act", lambda e: e.activation(out=hsq[:, c % 2, :], in_=bk[:], func=AF.Square,
                                                       bias=vcol(V_CB + c)), reads=[Rb, Rc], writes=[Rhsq[c % 2]])

                def emit_stat(c):
                    S.op("pe", lambda e: e.matmul(bm[:], lhsT=ones32[:], rhs=hc[:, b, c, :], start=(c == 0), stop=(c == 3)),
                         reads=[Rhc[b][c], Rc], writes=[Rbm])
                    S.op("pe", lambda e: e.matmul(bq[:], lhsT=ones32[:], rhs=hsq[:, c % 2, :], start=(c == 0), stop=(c == 3)),
                         reads=[Rhsq[c % 2], Rc], writes=[Rbq])

                emit_conv(0)
                emit_conv(1)
                emit_stat(0)
                emit_conv(2)
                emit_stat(1)
                emit_conv(3)
                emit_stat(2)
                emit_stat(3)
                return bm, Rbm, bq, Rbq

            def p1b_lnstat(b, bm, Rbm, bq, Rbq):
                S.op("act", lambda e: e.activation(out=hsq[:, 0, :], in_=bm[:], func=AF.Square), reads=[Rbm], writes=[Rhsq[0]])
                S.op("act", lambda e: e.copy(out=lst[:, 0, :], in_=bm[:]), reads=[Rbm], writes=[Rlst[0]])
                S.op("dve", lambda e: e.scalar_tensor_tensor(out=lst[:, 1, :], in0=bq[:], scalar=EPS, in1=hsq[:, 0, :],
                                                             op0=ALU.add, op1=ALU.subtract),
                     reads=[Rbq, Rhsq[0]], writes=[Rlst[1]])
                S.op("act", lambda e: e.activation(out=lst[:, 1, :], in_=lst[:, 1, :], func=AF.Ln), reads=[Rlst[1]], writes=[Rlst[1]])
                S.op("act", lambda e: e.activation(out=lst[:, 1, :], in_=lst[:, 1, :], func=AF.Exp, scale=-0.5),
                     reads=[Rlst[1]], writes=[Rlst[1]])

            def p1b_lnapply(b):
                def ln_apply(c):
                    hcb = hc[:, b, c, :]
                    S.op("dve", lambda e: e.tensor_tensor(out=hcb, in0=hcb, in1=lst[:, 0, :], op=ALU.subtract),
                         reads=[Rhc[b][c], Rlst[0]], writes=[Rhc[b][c]])
                    S.op("dve", lambda e: e.tensor_tensor(out=hcb, in0=hcb, in1=lst[:, 1, :], op=ALU.mult),
                         reads=[Rlst[1], Rhc[b][c]], writes=[Rhc[b][c]])
                    S.op("act", lambda e: e.activation(out=yT[:, c, :], in_=hcb, func=AF.Silu,
                                                       scale=vcol(V_LNG + c), bias=vcol(V_LNB + c)),
                         reads=[Rhc[b][c], Rc], writes=[RyT])

                for c in range(4):
                    ln_apply(c)

            def p1b_poolproj(b):
                def pool_proj(g):
                    bk, Rb = mbank()
                    mm_group(bk[:], [(poolw[:, g, :], pooledT[:, g, b * 512:(b + 1) * 512])], [Rpooled, Rpw], Rb)
                    S.op("act", lambda e: e.activation(out=yT[:, 4 + g, :], in_=bk[:], func=AF.Copy,
                                                       scale=vcol(V_PSC + g)), reads=[Rb, Rc], writes=[RyT])

                for g in range(4):
                    pool_proj(g)

            def p1b_wout(b):
                def wout_tile(i4, hf):
                    i = b * 4 + i4
                    bk, Rb = mbank()
                    mm_group(bk[:], [(yT[:, k, i4 * 128:(i4 + 1) * 128], w_out_v[:, k, hf * 512:(hf + 1) * 512]) for k in range(KD)],
                             [RyT, RWB0], Rb)
                    S.op("dve", lambda e: e.tensor_tensor(
                        out=X[:, i, hf * 512:(hf + 1) * 512], in0=bk[:], in1=X[:, i, hf * 512:(hf + 1) * 512], op=ALU.add),
                        reads=[Rb, RX[i]], writes=[RX[i]])
                    if hf == 1:
                        early_square(i)

                for i4 in range(4):
                    for hf in range(2):
                        wout_tile(i4, hf)

            passes = FFN_PASSES

            def load_pass(p):
                j0, n = passes[p]
                slot = p % 2
                g_, v_, dn_ = ffn_views(slot)
                R = [RWB0] if slot == 0 else [RWB1]
                sl = slots["wb0"] if slot == 0 else slots["wb1"]
                wload("p%d" % p, sl, WB[:, slot * 12288:(slot + 1) * 12288],
                      [(g_[:, :, 0:n * 128], kp(dw_up)[:, :, j0 * 128:(j0 + n) * 128]),
                       (v_[:, :, 0:n * 128], kp(dw_up)[:, :, DFF + j0 * 128:DFF + (j0 + n) * 128]),
                       (dn_[:, 0:n, :], dw_down.rearrange("(j p) n -> p j n", p=128)[:, j0:j0 + n, :])], R)


            def p2_norm(b):
                for i4 in range(4):
                    i = b * 4 + i4
                    norm_tile_to_T(X[:, i, :], RX[i], i, V_GX, hTb2[:, :, i4 * 128:(i4 + 1) * 128], RhTb2)

            def q_chunk(c):
                bk, Rb = mbank()
                mm_group(bk[:], [(w_q_v[:, k, c * 128:(c + 1) * 128], hTb2[:, k, :]) for k in range(KD)], [RhTb2, RWA], Rb)
                if c % 2 == 0:
                    S.op("act", lambda e: e.copy(out=QT[:, c, :], in_=bk[:]), reads=[Rb], writes=[RQT[c // 2]])
                else:
                    S.op("dve", lambda e: e.tensor_copy(QT[:, c, :], bk[:]), reads=[Rb], writes=[RQT[c // 2]])

            def emit_scores(hd):
                pb = hd % 2
                for kc in range(2):
                    bk, Rb = mbank()
                    mm_group(bk[:], [(KT[:, 0, 2 * hd + cc, kc * 128:(kc + 1) * 128], QT[:, 2 * hd + cc, :]) for cc in range(2)],
                             [RKV, RQT[hd]], Rb)
                    S.op("act", lambda e, bk=bk, kc=kc: e.activation(out=PT[:, pb, kc, :], in_=bk[:], func=AF.Exp,
                                                                      scale=1.0 / 16.0), reads=[Rb], writes=[RPT[pb]])

            def emit_pv(b, hd):
                pb = hd % 2
                bs, Rbs = mbank()
                mm_group(bs[:], [(onesb[:], PT[:, pb, kc, :]) for kc in range(2)], [RPT[pb], Rc], Rbs)
                S.op("act", lambda e: e.activation(out=rr[:, pb, :], in_=bs[:], func=AF.Ln), reads=[Rbs], writes=[Rrr[pb]])
                S.op("act", lambda e: e.activation(out=rr[:, pb, :], in_=rr[:, pb, :], func=AF.Exp, scale=-1.0),
                     reads=[Rrr[pb]], writes=[Rrr[pb]])
                for cc in range(2):
                    bo, Rbo = mbank()
                    mm_group(bo[:], [(Vt[:, 0, kc, (2 * hd + cc) * 128:(2 * hd + cc + 1) * 128], PT[:, pb, kc, :]) for kc in range(2)],
                             [RPT[pb], RKV], Rbo)
                    S.op("dve", lambda e, bo=bo, cc=cc: e.tensor_tensor(
                        out=OT[:, b, 2 * hd + cc, :], in0=bo[:], in1=rr[:, pb, :], op=ALU.mult),
                        reads=[Rbo, Rrr[pb]], writes=[ROT[b]])

            def wo_group(b, gidx):
                i4, hf = gidx // 2, gidx % 2
                i = b * 4 + i4
                bk, Rb = mbank()
                mm_group(bk[:], [(OT[:, b, k, i4 * 128:(i4 + 1) * 128], w_o_v[:, k, hf * 512:(hf + 1) * 512]) for k in range(KD)],
                         [ROT[b], RWB1], Rb)
                S.op("dve", lambda e: e.tensor_tensor(
                    out=X[:, i, hf * 512:(hf + 1) * 512], in0=bk[:], in1=X[:, i, hf * 512:(hf + 1) * 512], op=ALU.add),
                    reads=[Rb, RX[i]], writes=[RX[i]])
                if hf == 1:
                    early_square(i)

            st0 = p1b_conv(0)
            p1b_lnstat(0, *st0)
            p1b_lnapply(0)
            st1 = p1b_conv(1)
            p1b_lnstat(1, *st1)
            if not last_half:
                S.op("dve", lambda e: e.tensor_copy(gtail[:], gluT[:, :, GT:GT + 32]), reads=[Rglu], writes=[Rtail])
            p1b_poolproj(0)
            p1b_wout(0)
            p1b_poolproj(1)
            norm_stats([X[:, i, :] for i in range(4)], RX[0:4], 4, squares=False, c0=0)
            p2_norm(0)
            p1b_lnapply(1)
            for c in range(8):
                q_chunk(c)
            p1b_wout(1)
            load_pass(0)
            norm_stats([X[:, i, :] for i in range(4, 8)], RX[4:8], 4, squares=False, c0=4)
            p2_norm(1)
            emit_scores(0)
            emit_scores(1)
            q_chunk(0); q_chunk(1)
            emit_pv(0, 0)
            emit_scores(2)
            q_chunk(2); q_chunk(3)
            emit_pv(0, 1)
            emit_scores(3)
            q_chunk(4); q_chunk(5)
            emit_pv(0, 2)
            q_chunk(6); q_chunk(7)
            emit_pv(0, 3)
            emit_scores(0)
            emit_scores(1)
            wo_group(0, 0); wo_group(0, 1)
            emit_pv(1, 0)
            emit_scores(2)
            wo_group(0, 2); wo_group(0, 3)
            emit_pv(1, 1)
            emit_scores(3)
            wo_group(0, 4); wo_group(0, 5)
            emit_pv(1, 2)
            wo_group(0, 6); wo_group(0, 7)
            emit_pv(1, 3)
            for g_ in range(8):
                wo_group(1, g_)
            load_pass(1)
            if gi + 1 < len(groups):
                load_w_in()

            norm_stats([X[:, i, :] for i in range(NT)], RX, NT, squares=False)
            for i in range(NT):
                norm_tile_to_T(X[:, i, :], RX[i], i, V_GFFN, hTg[:, :, i * 128:(i + 1) * 128], RhTg)
            units = [(p, b) for p in range(len(passes)) for b in range(NBLK)]
            seq_first_blk = first
            ucount = [0]

            def emit_up(p, b, jj, ab):
                j0, n = passes[p]
                slot = p % 2
                g_, v_, dn_ = ffn_views(slot)
                RW = RWB0 if slot == 0 else RWB1
                j = j0 + jj
                pa = (ucount[0]) % 2
                ucount[0] += 1
                gb = h * NBLK + b
                rpar, wpar = gb % 2, (gb + 1) % 2
                banks = []
                for xi, wv in enumerate((g_, v_)):
                    bk, Rb = mbank()
                    mm_group(bk[:], [(wv[:, k, jj * 128:(jj + 1) * 128], hTg[:, k, b * 512:(b + 1) * 512]) for k in range(KD)],
                             [RhTg, RW], Rb)
                    banks.append((bk, Rb))
                for xi in range(2):
                    bk, Rb = banks[xi]
                    ch = j + xi * NFF
                    a = aGV[:, pa, xi, :]
                    Ra = RaGV[pa][xi]
                    U = Ub[:, pa, xi, :]
                    RU = RUb[pa][xi]
                    if not (last_half and b == NBLK - 1):
                        S.op("act", lambda e, bk=bk, ch=ch: e.copy(out=fH[:, wpar, ch, :], in_=bk[:, 510:512]),
                             reads=[Rb], writes=[RfH[wpar]])
                    if first and b == 0:
                        S.op("act", lambda e, U=U: e.activation(out=U[:, 0:2], in_=vecs[:, 0:2], func=AF.Copy, scale=0.0),
                             reads=[Rc], writes=[RU])
                    else:
                        S.op("act", lambda e, U=U, ch=ch: e.copy(out=U[:, 0:2], in_=fH[:, rpar, ch, :]),
                             reads=[RfH[rpar]], writes=[RU])
                    S.op("act", lambda e, bk=bk, U=U: e.copy(out=U[:, 2:514], in_=bk[:]), reads=[Rb], writes=[RU])
                    S.op("act", lambda e, bk=bk, a=a, ch=ch: e.activation(
                        out=a, in_=bk[:], func=AF.Identity, scale=vcol(V_FW + ch * 3 + 2), bias=vcol(V_FB + ch)),
                        reads=[Rb, Rc], writes=[Ra])
                for xi in range(2):
                    ch = j + xi * NFF
                    a = aGV[:, pa, xi, :]
                    Ra = RaGV[pa][xi]
                    U = Ub[:, pa, xi, :]
                    RU = RUb[pa][xi]
                    S.op("dve", lambda e, U=U, a=a, ch=ch: e.scalar_tensor_tensor(
                        out=a, in0=U[:, 1:513], scalar=vcol(V_FW + ch * 3 + 1), in1=a,
                        op0=ALU.mult, op1=ALU.add), reads=[RU, Rc, Ra], writes=[Ra])
                    S.op("dve", lambda e, U=U, a=a, ch=ch: e.scalar_tensor_tensor(
                        out=a, in0=U[:, 0:512], scalar=vcol(V_FW + ch * 3 + 0), in1=a,
                        op0=ALU.mult, op1=ALU.add), reads=[RU, Rc, Ra], writes=[Ra])
                return (pa, ab, jj)

            def emit_gate(pa, ab, jj):
                ag = aGV[:, pa, 0, :]
                S.op("act", lambda e: e.activation(out=ag, in_=ag, func=AF.Silu),
                     reads=[RaGV[pa][0]], writes=[RaGV[pa][0]])
                S.op("dve", lambda e: e.tensor_tensor(out=actT[:, ab, jj, :], in0=ag, in1=aGV[:, pa, 1, :], op=ALU.mult),
                     reads=[RaGV[pa][0], RaGV[pa][1]], writes=[RactT[ab]])

            def emit_down(p, b, ab):
                j0, n = passes[p]
                slot = p % 2
                g_, v_, dn_ = ffn_views(slot)
                RW = RWB0 if slot == 0 else RWB1
                for i4 in range(4):
                    i = b * 4 + i4
                    for hf in range(2):
                        bk, Rb = mbank()
                        mm_group(bk[:], [(actT[:, ab, jj, i4 * 128:(i4 + 1) * 128], dn_[:, jj, hf * 512:(hf + 1) * 512]) for jj in range(n)],
                                 [RactT[ab], RW], Rb)
                        S.op("dve", lambda e, bk=bk, i=i, hf=hf: e.tensor_tensor(
                            out=X[:, i, hf * 512:(hf + 1) * 512], in0=bk[:], in1=X[:, i, hf * 512:(hf + 1) * 512], op=ALU.add),
                            reads=[Rb, RX[i]], writes=[RX[i]])

            def emit_final(b):
                tiles = [b * 4 + i4 for i4 in range(4)]
                for i in tiles:
                    S.op("act", lambda e, i=i: e.activation(out=hb2[:, i % 2, :], in_=X[:, i, :], func=AF.Square,
                                                            accum_out=ssb[:, i:i + 1]),
                         reads=[RX[i]], writes=[Rhb2[i % 2], Rss])
                lo_, hi_ = tiles[0], tiles[-1] + 1
                S.op("dve", lambda e: e.tensor_scalar(out=rstd[:, lo_:hi_], in0=ssb[:, lo_:hi_], scalar1=1.0 / D, scalar2=EPS,
                                                      op0=ALU.mult, op1=ALU.add), reads=[Rss], writes=[Rrstd])
                S.op("act", lambda e: e.activation(out=rstd[:, lo_:hi_], in_=rstd[:, lo_:hi_], func=AF.Ln),
                     reads=[Rrstd], writes=[Rrstd])
                S.op("act", lambda e: e.activation(out=rstd[:, lo_:hi_], in_=rstd[:, lo_:hi_], func=AF.Exp, scale=-0.5),
                     reads=[Rrstd], writes=[Rrstd])
                for i in tiles:
                    ob = i % 2
                    S.op("dve", lambda e, i=i: e.scalar_tensor_tensor(
                        out=X[:, i, :], in0=X[:, i, :], scalar=rstd[:, i:i + 1], in1=gfinb[:], op0=ALU.mult, op1=ALU.mult),
                        reads=[RX[i], Rrstd, Rc], writes=[RX[i]])
                    S.dma("sp", oslots[i], dout[s, tok0 + i * 128: tok0 + (i + 1) * 128, :], X[:, i, :],
                          reads=[RX[i]], writes=[Rout[i]])

            prev = None
            pend_load = []
            STAGED = False
            pend_gate = None
            for ui, (p, b) in enumerate(units):
                j0, n = passes[p]
                ab = ui % 2
                g0 = emit_up(p, b, 0, ab)
                if STAGED:
                    if pend_gate is not None:
                        emit_gate(*pend_gate)
                    pend_gate = g0
                else:
                    emit_gate(*g0)
                if prev is not None:
                    pp, pb_, pab = prev
                    emit_down(pp, pb_, pab)
                    if pp == len(passes) - 1:
                        emit_final(pb_)
                        if gi + 1 < len(groups):
                            issue_x(gi + 1, range(pb_ * 4, pb_ * 4 + 4))
                    if pb_ == NBLK - 1 and pp + 2 < len(passes):
                        load_pass(pp + 2)
                for jj in range(1, n):
                    gj = emit_up(p, b, jj, ab)
                    if STAGED:
                        emit_gate(*pend_gate)
                        pend_gate = gj
                    else:
                        emit_gate(*gj)
                    if jj == 1 and pend_load:
                        load_pass(pend_load.pop())
                prev = (p, b, ab)
            if STAGED:
                emit_gate(*pend_gate)
            pp, pb_, pab = prev
            emit_down(pp, pb_, pab)
            if gi + 1 < len(groups):
                p1a_norm(0)
            emit_final(pb_)
            if gi + 1 < len(groups):
                issue_x(gi + 1, range(pb_ * 4, pb_ * 4 + 4))
            if gi + 1 < len(groups) and groups[gi + 1][1] != 0:
                load_w_out_o()

        flush_stores()
        S.final_wait("sp", Rout + [v_[1] for v_ in scr.values()])
        S.emit()
    return nc


_tt_small_cache = {}


def _prep_inputs(inputs):
    f = lambda a: np.ascontiguousarray(np.asarray(a, dtype=np.float32))
    vec = np.zeros((128, V_N), np.float32)

    def fm(v, n):
        return np.asarray(v, np.float32).reshape(n, 128).T

    vec[:, V_GMIX:V_GMIX + 8] = fm(inputs["norm_mix_g"][0], 8)
    vec[:, V_GX:V_GX + 8] = fm(inputs["norm_xattn_g"][0], 8)
    vec[:, V_GMEM:V_GMEM + 8] = fm(inputs["norm_mem_g"][0], 8)
    vec[:, V_GFFN:V_GFFN + 8] = fm(inputs["norm_ffn_g"][0], 8)
    vec[:, V_CB:V_CB + 4] = fm(inputs["conv_dw_b"][0], 4)
    vec[:, V_LNG:V_LNG + 4] = fm(inputs["conv_ln_g"][0], 4)
    vec[:, V_LNB:V_LNB + 4] = fm(inputs["conv_ln_b"][0], 4)
    vec[:, V_PSC:V_PSC + 4] = fm(inputs["pool_scale"][0], 4)
    cw = np.asarray(inputs["conv_dw_w"][0], np.float32)
    vec[:, V_CW:V_CW + 4 * CW] = cw.reshape(CW, 4, 128).transpose(2, 1, 0).reshape(128, 4 * CW)
    fw = np.asarray(inputs["ffn_dw_w"][0], np.float32)
    vec[:, V_FW:V_FW + 132] = fw.reshape(3, 44, 128).transpose(2, 1, 0).reshape(128, 132)
    vec[:, V_FB:V_FB + 44] = fm(inputs["ffn_dw_b"][0], 44)
    gfinb = np.ascontiguousarray(np.broadcast_to(np.asarray(inputs["norm_final_g"], np.float32)[None, :], (128, D)))
    shared = {
        "vecs": vec, "gfinb": gfinb,
        "w_in": f(inputs["w_in"][0]), "pool_w": f(inputs["pool_w"][0]), "w_out": f(inputs["w_out"][0]),
        "w_q": f(inputs["w_q"][0]), "w_kv": f(inputs["w_kv"][0]), "w_o": f(inputs["w_o"][0]),
        "w_up": f(inputs["w_up"][0]), "w_down": f(inputs["w_down"][0]),
    }
    return shared


def kernel(**inputs):
    x = np.asarray(inputs["x"], np.float32)
    mem = np.asarray(inputs["mem"], np.float32)
    shared = _prep_inputs(inputs)
    nc = build_program()
    in_maps = []
    for c in range(NCORES):
        m = dict(shared)
        m["x"] = np.ascontiguousarray(x[c * SEQ_PER_CORE:(c + 1) * SEQ_PER_CORE])
        m["mem"] = np.ascontiguousarray(mem[c * SEQ_PER_CORE:(c + 1) * SEQ_PER_CORE])
        in_maps.append(m)
    res = run_bass_kernel_spmd(nc, in_maps, core_ids=list(range(NCORES)))
    out = np.concatenate([np.asarray(r["out"], np.float32) for r in res.results], axis=0)
    return out
```
